# Optimizing a Trainium2 kernel written in Bass

```python
import math
import jax
import jax.numpy as jnp
from jax import lax
import numpy as np

D_MODEL = 2048
BATCH = 16
SEQ = 2048
DEPTH = 4

CTX_LEN = 256
GRID_W = 64
N_EVEN = (DEPTH + 1) // 2
N_ODD = DEPTH // 2
D_FF = 4 * D_MODEL
N_MOD = 6
EPS = 1e-6
CONV_W = 3

A_HEAD_DIM = 64
A_WIDTH = D_MODEL // 2
A_HEADS = A_WIDTH // A_HEAD_DIM
A_GROUPS = 2
A_STATE = 128
A_CONV_DIM = A_WIDTH + 2 * A_GROUPS * A_STATE
A_CHUNK = 128

B_WIDTH = D_MODEL // 2
B_HEADS = 8
B_VDIM = B_WIDTH // B_HEADS
B_KDIM = 128
B_FDIM = B_HEADS * B_KDIM
B_CHUNK = 64

C_HEADS = 4
C_V_WIDTH = D_MODEL
C_VDIM = C_V_WIDTH // C_HEADS
C_QKDIM = C_VDIM // 2
C_QK_WIDTH = C_HEADS * C_QKDIM
C_CHUNK = 128

EVEN_SIZES = (A_WIDTH, A_CONV_DIM, 2 * A_HEADS, B_FDIM, 2 * B_FDIM, B_WIDTH, B_WIDTH)
EVEN_IN = sum(EVEN_SIZES)
ODD_SIZES = (2 * C_QK_WIDTH, C_V_WIDTH, C_V_WIDTH, 2 * C_HEADS, 2 * C_HEADS)
ODD_IN = sum(ODD_SIZES)

kernel_name = "bidir_ssd_hgrn2_mlstm_prefix_dit"


def _split(u, sizes):
    return jnp.split(u, [int(s) for s in np.cumsum(sizes)[:-1]], axis=-1)


def rmsnorm(x, w):
    xf = x.astype(jnp.float32)
    y = xf * lax.rsqrt(jnp.mean(xf * xf, axis=-1, keepdims=True) + EPS)
    return (y * w.astype(jnp.float32)).astype(x.dtype)


def group_rmsnorm(y, w, groups):
    shp = y.shape
    yf = y.astype(jnp.float32).reshape(shp[:-1] + (groups, shp[-1] // groups))
    yf = yf * lax.rsqrt(jnp.mean(yf * yf, axis=-1, keepdims=True) + EPS)
    return (yf.reshape(shp) * w.astype(jnp.float32)).astype(y.dtype)


def _dwconv(u, w, b):
    r = CONV_W // 2
    l = u.shape[-2]
    up = jnp.pad(u, [(0, 0)] * (u.ndim - 2) + [(r, r), (0, 0)])
    out = up[..., 0:l, :] * w[0]
    for j in range(1, CONV_W):
        out = out + up[..., j:j + l, :] * w[j]
    return out + b


def short_conv(u_ctx, u_lat, w, b):
    bsz, s, ch = u_lat.shape
    rows = s // GRID_W
    lat = _dwconv(u_lat.reshape(bsz, rows, GRID_W, ch), w, b).reshape(bsz, s, ch)
    return _dwconv(u_ctx, w, b), lat


def _to_chunks(t, n):
    b, l = t.shape[:2]
    return jnp.moveaxis(t.reshape((b, l // n, n) + t.shape[2:]), 1, 0)


def _from_chunks(t):
    nc, b, n = t.shape[:3]
    return jnp.moveaxis(t, 0, 1).reshape((b, nc * n) + t.shape[3:])


def _segsum(a):
    t = a.shape[-1]
    cs = jnp.cumsum(a, axis=-1)
    return jnp.where(jnp.tril(jnp.ones((t, t), dtype=bool)), cs[..., :, None] - cs[..., None, :], -jnp.inf)


def ssd_scan(inputs, h0):
    xdt, a, bm, cm = (t.astype(jnp.float32) for t in inputs)
    bsz, l, h, p = xdt.shape
    g, n = bm.shape[2], bm.shape[3]
    r = h // g
    nc = l // A_CHUNK
    xr = xdt.reshape(bsz, nc, A_CHUNK, g, r, p)
    ar = a.reshape(bsz, nc, A_CHUNK, g, r).transpose(0, 3, 4, 1, 2)
    br = bm.reshape(bsz, nc, A_CHUNK, g, n)
    cr = cm.reshape(bsz, nc, A_CHUNK, g, n)
    a_cs = jnp.cumsum(ar, axis=-1)
    y_diag = jnp.einsum('bclgn,bcsgn,bgrcls,bcsgrp->bclgrp', cr, br, jnp.exp(_segsum(ar)), xr)
    states = jnp.einsum('bclgn,bgrcl,bclgrp->bcgrpn', br, jnp.exp(a_cs[..., -1:] - a_cs), xr)
    states = jnp.concatenate([h0.reshape(bsz, g, r, p, n)[:, None], states], axis=1)
    chunk_a = jnp.pad(a_cs[..., -1], ((0, 0), (0, 0), (0, 0), (1, 0)))
    states = jnp.einsum('bgrzc,bcgrpn->bzgrpn', jnp.exp(_segsum(chunk_a)), states)
    y_off = jnp.einsum('bclgn,bcgrpn,bgrcl->bclgrp', cr, states[:, :-1], jnp.exp(a_cs))
    return (y_diag + y_off).reshape(bsz, l, h, p), states[:, -1].reshape(bsz, h, p, n)


def hgrn2_scan(inputs, s0):
    q, logf, k, v = (t.astype(jnp.float32) for t in inputs)
    mask = jnp.tril(jnp.ones((B_CHUNK, B_CHUNK), dtype=bool))[None, :, :, None, None]

    def step(state, inp):
        qc, gc, kc, vc = inp
        bcum = jnp.cumsum(gc, axis=1)
        rel = jnp.where(mask, bcum[:, :, None] - bcum[:, None, :], -jnp.inf)
        att = jnp.einsum('bthk,btshk,bshk->bhts', qc, jnp.exp(rel), kc)
        o = jnp.einsum('bhts,bshv->bthv', att, vc) + jnp.einsum('bthk,bhkv->bthv', qc * jnp.exp(bcum), state)
        blast = bcum[:, -1]
        state = jnp.exp(blast)[..., None] * state + jnp.einsum('bshk,bshv->bhkv', kc * jnp.exp(blast[:, None] - bcum), vc)
        return state, o

    state, o = lax.scan(step, s0, tuple(_to_chunks(t, B_CHUNK) for t in (q, logf, k, v)))
    return _from_chunks(o), state


def mlstm_scan(inputs, state0):
    q, k, v, logi, logf = (t.astype(jnp.float32) for t in inputs)
    mask = jnp.tril(jnp.ones((C_CHUNK, C_CHUNK), dtype=bool))

    def step(carry, inp):
        cmat, nvec, m = carry
        qc, kc, vc, ic, fc = inp
        bcum = jnp.cumsum(fc, axis=1).transpose(0, 2, 1)
        ih = ic.transpose(0, 2, 1)
        logd = jnp.where(mask, bcum[..., :, None] - bcum[..., None, :] + ih[..., None, :], -jnp.inf)
        gstate = bcum + m[..., None]
        mt = jnp.maximum(jnp.max(logd, axis=-1), gstate)
        w = jnp.einsum('bthd,bshd->bhts', qc, kc) * jnp.exp(logd - mt[..., None])
        sw = jnp.exp(gstate - mt)
        num = jnp.einsum('bhts,bshv->bhtv', w, vc) + sw[..., None] * jnp.einsum('bthd,bhdv->bhtv', qc, cmat)
        den = jnp.sum(w, axis=-1) + sw * jnp.einsum('bthd,bhd->bht', qc, nvec)
        hout = num / jnp.maximum(jnp.abs(den), jnp.exp(-mt))[..., None]
        blast = bcum[..., -1]
        logw = blast[..., None] - bcum + ih
        m_new = jnp.maximum(blast + m, jnp.max(logw, axis=-1))
        ws = jnp.exp(logw - m_new[..., None])
        cs = jnp.exp(blast + m - m_new)
        cmat = cs[..., None, None] * cmat + jnp.einsum('bhs,bshd,bshv->bhdv', ws, kc, vc)
        nvec = cs[..., None] * nvec + jnp.einsum('bhs,bshd->bhd', ws, kc)
        return (cmat, nvec, m_new), hout.transpose(0, 2, 1, 3)

    state, hs = lax.scan(step, state0, tuple(_to_chunks(t, C_CHUNK) for t in (q, k, v, logi, logf)))
    return _from_chunks(hs), state


def _flip_seq(tree):
    return tuple(jnp.flip(t, axis=1) for t in tree)


def bidirectional(scan_fn, ctx_f, ctx_b, lat_f, lat_b, init):
    yc_f, s_f = scan_fn(ctx_f, init)
    yl_f, _ = scan_fn(lat_f, s_f)
    yc_b, s_b = scan_fn(_flip_seq(ctx_b), init)
    yl_b, _ = scan_fn(_flip_seq(lat_b), s_b)
    return yc_f + jnp.flip(yc_b, axis=1), yl_f + jnp.flip(yl_b, axis=1)


def even_mixer(h_c, h_l, w_in, w_out, conv_w, conv_b, a_log, dt_bias, d_skip, ssd_norm_w, lb, hgrn_norm_w, need_ctx):
    bsz = h_l.shape[0]
    z_c, xbc_c, dt_c, q_c, f_c, i_c, g_c = _split(h_c @ w_in, EVEN_SIZES)
    z_l, xbc_l, dt_l, q_l, f_l, i_l, g_l = _split(h_l @ w_in, EVEN_SIZES)

    xbc_c, xbc_l = short_conv(xbc_c, xbc_l, conv_w, conv_b)
    xbc_c, xbc_l = jax.nn.silu(xbc_c), jax.nn.silu(xbc_l)

    def ssd_streams(xbc, dt_raw):
        bs, l, _ = xbc.shape
        xs, bm, cm = _split(xbc, (A_WIDTH, A_GROUPS * A_STATE, A_GROUPS * A_STATE))
        xs = xs.reshape(bs, l, A_HEADS, A_HEAD_DIM)
        bm = bm.reshape(bs, l, A_GROUPS, A_STATE)
        cm = cm.reshape(bs, l, A_GROUPS, A_STATE)
        dirs = []
        for d in range(2):
            dt = jax.nn.softplus(dt_raw[..., d * A_HEADS:(d + 1) * A_HEADS].astype(jnp.float32) + dt_bias[d])
            dirs.append((xs * dt[..., None], -jnp.exp(a_log[d].astype(jnp.float32)) * dt, bm, cm))
        return xs, dirs[0], dirs[1]

    xs_c, sc_f, sc_b = ssd_streams(xbc_c, dt_c)
    xs_l, sl_f, sl_b = ssd_streams(xbc_l, dt_l)
    h0 = jnp.zeros((bsz, A_HEADS, A_HEAD_DIM, A_STATE), jnp.float32)
    ya_c, ya_l = bidirectional(ssd_scan, sc_f, sc_b, sl_f, sl_b, h0)

    def ssd_out(y, xs, z):
        y = y + xs * d_skip[:, None]
        bs, l = y.shape[:2]
        y = y.reshape(bs, l, A_WIDTH).astype(z.dtype)
        return group_rmsnorm(y * jax.nn.silu(z), ssd_norm_w, A_GROUPS)

    log_lb = jnp.log(lb)
    log_1mlb = jnp.log1p(-lb)

    def hgrn_streams(q_raw, f_raw, i_raw):
        bs, l, _ = q_raw.shape
        q = jax.nn.silu(q_raw).reshape(bs, l, B_HEADS, B_KDIM)
        v = i_raw.reshape(bs, l, B_HEADS, B_VDIM)
        dirs = []
        for d in range(2):
            zf = f_raw[..., d * B_FDIM:(d + 1) * B_FDIM].astype(jnp.float32)
            logf = jnp.logaddexp(log_lb, log_1mlb + jax.nn.log_sigmoid(zf))
            kin = (1.0 - lb) * jax.nn.sigmoid(-zf)
            dirs.append((q, logf.reshape(bs, l, B_HEADS, B_KDIM), kin.reshape(bs, l, B_HEADS, B_KDIM), v))
        return dirs[0], dirs[1]

    hc_f, hc_b = hgrn_streams(q_c, f_c, i_c)
    hl_f, hl_b = hgrn_streams(q_l, f_l, i_l)
    s0 = jnp.zeros((bsz, B_HEADS, B_KDIM, B_VDIM), jnp.float32)
    yb_c, yb_l = bidirectional(hgrn2_scan, hc_f, hc_b, hl_f, hl_b, s0)

    def hgrn_out(o, g):
        bs, l = o.shape[:2]
        o = o.reshape(bs, l, B_WIDTH).astype(g.dtype)
        return group_rmsnorm(o, hgrn_norm_w, B_HEADS) * jax.nn.silu(g)

    out_l = jnp.concatenate([ssd_out(ya_l, xs_l, z_l), hgrn_out(yb_l, g_l)], axis=-1) @ w_out
    out_c = None
    if need_ctx:
        out_c = jnp.concatenate([ssd_out(ya_c, xs_c, z_c), hgrn_out(yb_c, g_c)], axis=-1) @ w_out
    return out_c, out_l


def odd_mixer(h_c, h_l, w_in, w_out, conv_w, conv_b, gate_b, norm_w, need_ctx):
    bsz = h_l.shape[0]
    qk_c, v_c, o_c, i_c, f_c = _split(h_c @ w_in, ODD_SIZES)
    qk_l, v_l, o_l, i_l, f_l = _split(h_l @ w_in, ODD_SIZES)
    qk_c, qk_l = short_conv(qk_c, qk_l, conv_w, conv_b)

    def streams(qk, v, i_raw, f_raw):
        bs, l, _ = qk.shape
        q, k = _split(jax.nn.silu(qk), (C_QK_WIDTH, C_QK_WIDTH))
        q = q.reshape(bs, l, C_HEADS, C_QKDIM)
        k = k.reshape(bs, l, C_HEADS, C_QKDIM) * (C_QKDIM ** -0.5)
        v = v.reshape(bs, l, C_HEADS, C_VDIM)
        dirs = []
        for d in range(2):
            logi = i_raw[..., d * C_HEADS:(d + 1) * C_HEADS].astype(jnp.float32) + gate_b[d]
            logf = jax.nn.log_sigmoid(f_raw[..., d * C_HEADS:(d + 1) * C_HEADS].astype(jnp.float32) + gate_b[2 + d])
            dirs.append((q, k, v, logi, logf))
        return dirs[0], dirs[1]

    c_f, c_b = streams(qk_c, v_c, i_c, f_c)
    l_f, l_b = streams(qk_l, v_l, i_l, f_l)
    init = (jnp.zeros((bsz, C_HEADS, C_QKDIM, C_VDIM), jnp.float32),
            jnp.zeros((bsz, C_HEADS, C_QKDIM), jnp.float32),
            jnp.zeros((bsz, C_HEADS), jnp.float32))
    hc, hl = bidirectional(mlstm_scan, c_f, c_b, l_f, l_b, init)

    def readout(hh, o_raw):
        bs, l = hh.shape[:2]
        hh = hh.reshape(bs, l, C_V_WIDTH).astype(o_raw.dtype)
        return (group_rmsnorm(hh, norm_w, C_HEADS) * jax.nn.sigmoid(o_raw)) @ w_out

    out_l = readout(hl, o_l)
    out_c = readout(hc, o_c) if need_ctx else None
    return out_c, out_l


def squared_relu_mlp(h, w1, w2):
    return jnp.square(jax.nn.relu(h @ w1)) @ w2


def setup_inputs(seed: int = 0) -> dict:
    key = jax.random.key(seed)
    ks = iter(jax.random.split(key, 32))

    def nrm(shape, scale):
        return jax.random.normal(next(ks), shape, jnp.float32) * scale

    def unif(shape, lo, hi):
        return jax.random.uniform(next(ks), shape, jnp.float32, minval=lo, maxval=hi)

    dt0 = jnp.exp(unif((N_EVEN, 2, A_HEADS), math.log(1e-3), math.log(1e-1)))
    return {
        "x": nrm((BATCH, SEQ, D_MODEL), 1.0),
        "c": nrm((BATCH, D_MODEL), 1.0),
        "ctx": nrm((BATCH, CTX_LEN, D_MODEL), 1.0),
        "c_ctx": nrm((D_MODEL,), 1.0),
        "mod_w": nrm((DEPTH, D_MODEL, N_MOD * D_MODEL), D_MODEL ** -0.5),
        "mod_b": nrm((DEPTH, N_MOD * D_MODEL), 0.02),
        "norm_w": 1.0 + nrm((DEPTH, 2, D_MODEL), 0.1),
        "final_norm_w": 1.0 + nrm((D_MODEL,), 0.1),
        "mlp_w1": nrm((DEPTH, D_MODEL, D_FF), D_MODEL ** -0.5),
        "mlp_w2": nrm((DEPTH, D_FF, D_MODEL), D_FF ** -0.5),
        "even_w_in": nrm((N_EVEN, D_MODEL, EVEN_IN), D_MODEL ** -0.5),
        "even_w_out": nrm((N_EVEN, A_WIDTH + B_WIDTH, D_MODEL), (A_WIDTH + B_WIDTH) ** -0.5),
        "ssd_conv_w": nrm((N_EVEN, CONV_W, A_CONV_DIM), CONV_W ** -0.5),
        "ssd_conv_b": nrm((N_EVEN, A_CONV_DIM), 0.02),
        "ssd_a_log": jnp.log(unif((N_EVEN, 2, A_HEADS), 1.0, 16.0)),
        "ssd_dt_bias": dt0 + jnp.log(-jnp.expm1(-dt0)),
        "ssd_d": 1.0 + nrm((N_EVEN, A_HEADS), 0.1),
        "ssd_norm_w": 1.0 + nrm((N_EVEN, A_WIDTH), 0.1),
        "hgrn_lb": nrm((N_EVEN, B_FDIM), 0.5),
        "hgrn_norm_w": 1.0 + nrm((N_EVEN, B_WIDTH), 0.1),
        "odd_w_in": nrm((N_ODD, D_MODEL, ODD_IN), D_MODEL ** -0.5),
        "odd_w_out": nrm((N_ODD, C_V_WIDTH, D_MODEL), C_V_WIDTH ** -0.5),
        "mlstm_conv_w": nrm((N_ODD, CONV_W, 2 * C_QK_WIDTH), CONV_W ** -0.5),
        "mlstm_conv_b": nrm((N_ODD, 2 * C_QK_WIDTH), 0.02),
        "mlstm_gate_b": jnp.concatenate([nrm((N_ODD, 2, C_HEADS), 0.1), unif((N_ODD, 2, C_HEADS), 3.0, 6.0)], axis=1),
        "mlstm_norm_w": 1.0 + nrm((N_ODD, C_V_WIDTH), 0.1),
    }


def reference(x, c, ctx, c_ctx, mod_w, mod_b, norm_w, final_norm_w, mlp_w1, mlp_w2,
              even_w_in, even_w_out, ssd_conv_w, ssd_conv_b, ssd_a_log, ssd_dt_bias, ssd_d, ssd_norm_w,
              hgrn_lb, hgrn_norm_w,
              odd_w_in, odd_w_out, mlstm_conv_w, mlstm_conv_b, mlstm_gate_b, mlstm_norm_w):
    lb_all = jnp.cumsum(jax.nn.softmax(hgrn_lb.astype(jnp.float32), axis=0), axis=0)
    lb_all = lb_all - lb_all[0]
    xc = ctx
    for layer in range(DEPTH):
        need_ctx = layer < DEPTH - 1
        mod_l = (jax.nn.silu(c) @ mod_w[layer] + mod_b[layer])[:, None, :]
        mod_c = jax.nn.silu(c_ctx) @ mod_w[layer] + mod_b[layer]
        sh1, sc1, g1, sh2, sc2, g2 = jnp.split(mod_l, N_MOD, axis=-1)
        csh1, csc1, cg1, csh2, csc2, cg2 = jnp.split(mod_c, N_MOD, axis=-1)
        h_l = rmsnorm(x, norm_w[layer, 0]) * (1 + sc1) + sh1
        h_c = rmsnorm(xc, norm_w[layer, 0]) * (1 + csc1) + csh1
        if layer % 2 == 0:
            e = layer // 2
            m_c, m_l = even_mixer(h_c, h_l, even_w_in[e], even_w_out[e], ssd_conv_w[e], ssd_conv_b[e],
                                  ssd_a_log[e], ssd_dt_bias[e], ssd_d[e], ssd_norm_w[e],
                                  lb_all[e], hgrn_norm_w[e], need_ctx)
        else:
            o = layer // 2
            m_c, m_l = odd_mixer(h_c, h_l, odd_w_in[o], odd_w_out[o], mlstm_conv_w[o], mlstm_conv_b[o],
                                 mlstm_gate_b[o], mlstm_norm_w[o], need_ctx)
        x = x + g1 * m_l
        x = x + g2 * squared_relu_mlp(rmsnorm(x, norm_w[layer, 1]) * (1 + sc2) + sh2, mlp_w1[layer], mlp_w2[layer])
        if need_ctx:
            xc = xc + cg1 * m_c
            xc = xc + cg2 * squared_relu_mlp(rmsnorm(xc, norm_w[layer, 1]) * (1 + csc2) + csh2, mlp_w1[layer], mlp_w2[layer])
    return rmsnorm(x, final_norm_w)
```

```python
import contextlib
import math
import numpy as np
import concourse.bass as bass
import concourse.mybir as mybir
from concourse.bass_utils import run_bass_kernel_spmd

F32 = mybir.dt.float32
BF16 = mybir.dt.bfloat16
ALU = mybir.AluOpType
AF = mybir.ActivationFunctionType

D = 2048
NTOK = 4608
NBLK = 9
NCH = 36
DEPTH = 4
EVEN_IN = 7712
ODD_IN = 6160
EPS = 1e-6
BIG = 30000.0


class Res:
    __slots__ = ("name", "w", "r")

    def __init__(self, name):
        self.name = name
        self.w = None
        self.r = []


class Buf:
    def __init__(self, t, name):
        self.t = t
        self.name = name
        self.r = Res(name)
        self.subs = {}

    def k(self, key):
        s = self.subs.get(key)
        if s is None:
            s = Res(f"{self.name}.{key}")
            self.subs[key] = s
        return s


def _res(x):
    return x.r if isinstance(x, Buf) else x


class Trk:
    EPOCH = 60000

    def __init__(self, nc, es, n_dma_sems=(10, 6, 10)):
        self.nc = nc
        self.es = es
        self.engs = {"pe": nc.tensor, "act": nc.scalar, "dve": nc.vector, "pool": nc.gpsimd, "sp": nc.sync}
        self.sems = {}
        self.cnt = {}
        self.waited = {k: {} for k in self.engs}
        self.epoch = {}
        self.last = {}
        for k in ("pe", "act", "dve", "pool"):
            self.epoch[k] = 0
            self._new_sem(f"e_{k}0")
        self.dq = {}
        self.gen = 0
        for q, n in zip(("sp", "act", "pool"), n_dma_sems):
            keys = []
            for i in range(n):
                key = f"d_{q}{i}"
                self._new_sem(key)
                keys.append(key)
            self.dq[q] = {"keys": keys, "i": 0}
        self.n_instr = 0
        self.n_wait = 0

    def _new_sem(self, key):
        self.sems[key] = self.es.enter_context(self.nc.semaphore(key))
        self.cnt[key] = 0

    def _eng_key(self, k):
        key = f"e_{k}{self.epoch[k]}"
        if self.cnt[key] >= self.EPOCH:
            self.epoch[k] += 1
            key = f"e_{k}{self.epoch[k]}"
            self._new_sem(key)
        return key

    def _wait(self, ek, tok):
        if tok is None:
            return
        key, val = tok
        if self.waited[ek].get(key, 0) >= val:
            return
        self.engs[ek].wait_ge(self.sems[key], val)
        self.waited[ek][key] = val
        self.n_wait += 1

    def _deps(self, ek, reads, writes):
        toks = []
        for r in reads:
            r = _res(r)
            if r.w is not None:
                toks.append((r.w, True))
        for w in writes:
            w = _res(w)
            if w.w is not None:
                toks.append((w.w, True))
            for t in w.r:
                toks.append((t, False))
        for tok, strong in toks:
            if tok[0].startswith("e_" + ek):
                if ek == "pe" or not strong:
                    continue
            self._wait(ek, tok)

    def _commit(self, tok, reads, writes):
        for r in reads:
            r = _res(r)
            r.r.append(tok)
            if len(r.r) > 48:
                best = {}
                for k, v in r.r:
                    if best.get(k, 0) < v:
                        best[k] = v
                r.r = list(best.items())
        for w in writes:
            w = _res(w)
            w.w = tok
            w.r = []

    def op(self, ek, fn, reads=(), writes=()):
        return self.group(ek, [fn], reads, writes)

    def group(self, ek, fns, reads=(), writes=()):
        key = self._eng_key(ek)
        self._deps(ek, reads, writes)
        ins = None
        for fn in fns:
            ins = fn(self.engs[ek])
            self.n_instr += 1
        ins.then_inc(self.sems[key], 1)
        self.cnt[key] += 1
        tok = (key, self.cnt[key])
        self.last[ek] = tok
        self._commit(tok, reads, writes)
        return tok

    def dma(self, q, out, in_, reads=(), writes=(), **kw):
        d = self.dq[q]
        slot = d["i"] % len(d["keys"])
        d["i"] += 1
        key = d["keys"][slot]
        if self.cnt[key] + 16 > self.EPOCH:
            self.gen += 1
            key = f"d_{q}{slot}_{self.gen}"
            self._new_sem(key)
            d["keys"][slot] = key
        if self.cnt[key] > 0:
            self._wait(q, (key, self.cnt[key]))
        self._deps(q, reads, writes)
        ins = self.engs[q].dma_start(out=out, in_=in_, **kw)
        self.n_instr += 1
        ins.then_inc(self.sems[key], 16)
        self.cnt[key] += 16
        tok = (key, self.cnt[key])
        self._commit(tok, reads, writes)
        return tok

    def all_tokens(self):
        toks = [t for t in self.last.values()]
        for q in self.dq.values():
            for key in q["keys"]:
                if self.cnt[key]:
                    toks.append((key, self.cnt[key]))
        for key, c in self.cnt.items():
            if key.startswith("d_") and c and (key, c) not in toks:
                toks.append((key, c))
        return toks

    def barrier(self, engines=("pe", "act", "dve", "pool", "sp")):
        toks = self.all_tokens()
        for ek in engines:
            for t in toks:
                self._wait(ek, t)


def blk_cols(tb):
    return tb * 512


def chunk_col(b, kind, j):
    if kind == "lat":
        return b * 2048 + j * 128
    return 4096 + b * 256 + j * 128


def chain_chunks(b, d):
    ctx = [("ctx", 0), ("ctx", 1)]
    lat = [("lat", j) for j in range(16)]
    if d == 0:
        seq = ctx + lat
    else:
        seq = ctx[::-1] + lat[::-1]
    return [chunk_col(b, k, j) for k, j in seq]


class Prog:
    def __init__(self, n_layers=DEPTH, debug=False):
        self.n_layers = n_layers
        self.debug = debug
        self.nc = bass.Bass("TRN2", target_bir_lowering=False)
        self.uid = 0

    def kind(self, layer):
        f = getattr(self, "force_kind", None)
        if f is not None:
            return f
        return (layer % 2 == 1, layer // 2)

    def sb(self, es, name, shape, dt):
        self.uid += 1
        nm = f"{name}_{self.uid}"
        return Buf(es.enter_context(self.nc.sbuf_tensor(nm, list(shape), dt)), nm)

    def ps(self, es, name, shape, dt=F32):
        self.uid += 1
        nm = f"{name}_{self.uid}"
        return Buf(es.enter_context(self.nc.psum_tensor(nm, list(shape), dt)), nm)

    def dram(self, name, shape, dt, kind="Internal"):
        if self.debug and kind == "Internal" and name in self.debug_outs:
            kind = "ExternalOutput"
        t = self.nc.dram_tensor(name, list(shape), dt, kind=kind)
        return Buf(t.ap(), name)

    def build(self, debug_outs=()):
        self.debug_outs = set(debug_outs)
        nc = self.nc
        I = {}
        def inp(name, shape):
            I[name] = nc.dram_tensor(name, list(shape), F32, kind="ExternalInput").ap()
        inp("x", [2, 2048, D]); inp("ctx", [2, 256, D]); inp("cvec", [3, D])
        inp("mod_w", [DEPTH, D, 6 * D]); inp("mod_b", [DEPTH, 6 * D]); inp("norm_w", [DEPTH, 2, D])
        inp("final_norm_w", [D]); inp("mlp_w1", [DEPTH, D, 4 * D]); inp("mlp_w2", [DEPTH, 4 * D, D])
        inp("even_w_in", [2, D, EVEN_IN]); inp("even_w_out", [2, D, D])
        inp("ssd_conv_w", [2, 3, 1536]); inp("ssd_conv_b", [2, 1536]); inp("ssd_a_log", [2, 2, 16])
        inp("ssd_dt_bias", [2, 2, 16]); inp("ssd_d", [2, 16]); inp("ssd_norm_w", [2, 1024])
        inp("hgrn_lb", [2, 1024]); inp("hgrn_norm_w", [2, 1024])
        inp("odd_w_in", [2, D, ODD_IN]); inp("odd_w_out", [2, D, D])
        inp("mlstm_conv_w", [2, 3, 2048]); inp("mlstm_conv_b", [2, 2048]); inp("mlstm_gate_b", [2, 4, 4])
        inp("mlstm_norm_w", [2, 2048])
        self.I = I
        self.out = nc.dram_tensor("out", [2, 2048, D], F32, kind="ExternalOutput").ap()
        self.r_out = Res("out")

        self.XT = self.dram("XT", [D, NTOK], F32)
        self.U_FM = self.dram("U_FM", [6656, NTOK], BF16)
        self.U_TM = self.dram("U_TM", [NTOK, 2048], BF16)
        self.U_G = self.dram("U_G", [NTOK, 32], F32)
        self.HF = [self.dram("HF0", [D, NTOK], BF16), self.dram("HF1", [D, NTOK], BF16)]

        with contextlib.ExitStack() as es:
            self.T = Trk(nc, es)
            with contextlib.ExitStack() as ces:
                self.consts(ces)
                with contextlib.ExitStack() as pes:
                    self.prologue(pes)
                self.T.barrier()
                stop = getattr(self, "stop_after", None)
                if self.debug:
                    dbg = self.nc.dram_tensor("DBG_MOD", [128, DEPTH * 288], F32, kind="ExternalOutput").ap()
                    self.T.dma("sp", dbg, self.c["MOD"].t[:], reads=[self.c["MOD"]])
                    dbg2 = self.nc.dram_tensor("DBG_S", [128, DEPTH * 96], F32, kind="ExternalOutput").ap()
                    self.T.dma("sp", dbg2, self.c["S"].t[:], reads=[self.c["S"]])
                stopped = stop == ("prologue",)
                for layer in range(self.n_layers):
                    if stopped:
                        break
                    with contextlib.ExitStack() as des:
                        self.dense_phase(des, layer)
                    self.T.barrier()
                    if stop == ("dense", layer):
                        stopped = True
                        break
                    with contextlib.ExitStack() as mes:
                        self.mixer_phase(mes, layer)
                    self.T.barrier()
                    if stop == ("mixer", layer):
                        stopped = True
                        break
                if not stopped:
                    with contextlib.ExitStack() as des:
                        self.dense_phase(des, self.n_layers)
                self.T.barrier()
        return nc

    def consts(self, es):
        T = self.T
        c = self.c = {}

        def mk(name, shape, dt):
            c[name] = self.sb(es, name, shape, dt)
            return c[name]
        self.P = [self.ps(es, f"P{i}", [128, 512], F32) for i in range(8)]
        idf = mk("ident_f", [128, 128], F32)
        T.op("pool", lambda e: e.memset(idf.t[:], 1.0), writes=[idf])
        T.op("pool", lambda e: e.affine_select(idf.t[:], idf.t[:], pattern=[[-1, 128]], compare_op=ALU.is_equal,
                                               fill=0.0, base=0, channel_multiplier=1), reads=[idf], writes=[idf])
        idb = mk("ident_b", [128, 128], BF16)
        T.op("dve", lambda e: e.tensor_copy(idb.t[:], idf.t[:]), reads=[idf], writes=[idb])
        of = mk("ones_f", [128, 128], F32)
        T.op("pool", lambda e: e.memset(of.t[:], 1.0), writes=[of])
        ob = mk("ones_b", [128, 128], BF16)
        T.op("pool", lambda e: e.memset(ob.t[:], 1.0), writes=[ob])
        for d in range(2):
            sgn = 1 if d == 0 else -1
            tri = mk(f"tri{d}", [128, 128], F32)
            T.op("pool", lambda e: e.memset(tri.t[:], 1.0), writes=[tri])
            T.op("pool", lambda e: e.affine_select(tri.t[:], tri.t[:], pattern=[[sgn, 128]], compare_op=ALU.is_ge,
                                                   fill=0.0, base=0, channel_multiplier=-sgn), reads=[tri], writes=[tri])
            msk = mk(f"msk{d}", [128, 128], F32)
            T.op("dve", lambda e: e.tensor_scalar(msk.t[:], tri.t[:], -BIG, BIG, ALU.mult, ALU.add), reads=[tri], writes=[msk])
        mk("MOD", [128, DEPTH * 6 * 16 * 3], F32)
        mk("S", [128, DEPTH * 2 * 16 * 3], F32)
        mk("MODB", [128, DEPTH * 96], F32)
        mk("NW", [128, 128], F32)
        mk("FNW", [128, 16], F32)
        mk("SCW", [128, 72], F32); mk("SCB", [128, 24], F32)
        mk("MCW", [128, 96], F32); mk("MCB", [128, 32], F32)
        mk("SNW", [128, 16], F32); mk("LBR", [128, 16], F32); mk("HNW", [128, 16], F32); mk("MNW", [128, 32], F32)
        mk("LB", [128, 16], F32); mk("LB1", [128, 16], F32)
        mk("DSK", [128, 16], F32)
        mk("GB", [128, 32], F32)
        mk("DTB", [128, 64], F32)
        mk("EA", [128, 64], F32)
        mk("scT", [128, 16 * 3], BF16)

    def MODv(self, l, m, dc, v):
        i = ((l * 6 + m) * 16 + dc) * 3 + v
        return self.c["MOD"].t[:, i:i + 1]

    def Sv(self, l, i, dc, v):
        j = ((l * 2 + i) * 16 + dc) * 3 + v
        return self.c["S"].t[:, j:j + 1]

    def load_vecT(self, es, dst, src2d, k):
        T = self.T
        st = self.sb(es, "vst", [k, 128], F32)
        T.dma("sp", st.t[:], src2d, writes=[st])
        P = self.P[7]
        T.op("pe", lambda e: e.transpose(P.t[:, 0:k], st.t[:], self.c["ident_f"].t[0:k, 0:k]),
             reads=[st, self.c["ident_f"]], writes=[P])
        T.op("dve", lambda e: e.tensor_copy(dst.t[:, 0:k], P.t[:, 0:k]), reads=[P], writes=[dst])

    def prologue(self, es):
        T, c, I, nc = self.T, self.c, self.I, self.nc
        v128 = lambda ap, pat, **kw: ap.rearrange(pat, **kw)
        self.load_vecT(es, c["NW"], I["norm_w"].rearrange("l i (c p) -> (l i c) p", p=128), 128)
        self.load_vecT(es, c["FNW"], I["final_norm_w"].rearrange("(c p) -> c p", p=128), 16)
        self.load_vecT(es, c["SCW"], I["ssd_conv_w"].rearrange("e j (c p) -> (e j c) p", p=128), 72)
        self.load_vecT(es, c["SCB"], I["ssd_conv_b"].rearrange("e (c p) -> (e c) p", p=128), 24)
        self.load_vecT(es, c["MCW"], I["mlstm_conv_w"].rearrange("e j (c p) -> (e j c) p", p=128), 96)
        self.load_vecT(es, c["MCB"], I["mlstm_conv_b"].rearrange("e (c p) -> (e c) p", p=128), 32)
        self.load_vecT(es, c["SNW"], I["ssd_norm_w"].rearrange("e (c p) -> (e c) p", p=128), 16)
        self.load_vecT(es, c["LBR"], I["hgrn_lb"].rearrange("e (c p) -> (e c) p", p=128), 16)
        self.load_vecT(es, c["HNW"], I["hgrn_norm_w"].rearrange("e (c p) -> (e c) p", p=128), 16)
        self.load_vecT(es, c["MNW"], I["mlstm_norm_w"].rearrange("e (c p) -> (e c) p", p=128), 32)
        for l in range(DEPTH):
            tmp = self.sb(es, "mbt", [128, 96], F32)
            self.load_vecT(es, tmp, I["mod_b"][l].rearrange("(c p) -> c p", p=128), 96)
            T.op("dve", lambda e: e.tensor_copy(c["MODB"].t[:, l * 96:(l + 1) * 96], tmp.t[:]), reads=[tmp], writes=[c["MODB"]])
        LB, LB1, LBR = c["LB"], c["LB1"], c["LBR"]
        T.op("pool", lambda e: e.memset(LB.t[:, 0:8], 0.0), writes=[LB])
        T.op("dve", lambda e: e.tensor_tensor(LB.t[:, 8:16], LBR.t[:, 8:16], LBR.t[:, 0:8], ALU.subtract), reads=[LBR, LB], writes=[LB])
        T.op("act", lambda e: e.activation(LB.t[:, 8:16], LB.t[:, 8:16], AF.Sigmoid), reads=[LB], writes=[LB])
        T.op("dve", lambda e: e.tensor_scalar(LB1.t[:], LB.t[:], -1.0, 1.0, ALU.mult, ALU.add), reads=[LB], writes=[LB1])
        T.dma("sp", c["GB"].t[:], I["mlstm_gate_b"].rearrange("o a b -> (o a b)").partition_broadcast(128), writes=[c["GB"]])
        T.dma("sp", c["DTB"].t[:], I["ssd_dt_bias"].rearrange("e d h -> (e d h)").partition_broadcast(128), writes=[c["DTB"]])
        T.dma("sp", c["EA"].t[:], I["ssd_a_log"].rearrange("e d h -> (e d h)").partition_broadcast(128), writes=[c["EA"]])
        T.op("act", lambda e: e.activation(c["EA"].t[:], c["EA"].t[:], AF.Exp), reads=[c["EA"]], writes=[c["EA"]])
        for o in range(2):
            T.op("dve", lambda e: e.tensor_scalar_add(c["GB"].t[:, o * 16:o * 16 + 8], c["GB"].t[:, o * 16:o * 16 + 8], -math.log(16.0)),
                 reads=[c["GB"]], writes=[c["GB"]])
        for e_ in range(2):
            for h in range(16):
                T.dma("sp", c["DSK"].t[(h % 2) * 64:(h % 2) * 64 + 64, e_ * 8 + h // 2:e_ * 8 + h // 2 + 1],
                      I["ssd_d"][e_, h:h + 1].partition_broadcast(64), writes=[c["DSK"]])
        cT = self.sb(es, "cT", [128, 48], F32)
        self.load_vecT(es, cT, I["cvec"].rearrange("v (c p) -> (v c) p", p=128), 48)
        scv = c["scT"].t[:].rearrange("p (k v) -> p k v", v=3)
        for v in range(3):
            T.op("act", lambda e: e.activation(scv[:, :, v], cT.t[:, v * 16:(v + 1) * 16], AF.Silu), reads=[cT], writes=[c["scT"]])
        wb = [self.sb(es, f"mw{i}", [128, 16, 512], BF16) for i in range(3)]
        P = self.P
        n = 0
        for l in range(DEPTH):
            Pm = P[l % 2]
            for jb in range(24):
                w = wb[n % 3]; n += 1
                T.dma("pool", w.t[:], I["mod_w"][l][:, jb * 512:(jb + 1) * 512].rearrange("(k p) n -> p k n", p=128), writes=[w])
                for jt in range(4):
                    j = jb * 4 + jt
                    T.group("pe", [
                        (lambda e, kc=kc: e.matmul(Pm.t[:, j * 3:j * 3 + 3], w.t[:, kc, jt * 128:(jt + 1) * 128], scv[:, kc, :],
                                                   start=(kc == 0), stop=(kc == 15))) for kc in range(16)],
                        reads=[w, c["scT"]], writes=[Pm])
            MODl = c["MOD"].t[:, l * 288:(l + 1) * 288].rearrange("p (j v) -> p j v", v=3)
            T.op("dve", lambda e: e.tensor_tensor(MODl, Pm.t[:, 0:288].rearrange("p (j v) -> p j v", v=3),
                                                  c["MODB"].t[:, l * 96:(l + 1) * 96].unsqueeze(2).to_broadcast([128, 96, 3]), ALU.add),
                 reads=[Pm, c["MODB"]], writes=[c["MOD"]])
            for i in range(2):
                Sl = c["S"].t[:, (l * 2 + i) * 48:(l * 2 + i + 1) * 48].rearrange("p (k v) -> p k v", v=3)
                sc = c["MOD"].t[:, ((l * 6 + 1 + 3 * i) * 16) * 3:((l * 6 + 2 + 3 * i) * 16) * 3].rearrange("p (k v) -> p k v", v=3)
                nw = c["NW"].t[:, (l * 2 + i) * 16:(l * 2 + i + 1) * 16].unsqueeze(2).to_broadcast([128, 16, 3])
                T.op("dve", lambda e: e.tensor_scalar_add(Sl, sc, 1.0), reads=[c["MOD"]], writes=[c["S"]])
                T.op("dve", lambda e: e.tensor_tensor(Sl, Sl, nw, ALU.mult), reads=[c["S"], c["NW"]], writes=[c["S"]])

    def dense_phase(self, es, layer):
        T, c, I, P = self.T, self.c, self.I, self.P
        L = self.n_layers
        self.xT = self.sb(es, "xT", [128, 16, 512], F32)
        self.actT = self.sb(es, "actT", [128, 16, 512], BF16)
        self.wb = [self.sb(es, f"wb{i}", [128, 16 * 512], BF16) for i in range(3)]
        self.wn = 0
        self.f32t = [self.sb(es, f"f32t{i}", [128, 512], F32) for i in range(4)]
        self.fn = 0
        self.rstd = self.sb(es, "rstd", [128, 512], F32)
        self.so = [self.sb(es, f"so{i}", [128, 4, 512], BF16) for i in range(2)]
        self.son = 0
        self.pn = 0
        if layer > 0:
            self.aT = self.sb(es, "aT", [128, 64, 512], BF16)
            self.hA = self.sb(es, "hA", [128, 4, 512], BF16)
            self.hB = self.sb(es, "hB", [128, 4, 512], BF16)
            self.hC = self.so[0]
            self.hD = self.so[1]
            base = self.sb(es, "hs", [128, 2048], F32)
            self.hs = Buf(base.t[:].rearrange("p (a b) -> p a b", b=512), base.name)
            self.hs.r = base.r
            self.xin = base
        if layer == 0:
            self.xin = self.sb(es, "xin", [128, 2048], F32)
        for tb in range(NBLK):
            if layer == L and tb == 8:
                continue
            v = 2 if tb == 8 else tb // 4
            c0 = tb * 512
            xT = self.xT
            if layer == 0:
                self.load_x_input(tb)
            else:
                T.dma("sp", xT.t[:], self.XT.t[:, c0:c0 + 512].rearrange("(k p) n -> p k n", p=128),
                      reads=[self.XT.k(tb)], writes=[xT])
                self.stage_c(layer - 1, tb, v)
            if layer < L:
                self.norm_mod(layer, 0, v)
                self.stage_a(layer, tb)
                T.dma("sp", self.XT.t[:, c0:c0 + 512].rearrange("(k p) n -> p k n", p=128), xT.t[:],
                      reads=[xT], writes=[self.XT.k(tb)])
            else:
                self.final_out(tb)

    def nextP(self):
        p = self.P[self.pn % 4]
        self.pn += 1
        return p

    def nextF(self):
        f = self.f32t[self.fn % 4]
        self.fn += 1
        return f

    def load_x_input(self, tb):
        T, I, P = self.T, self.I, self.P
        idf = self.c["ident_f"]
        for tt in range(4):
            if tb < 8:
                src = I["x"][tb // 4, (tb % 4) * 512 + tt * 128:(tb % 4) * 512 + (tt + 1) * 128, :]
            else:
                src = I["ctx"][tt // 2, (tt % 2) * 128:(tt % 2 + 1) * 128, :]
            T.dma("sp", self.xin.t[:], src, writes=[self.xin])
            for g in range(4):
                Pg = self.nextP()
                T.group("pe", [(lambda e, j=j: e.transpose(Pg.t[:, j * 128:(j + 1) * 128],
                                                           self.xin.t[:, (g * 4 + j) * 128:(g * 4 + j + 1) * 128], idf.t[:])) for j in range(4)],
                        reads=[self.xin, idf], writes=[Pg])
                T.op("act" if g % 2 else "dve",
                     (lambda e: e.activation(self.xT.t[:, g * 4:(g + 1) * 4, tt * 128:(tt + 1) * 128],
                                             Pg.t[:].rearrange("p (j t) -> p j t", t=128), AF.Copy)) if g % 2 else
                     (lambda e: e.tensor_copy(self.xT.t[:, g * 4:(g + 1) * 4, tt * 128:(tt + 1) * 128],
                                              Pg.t[:].rearrange("p (j t) -> p j t", t=128))),
                     reads=[Pg], writes=[self.xT])

    def rms_rstd(self, src_fn, nchunks, nfeat, reads):
        T, P = self.T, self.P
        Pss = P[4]
        of = self.c["ones_f"]
        for j in range(nchunks):
            sq = self.nextF()
            T.op("act", lambda e: e.activation(sq.t[:], src_fn(j), AF.Square), reads=reads, writes=[sq])
            T.op("pe", lambda e: e.matmul(Pss.t[:], of.t[:], sq.t[:], start=(j == 0), stop=(j == nchunks - 1)),
                 reads=[sq, of], writes=[Pss])
        T.op("act", lambda e: e.activation(self.rstd.t[:], Pss.t[:], AF.Ln, bias=EPS, scale=1.0 / nfeat), reads=[Pss], writes=[self.rstd])
        T.op("act", lambda e: e.activation(self.rstd.t[:], self.rstd.t[:], AF.Exp, scale=-0.5), reads=[self.rstd], writes=[self.rstd])

    def norm_mod(self, l, i, v):
        T = self.T
        xT, actT = self.xT, self.actT
        self.rms_rstd(lambda j: xT.t[:, j, :], 16, D, [xT])
        for dc in range(16):
            tmp = self.nextF()
            T.op("dve", lambda e: e.tensor_tensor(tmp.t[:], xT.t[:, dc, :], self.rstd.t[:], ALU.mult), reads=[xT, self.rstd], writes=[tmp])
            T.op("act", lambda e: e.activation(actT.t[:, dc, :], tmp.t[:], AF.Identity, bias=self.MODv(l, 3 * i, dc, v), scale=self.Sv(l, i, dc, v)),
                 reads=[tmp, self.c["MOD"], self.c["S"]], writes=[actT])

    def load_w(self, src, wc):
        w = self.wb[self.wn % 3]
        self.wn += 1
        k = src.shape[0] // 128
        view = w.t[:, 0:k * wc].rearrange("p (k n) -> p k n", n=wc)
        self.T.dma("pool", view, src.rearrange("(k p) n -> p k n", p=128), writes=[w])
        return w, view

    def proj_fm(self, wsrc, col0, ncols, evac, nk=16, rhs_fn=None):
        T = self.T
        if rhs_fn is None:
            rhs_fn = lambda kc: self.actT.t[:, kc, :]
            rd = [self.actT]
        else:
            rd = [self.aT]
        wcb = 512 if nk == 16 else 128
        ct = 0
        for b0 in range(0, ncols, wcb):
            wc = min(wcb, ncols - b0)
            w, view = self.load_w(wsrc[:, col0 + b0:col0 + b0 + wc], wc)
            for t0 in range(0, wc, 128):
                Pt = self.nextP()
                T.group("pe", [(lambda e, kc=kc: e.matmul(Pt.t[:], view[:, kc, t0:t0 + 128], rhs_fn(kc), start=(kc == 0), stop=(kc == nk - 1)))
                               for kc in range(nk)], reads=[w] + rd, writes=[Pt])
                evac(ct, Pt)
                ct += 1

    def proj_tm(self, wsrc, col0, ncols, evac):
        T = self.T
        for b0 in range(0, ncols, 512):
            wc = min(512, ncols - b0)
            w, view = self.load_w(wsrc[:, col0 + b0:col0 + b0 + wc], wc)
            for tt in range(4):
                Pt = self.nextP()
                T.group("pe", [(lambda e, kc=kc: e.matmul(Pt.t[:, 0:wc], self.actT.t[:, kc, tt * 128:(tt + 1) * 128], view[:, kc, :],
                                                          start=(kc == 0), stop=(kc == 15))) for kc in range(16)],
                        reads=[w, self.actT], writes=[Pt])
                evac(b0, wc, tt, Pt)

    class FmOut:
        def __init__(self, prog, row0, tb):
            self.p = prog; self.row0 = row0; self.tb = tb; self.n = 0; self.cur = None

        def slot(self):
            if self.n % 4 == 0:
                self.cur = self.p.so[self.p.son % 2]
                self.p.son += 1
            j = self.n % 4
            self.n += 1
            return self.cur, self.cur.t[:, j, :]

        def done_tile(self, last=False):
            if self.n % 4 == 0 or last:
                cnt = (self.n - 1) % 4 + 1
                r0 = self.row0 + (self.n - cnt) * 128
                p = self.p
                c0 = self.tb * 512
                p.T.dma("act", p.U_FM.t[r0:r0 + cnt * 128, c0:c0 + 512].rearrange("(t p) n -> p t n", p=128), self.cur.t[:, 0:cnt, :],
                        reads=[self.cur], writes=[p.U_FM.k((r0 // 128, self.tb))])

    def fm_group(self, wsrc, col0, ncols, row0, tb, evac_to):
        out = Prog.FmOut(self, row0, tb)
        ntile = ncols // 128

        def ev(ct, Pt):
            buf, dst = out.slot()
            evac_to(ct, Pt, buf, dst)
            out.done_tile(last=(ct == ntile - 1))
        self.proj_fm(wsrc, col0, ncols, ev)

    def conv_evac(self, tb, CW, CB, base_w, base_b, nchan_chunks):
        T = self.T
        seg = 256 if tb == 8 else 64

        def ev(ct, Pt, buf, dst):
            cv = self.nextF()
            w0 = CW.t[:, base_w + ct:base_w + ct + 1]
            w1 = CW.t[:, base_w + nchan_chunks + ct:base_w + nchan_chunks + ct + 1]
            w2 = CW.t[:, base_w + 2 * nchan_chunks + ct:base_w + 2 * nchan_chunks + ct + 1]
            bb = CB.t[:, base_b + ct:base_b + ct + 1]
            T.op("act", lambda e: e.activation(cv.t[:], Pt.t[:], AF.Identity, bias=bb, scale=w1), reads=[Pt, CW, CB], writes=[cv])
            cvv = cv.t[:].rearrange("p (s l) -> p s l", l=seg)
            pv = Pt.t[:].rearrange("p (s l) -> p s l", l=seg)
            T.op("dve", lambda e: e.scalar_tensor_tensor(cvv[:, :, 1:seg], pv[:, :, 0:seg - 1], w0, cvv[:, :, 1:seg], ALU.mult, ALU.add),
                 reads=[Pt, cv, CW], writes=[cv])
            T.op("dve", lambda e: e.scalar_tensor_tensor(cvv[:, :, 0:seg - 1], pv[:, :, 1:seg], w2, cvv[:, :, 0:seg - 1], ALU.mult, ALU.add),
                 reads=[Pt, cv, CW], writes=[cv])
            T.op("act", lambda e: e.activation(dst, cv.t[:], AF.Silu), reads=[cv], writes=[buf])
        return ev

    def act_evac(self, func):
        T = self.T

        def ev(ct, Pt, buf, dst):
            T.op("act", lambda e: e.activation(dst, Pt.t[:], func), reads=[Pt], writes=[buf])
        return ev

    def copy_evac(self):
        T = self.T

        def ev(ct, Pt, buf, dst):
            T.op("dve", lambda e: e.tensor_copy(dst, Pt.t[:]), reads=[Pt], writes=[buf])
        return ev

    def tm_evac(self, tb, dst_buf, col_off, dt):
        T = self.T
        c0 = tb * 512

        def ev(b0, wc, tt, Pt):
            if dt == BF16:
                st = self.so[self.son % 2]; self.son += 1
                sv = st.t[:, 0, 0:wc]
            else:
                st = self.nextF()
                sv = st.t[:, 0:wc]
            T.op("dve", lambda e: e.tensor_copy(sv, Pt.t[:, 0:wc]), reads=[Pt], writes=[st])
            r0 = c0 + tt * 128
            T.dma("act", dst_buf.t[r0:r0 + 128, col_off + b0:col_off + b0 + wc], sv, reads=[st],
                  writes=[dst_buf.k((r0 // 128, (col_off + b0) // 512))])
        return ev

    def stage_a(self, layer, tb):
        I, c = self.I, self.c
        odd, idx = self.kind(layer)
        if not odd:
            e_ = idx
            W = I["even_w_in"][e_]
            self.fm_group(W, 0, 1024, 0, tb, self.act_evac(AF.Silu))
            self.fm_group(W, 1024, 1536, 1024, tb, self.conv_evac(tb, c["SCW"], c["SCB"], e_ * 36, e_ * 12, 12))
            self.fm_group(W, 2592, 1024, 2560, tb, self.act_evac(AF.Silu))
            self.fm_group(W, 3616, 2048, 3584, tb, self.copy_evac())
            self.fm_group(W, 6688, 1024, 5632, tb, self.act_evac(AF.Silu))
            self.proj_tm(W, 5664, 1024, self.tm_evac(tb, self.U_TM, 0, BF16))
            self.proj_tm(W, 2560, 32, self.tm_evac(tb, self.U_G, 0, F32))
        else:
            o_ = idx
            W = I["odd_w_in"][o_]
            self.fm_group(W, 0, 2048, 0, tb, self.conv_evac(tb, c["MCW"], c["MCB"], o_ * 48, o_ * 16, 16))
            self.fm_group(W, 4096, 2048, 2048, tb, self.act_evac(AF.Sigmoid))
            self.proj_tm(W, 2048, 2048, self.tm_evac(tb, self.U_TM, 0, BF16))
            self.proj_tm(W, 6144, 16, self.tm_evac(tb, self.U_G, 0, F32))

    def xk(self, lst=None):
        return [self.xT.k(i) for i in (range(16) if lst is None else lst)]

    def stage_c(self, l, tb, v):
        T, c, I = self.T, self.c, self.I
        xT, aT = self.xT, self.aT
        self.finalize_mixer(l, tb)
        odd, idx = self.kind(l)
        W = I["odd_w_out"][idx] if odd else I["even_w_out"][idx]

        def ev_res(m):
            def ev(ct, Pt):
                T.op("dve", lambda e: e.scalar_tensor_tensor(xT.t[:, ct, :], Pt.t[:], self.MODv(l, m, ct, v), xT.t[:, ct, :], ALU.mult, ALU.add),
                     reads=[Pt, c["MOD"], xT], writes=[xT])
            return ev
        self.proj_fm(W, 0, 2048, ev_res(2))
        self.norm_mod(l, 1, v)

        def ev1(ct, Pt):
            tmp = self.nextF()
            T.op("act", lambda e: e.activation(tmp.t[:], Pt.t[:], AF.Relu), reads=[Pt], writes=[tmp])
            T.op("dve", lambda e: e.tensor_tensor(aT.t[:, ct, :], tmp.t[:], tmp.t[:], ALU.mult), reads=[tmp], writes=[aT])
        self.proj_fm(I["mlp_w1"][l], 0, 8192, ev1)
        self.proj_fm(I["mlp_w2"][l], 0, 2048, ev_res(5), nk=64, rhs_fn=lambda kc: aT.t[:, kc, :])

    def ld_blk(self, dst, src_buf, r0, tb):
        c0 = tb * 512
        self.T.dma("sp", dst.t[:], src_buf.t[r0:r0 + 512, c0:c0 + 512].rearrange("(k p) n -> p k n", p=128), writes=[dst])

    def finalize_mixer(self, l, tb):
        T, c = self.T, self.c
        hA, hB, hC, hD, hs, actT = self.hA, self.hB, self.hC, self.hD, self.hs, self.actT
        odd, idx = self.kind(l)
        for gI in range(4):
            r0 = gI * 512
            self.ld_blk(hA, self.HF[0], r0, tb)
            self.ld_blk(hB, self.HF[1], r0, tb)
            T.op("dve", lambda e: e.tensor_tensor(hs.t[:], hA.t[:], hB.t[:], ALU.add), reads=[hA, hB], writes=[hs])
            if odd:
                self.ld_blk(hC, self.U_FM, 2048 + r0, tb)
                self.rms_rstd(lambda j: hs.t[:, j, :], 4, 512, [hs])
                for j in range(4):
                    dc = gI * 4 + j
                    tmp = self.nextF()
                    T.op("dve", lambda e: e.tensor_tensor(tmp.t[:], hs.t[:, j, :], self.rstd.t[:], ALU.mult), reads=[hs, self.rstd], writes=[tmp])
                    T.op("dve", lambda e: e.scalar_tensor_tensor(actT.t[:, dc, :], tmp.t[:], c["MNW"].t[:, idx * 16 + dc:idx * 16 + dc + 1],
                                                                 hC.t[:, j, :], ALU.mult, ALU.mult), reads=[tmp, hC, c["MNW"]], writes=[actT])
            elif gI < 2:
                e_ = idx
                self.ld_blk(hC, self.U_FM, 1024 + r0, tb)
                self.ld_blk(hD, self.U_FM, r0, tb)
                for j in range(4):
                    dc = gI * 4 + j
                    T.op("dve", lambda e: e.scalar_tensor_tensor(hs.t[:, j, :], hC.t[:, j, :], c["DSK"].t[:, e_ * 8 + dc:e_ * 8 + dc + 1],
                                                                 hs.t[:, j, :], ALU.mult, ALU.add), reads=[hC, hs, c["DSK"]], writes=[hs])
                T.op("dve", lambda e: e.tensor_tensor(hs.t[:], hs.t[:], hD.t[:], ALU.mult), reads=[hs, hD], writes=[hs])
                self.rms_rstd(lambda j: hs.t[:, j, :], 4, 512, [hs])
                for j in range(4):
                    dc = gI * 4 + j
                    tmp = self.nextF()
                    T.op("dve", lambda e: e.tensor_tensor(tmp.t[:], hs.t[:, j, :], self.rstd.t[:], ALU.mult), reads=[hs, self.rstd], writes=[tmp])
                    T.op("act", lambda e: e.activation(actT.t[:, dc, :], tmp.t[:], AF.Copy, scale=c["SNW"].t[:, e_ * 8 + dc:e_ * 8 + dc + 1]),
                         reads=[tmp, c["SNW"]], writes=[actT])
            else:
                e_ = idx
                self.ld_blk(hD, self.U_FM, 5632 + (gI - 2) * 512, tb)
                for j in range(4):
                    dc = gI * 4 + j
                    self.rms_rstd(lambda _: hs.t[:, j, :], 1, 128, [hs])
                    tmp = self.nextF()
                    T.op("dve", lambda e: e.tensor_tensor(tmp.t[:], hs.t[:, j, :], self.rstd.t[:], ALU.mult), reads=[hs, self.rstd], writes=[tmp])
                    T.op("dve", lambda e: e.scalar_tensor_tensor(actT.t[:, dc, :], tmp.t[:], c["HNW"].t[:, e_ * 8 + dc - 8:e_ * 8 + dc - 7],
                                                                 hD.t[:, j, :], ALU.mult, ALU.mult), reads=[tmp, hD, c["HNW"]], writes=[actT])

    def final_out(self, tb):
        T, c = self.T, self.c
        xT = self.xT
        idf = c["ident_f"]
        self.rms_rstd(lambda j: xT.t[:, j, :], 16, D, [xT])
        for dc in range(16):
            tmp = self.nextF()
            T.op("dve", lambda e: e.tensor_tensor(tmp.t[:], xT.t[:, dc, :], self.rstd.t[:], ALU.mult), reads=[xT, self.rstd], writes=[tmp])
            T.op("act", lambda e: e.activation(xT.t[:, dc, :], tmp.t[:], AF.Copy, scale=c["FNW"].t[:, dc:dc + 1]), reads=[tmp, c["FNW"]], writes=[xT])
        for tt in range(4):
            for g in range(4):
                Pg = self.nextP()
                T.group("pe", [(lambda e, j=j: e.transpose(Pg.t[:, j * 128:(j + 1) * 128], xT.t[:, g * 4 + j, tt * 128:(tt + 1) * 128], idf.t[:]))
                               for j in range(4)], reads=[xT, idf], writes=[Pg])
                if g % 2:
                    T.op("act", lambda e: e.activation(self.xin.t[:, g * 512:(g + 1) * 512], Pg.t[:], AF.Copy), reads=[Pg], writes=[self.xin])
                else:
                    T.op("dve", lambda e: e.tensor_copy(self.xin.t[:, g * 512:(g + 1) * 512], Pg.t[:]), reads=[Pg], writes=[self.xin])
            r0 = (tb % 4) * 512 + tt * 128
            T.dma("sp", self.out[tb // 4, r0:r0 + 128, :], self.xin.t[:], reads=[self.xin], writes=[self.r_out])

    def mixer_phase(self, es, layer):
        odd, idx = self.kind(layer)
        if odd:
            self.mlstm_phase(es, idx)
        else:
            self.even_phase(es, idx)

    def trps(self, i):
        return self.P[i].t[:].bitcast(BF16)

    def gate_cums(self, d, src, n, ws):
        T, c, P = self.T, self.c, self.P
        Pg = P[7]
        T.op("pe", lambda e: e.matmul(Pg.t[:, 256:256 + n], c[f"tri{d}"].t[:], src.t[:, 0:n], start=True, stop=True),
             reads=[src, c[f"tri{d}"]], writes=[Pg])
        T.op("pe", lambda e: e.matmul(Pg.t[:, 256 + n:256 + 2 * n], c["ones_f"].t[:], src.t[:, 0:n], start=True, stop=True),
             reads=[src, c["ones_f"]], writes=[Pg])
        T.op("dve", lambda e: e.tensor_copy(ws["cs"].t[:, 0:2 * n], Pg.t[:, 256:256 + 2 * n]), reads=[Pg], writes=[ws["cs"]])
        return ws["cs"]

    def decay_mats(self, d, src, h, bias_ap, ws, need_e=True):
        T, c, P = self.T, self.c, self.P
        PA = P[1]
        bc = src.t[:, h:h + 1].to_broadcast([128, 128])
        fns = [lambda e: e.matmul(PA.t[:, 128:256], bc, c[f"tri{d}"].t[:], start=True, stop=False),
               lambda e: e.matmul(PA.t[:, 128:256], c["ident_f"].t[:], c[f"msk{d}"].t[:], start=False, stop=True)]
        if need_e:
            fns.insert(0, lambda e: e.matmul(PA.t[:, 0:128], bc, c[f"tri{d}"].t[:], start=True, stop=True))
        T.group("pe", fns, reads=[src, c[f"tri{d}"], c[f"msk{d}"], c["ident_f"]], writes=[PA])
        T.op("act", lambda e: e.activation(ws["Dm"].t[:], PA.t[:, 128:256], AF.Exp, bias=bias_ap, scale=-1.0),
             reads=[PA, ws["bias"]], writes=[ws["Dm"]])
        if need_e:
            T.op("act", lambda e: e.activation(ws["E"].t[:], PA.t[:, 0:128], AF.Exp, scale=-1.0), reads=[PA], writes=[ws["E"]])

    def mlstm_phase(self, es, o_):
        T, c, P = self.T, self.c, self.P
        chains = [(b, d) for b in range(2) for d in range(2)]
        Cf = {ch: self.sb(es, "Cf", [128, 4, 2, 640], F32) for ch in chains}
        Cb = {ch: self.sb(es, "Cb", [128, 4, 2, 640], BF16) for ch in chains}
        for ch in chains:
            T.op("pool", lambda e: e.memset(Cf[ch].t[:], 0.0), writes=[Cf[ch]])
            T.op("pool", lambda e: e.memset(Cb[ch].t[:], 0.0), writes=[Cb[ch]])
        sets = []
        for i in range(2):
            w = {}
            w["qT"] = self.sb(es, "qT", [128, 8, 128], BF16)
            w["kT"] = self.sb(es, "kT", [128, 8, 128], BF16)
            w["Vt"] = self.sb(es, "Vt", [128, 2048], BF16)
            w["g"] = self.sb(es, "g", [128, 16], F32)
            w["li"] = self.sb(es, "li", [128, 4], F32)
            w["sp"] = self.sb(es, "sp", [128, 4], F32)
            w["cs"] = self.sb(es, "cs", [128, 8], F32)
            w["bias"] = self.sb(es, "bias", [128, 4], F32)
            w["wsx"] = self.sb(es, "wsx", [128, 4], F32)
            w["cdec"] = self.sb(es, "cdec", [128, 4], F32)
            w["Dm"] = self.sb(es, "Dm", [128, 128], F32)
            w["E"] = self.sb(es, "E", [128, 128], F32)
            w["WT"] = self.sb(es, "WT", [128, 128], BF16)
            w["qs"] = self.sb(es, "qs", [128, 2, 128], BF16)
            w["ke"] = self.sb(es, "ke", [128, 256], BF16)
            w["dd"] = self.sb(es, "dd", [128, 128], F32)
            w["ho"] = self.sb(es, "ho", [128, 16, 128], BF16)
            sets.append(w)
        GB = c["GB"]
        n = 0
        for step in range(18):
            for ch in chains:
                b, d = ch
                c0 = chain_chunks(b, d)[step]
                w = sets[n % 2]; n += 1
                T.dma("sp", w["qT"].t[:], self.U_FM.t[0:1024, c0:c0 + 128].rearrange("(k p) n -> p k n", p=128), writes=[w["qT"]])
                T.dma("sp", w["kT"].t[:], self.U_FM.t[1024:2048, c0:c0 + 128].rearrange("(k p) n -> p k n", p=128), writes=[w["kT"]])
                T.dma("sp", w["Vt"].t[:], self.U_TM.t[c0:c0 + 128, 0:2048], writes=[w["Vt"]])
                T.dma("sp", w["g"].t[:], self.U_G.t[c0:c0 + 128, 0:16], writes=[w["g"]])
                g, li, sp = w["g"], w["li"], w["sp"]
                T.op("dve", lambda e: e.tensor_tensor(li.t[:], g.t[:, d * 4:d * 4 + 4], GB.t[:, o_ * 16 + d * 4:o_ * 16 + d * 4 + 4], ALU.add),
                     reads=[g, GB], writes=[li])
                T.op("dve", lambda e: e.tensor_tensor(sp.t[:], g.t[:, 8 + d * 4:12 + d * 4], GB.t[:, o_ * 16 + 8 + d * 4:o_ * 16 + 12 + d * 4], ALU.add),
                     reads=[g, GB], writes=[sp])
                T.op("act", lambda e: e.activation(sp.t[:], sp.t[:], AF.Exp, scale=-1.0), reads=[sp], writes=[sp])
                T.op("act", lambda e: e.activation(sp.t[:], sp.t[:], AF.Ln, bias=1.0), reads=[sp], writes=[sp])
                cs = self.gate_cums(d, sp, 4, w)
                bias, wsx, cdec = w["bias"], w["wsx"], w["cdec"]
                T.op("dve", lambda e: e.tensor_tensor(bias.t[:], li.t[:], cs.t[:, 0:4], ALU.add), reads=[li, cs], writes=[bias])
                T.op("dve", lambda e: e.tensor_tensor(wsx.t[:], bias.t[:], cs.t[:, 4:8], ALU.subtract), reads=[bias, cs], writes=[wsx])
                T.op("act", lambda e: e.activation(wsx.t[:], wsx.t[:], AF.Exp), reads=[wsx], writes=[wsx])
                T.op("act", lambda e: e.activation(cdec.t[:], cs.t[:, 4:8], AF.Exp, scale=-1.0), reads=[cs], writes=[cdec])
                qT, kT, Vt, WT, qs, ke, dd, ho = w["qT"], w["kT"], w["Vt"], w["WT"], w["qs"], w["ke"], w["dd"], w["ho"]
                cf, cb = Cf[ch], Cb[ch]
                for h in range(4):
                    PS = P[0]
                    T.group("pe", [(lambda e, j=j: e.matmul(PS.t[:, 0:128], kT.t[:, h * 2 + j, :], qT.t[:, h * 2 + j, :], start=(j == 0), stop=(j == 1)))
                                   for j in range(2)], reads=[kT, qT], writes=[PS])
                    self.decay_mats(d, sp, h, bias.t[:, h:h + 1], w)
                    T.op("dve", lambda e: e.tensor_tensor(WT.t[:], PS.t[:, 0:128], w["Dm"].t[:], ALU.mult), reads=[PS, w["Dm"]], writes=[WT])
                    T.op("dve", lambda e: e.tensor_tensor(qs.t[:], qT.t[:, h * 2:h * 2 + 2, :], w["E"].t[:].unsqueeze(1).to_broadcast([128, 2, 128]), ALU.mult),
                         reads=[qT, w["E"]], writes=[qs])
                    PT = P[2]
                    trv = self.trps(2)
                    T.group("pe", [(lambda e, j=j: e.transpose(trv[:, j * 128:(j + 1) * 128], kT.t[:, h * 2 + j, :], c["ident_b"].t[:])) for j in range(2)],
                            reads=[kT, c["ident_b"]], writes=[PT])
                    T.op("act", lambda e: e.activation(ke.t[:], trv[:, 0:256], AF.Copy, scale=wsx.t[:, h:h + 1]), reads=[PT, wsx], writes=[ke])
                    PN, PD = P[3], P[4]
                    fns = []
                    for vc in range(5):
                        if vc < 4:
                            dst = PN.t[:, vc * 128:(vc + 1) * 128]
                            l0 = Vt.t[:, h * 512 + vc * 128:h * 512 + (vc + 1) * 128]
                        else:
                            dst = PD.t[:, 0:128]
                            l0 = c["ones_b"].t[:]
                        fns.append(lambda e, dst=dst, l0=l0: e.matmul(dst, l0, WT.t[:], start=True, stop=False))
                        for j in range(2):
                            fns.append(lambda e, dst=dst, j=j, vc=vc: e.matmul(dst, cb.t[:, h, j, vc * 128:(vc + 1) * 128], qs.t[:, j, :], start=False, stop=(j == 1)))
                    T.group("pe", fns, reads=[Vt, WT, cb, qs, c["ones_b"]], writes=[PN, PD])
                    T.op("act", lambda e: e.activation(dd.t[:], PD.t[:, 0:128], AF.Abs), reads=[PD], writes=[dd])
                    T.op("dve", lambda e: e.tensor_scalar_max(dd.t[:], dd.t[:], 1.0), reads=[dd], writes=[dd])
                    T.op("dve", lambda e: e.reciprocal(dd.t[:], dd.t[:]), reads=[dd], writes=[dd])
                    T.op("dve", lambda e: e.tensor_tensor(ho.t[:, h * 4:(h + 1) * 4, :], PN.t[:].rearrange("p (v t) -> p v t", t=128),
                                                          dd.t[:].unsqueeze(1).to_broadcast([128, 4, 128]), ALU.mult), reads=[PN, dd], writes=[ho])
                    for j in range(2):
                        Pc = P[5 + j]
                        T.op("pe", lambda e: e.matmul(Pc.t[:], ke.t[:, j * 128:(j + 1) * 128], Vt.t[:, h * 512:(h + 1) * 512], start=True, stop=True),
                             reads=[ke, Vt], writes=[Pc])
                        T.op("dve", lambda e: e.scalar_tensor_tensor(cf.t[:, h, j, 0:512], cf.t[:, h, j, 0:512], cdec.t[:, h:h + 1], Pc.t[:], ALU.mult, ALU.add),
                             reads=[cf, cdec, Pc], writes=[cf])
                    Pn = P[7]
                    T.group("pe", [(lambda e, j=j: e.matmul(Pn.t[:, j * 128:(j + 1) * 128], ke.t[:, j * 128:(j + 1) * 128], c["ones_b"].t[:], start=True, stop=True))
                                   for j in range(2)], reads=[ke, c["ones_b"]], writes=[Pn])
                    T.op("dve", lambda e: e.scalar_tensor_tensor(cf.t[:, h, :, 512:640], cf.t[:, h, :, 512:640], cdec.t[:, h:h + 1],
                                                                 Pn.t[:, 0:256].rearrange("p (j n) -> p j n", n=128), ALU.mult, ALU.add),
                         reads=[cf, cdec, Pn], writes=[cf])
                    T.op("act", lambda e: e.activation(cb.t[:, h], cf.t[:, h], AF.Copy), reads=[cf], writes=[cb])
                T.dma("act", self.HF[d].t[:, c0:c0 + 128].rearrange("(k p) n -> p k n", p=128), ho.t[:], reads=[ho], writes=[self.HF[d].k(c0 // 128)])

    def even_phase(self, es, e_):
        T, c, P = self.T, self.c, self.P
        chains = [(b, d) for b in range(2) for d in range(2)]
        rst = self.sb(es, "rst", [128, 1024], F32)
        T.op("pool", lambda e: e.memset(rst.t[:], 1.0), writes=[rst])
        T.op("pool", lambda e: e.memset(rst.t[:].rearrange("p (c t) -> p c t", t=128)[:, :, 0:1], 0.0), reads=[rst], writes=[rst])
        Hf = {ch: self.sb(es, "Hf", [128, 1024], F32) for ch in chains}
        Hb = {ch: self.sb(es, "Hb", [128, 1024], BF16) for ch in chains}
        Sf = {ch: self.sb(es, "Sf", [128, 8, 128], F32) for ch in chains}
        Sb = {ch: self.sb(es, "Sb", [128, 8, 128], BF16) for ch in chains}
        for ch in chains:
            for t in (Hf[ch], Hb[ch], Sf[ch], Sb[ch]):
                T.op("pool", lambda e: e.memset(t.t[:], 0.0), writes=[t])
        KI = {}
        for d in range(2):
            KI[d] = [self.sb(es, f"KI{d}_{i}", [128, 8, 128], BF16) for i in range(8)]
            for t in KI[d]:
                T.op("pool", lambda e: e.memset(t.t[:], 0.0), writes=[t])
        sets = []
        for i in range(2):
            w = {}
            def mk(name, shape, dt):
                w[name] = self.sb(es, name, shape, dt)
            mk("BCT", [128, 4, 128], BF16); mk("xsT", [128, 8, 128], BF16); mk("dtr", [128, 32], F32)
            mk("dt", [128, 16], F32); mk("na", [128, 16], F32); mk("cs", [128, 32], F32); mk("wsx", [128, 16], F32)
            mk("cdec", [128, 16], F32); mk("xdt", [128, 1024], BF16); mk("xdtw", [128, 1024], BF16); mk("Btm", [128, 256], BF16)
            mk("G", [128, 256], F32); mk("Dm", [128, 128], F32); mk("E", [128, 128], F32); mk("MT", [128, 128], BF16)
            mk("CsT", [128, 128], BF16); mk("ysb", [128, 1024], BF16); mk("hos", [128, 8, 128], BF16)
            w["bias"] = w["cs"]
            mk("qT", [128, 8, 128], BF16); mk("fT", [128, 8, 128], BF16); mk("Vt", [128, 1024], BF16)
            mk("A", [128, 8, 128], F32); mk("KIN", [128, 8, 128], F32); mk("LG", [128, 8, 128], F32); mk("BC", [128, 8, 128], F32)
            mk("TMP", [128, 8, 128], F32); mk("tot", [128, 8], F32); mk("hdec", [128, 8], F32); mk("CST", [128, 8, 8], F32)
            mk("qs", [128, 8, 128], BF16); mk("keT", [128, 8, 128], BF16); mk("ketm", [128, 1024], BF16); mk("qloc", [128, 8, 128], BF16)
            mk("attm", [128, 128], BF16); mk("hoh", [128, 8, 128], BF16)
            sets.append(w)
        n = 0
        for step in range(18):
            for ch in chains:
                b, d = ch
                c0 = chain_chunks(b, d)[step]
                w = sets[n % 2]; n += 1
                self.ssd_step(e_, d, c0, w, Hf[ch], Hb[ch])
                self.hgrn_step(e_, d, c0, w, Sf[ch], Sb[ch], KI[d], rst)

    def ssd_step(self, e_, d, c0, w, hf, hb):
        T, c, P = self.T, self.c, self.P
        BCT, xsT, dtr, dt, na, wsx, cdec = w["BCT"], w["xsT"], w["dtr"], w["dt"], w["na"], w["wsx"], w["cdec"]
        xdt, xdtw, Btm, G, MT, CsT, ysb, hos = w["xdt"], w["xdtw"], w["Btm"], w["G"], w["MT"], w["CsT"], w["ysb"], w["hos"]
        T.dma("sp", BCT.t[:], self.U_FM.t[2048:2560, c0:c0 + 128].rearrange("(k p) n -> p k n", p=128), writes=[BCT])
        T.dma("sp", xsT.t[:], self.U_FM.t[1024:2048, c0:c0 + 128].rearrange("(k p) n -> p k n", p=128), writes=[xsT])
        T.dma("sp", dtr.t[:], self.U_G.t[c0:c0 + 128, 0:32], writes=[dtr])
        o = e_ * 32 + d * 16
        T.op("dve", lambda e: e.tensor_tensor(dt.t[:], dtr.t[:, d * 16:(d + 1) * 16], c["DTB"].t[:, o:o + 16], ALU.add), reads=[dtr, c["DTB"]], writes=[dt])
        T.op("act", lambda e: e.activation(dt.t[:], dt.t[:], AF.Exp), reads=[dt], writes=[dt])
        T.op("act", lambda e: e.activation(dt.t[:], dt.t[:], AF.Ln, bias=1.0), reads=[dt], writes=[dt])
        T.op("dve", lambda e: e.tensor_tensor(na.t[:], dt.t[:], c["EA"].t[:, o:o + 16], ALU.mult), reads=[dt, c["EA"]], writes=[na])
        cs = self.gate_cums(d, na, 16, w)
        T.op("dve", lambda e: e.tensor_tensor(wsx.t[:], cs.t[:, 16:32], cs.t[:, 0:16], ALU.subtract), reads=[cs], writes=[wsx])
        T.op("act", lambda e: e.activation(wsx.t[:], wsx.t[:], AF.Exp, scale=-1.0), reads=[wsx], writes=[wsx])
        T.op("act", lambda e: e.activation(cdec.t[:], cs.t[:, 16:32], AF.Exp, scale=-1.0), reads=[cs], writes=[cdec])
        tr2 = self.trps(2)
        T.group("pe", [(lambda e, j=j: e.transpose(tr2[:, j * 128:(j + 1) * 128], xsT.t[:, j, :], c["ident_b"].t[:])) for j in range(8)],
                reads=[xsT, c["ident_b"]], writes=[P[2]])
        T.op("dve", lambda e: e.tensor_tensor(xdt.t[:].rearrange("p (h q) -> p h q", q=64), tr2[:, 0:1024].rearrange("p (h q) -> p h q", q=64),
                                              dt.t[:].unsqueeze(2).to_broadcast([128, 16, 64]), ALU.mult), reads=[P[2], dt], writes=[xdt])
        T.op("dve", lambda e: e.tensor_tensor(xdtw.t[:].rearrange("p (h q) -> p h q", q=64), xdt.t[:].rearrange("p (h q) -> p h q", q=64),
                                              wsx.t[:].unsqueeze(2).to_broadcast([128, 16, 64]), ALU.mult), reads=[xdt, wsx], writes=[xdtw])
        tr7 = self.trps(7)
        T.group("pe", [(lambda e, g=g: e.transpose(tr7[:, g * 128:(g + 1) * 128], BCT.t[:, g, :], c["ident_b"].t[:])) for g in range(2)],
                reads=[BCT, c["ident_b"]], writes=[P[7]])
        T.op("act", lambda e: e.activation(Btm.t[:], tr7[:, 0:256], AF.Copy), reads=[P[7]], writes=[Btm])
        T.group("pe", [(lambda e, g=g: e.matmul(P[0].t[:, g * 128:(g + 1) * 128], BCT.t[:, g, :], BCT.t[:, 2 + g, :], start=True, stop=True)) for g in range(2)],
                reads=[BCT], writes=[P[0]])
        T.op("act", lambda e: e.activation(G.t[:], P[0].t[:, 0:256], AF.Copy), reads=[P[0]], writes=[G])
        for h in range(16):
            g = h // 8
            self.decay_mats(d, na, h, cs.t[:, h:h + 1], w)
            T.op("dve", lambda e: e.tensor_tensor(MT.t[:], G.t[:, g * 128:(g + 1) * 128], w["Dm"].t[:], ALU.mult), reads=[G, w["Dm"]], writes=[MT])
            T.op("dve", lambda e: e.tensor_tensor(CsT.t[:], BCT.t[:, 2 + g, :], w["E"].t[:], ALU.mult), reads=[BCT, w["E"]], writes=[CsT])
            PY = P[3 + h // 8]
            dst = PY.t[:, (h % 8) * 64:(h % 8 + 1) * 64]
            T.group("pe", [lambda e: e.matmul(dst, MT.t[:], xdt.t[:, h * 64:(h + 1) * 64], start=True, stop=False),
                           lambda e: e.matmul(dst, CsT.t[:], hb.t[:, h * 64:(h + 1) * 64], start=False, stop=True)],
                    reads=[MT, xdt, CsT, hb], writes=[PY])
        T.op("act", lambda e: e.activation(ysb.t[:, 0:512], P[3].t[:], AF.Copy), reads=[P[3]], writes=[ysb])
        T.op("dve", lambda e: e.tensor_copy(ysb.t[:, 512:1024], P[4].t[:]), reads=[P[4]], writes=[ysb])
        for g in range(2):
            Pc = P[5 + g]
            T.op("pe", lambda e: e.matmul(Pc.t[:], Btm.t[:, g * 128:(g + 1) * 128], xdtw.t[:, g * 512:(g + 1) * 512], start=True, stop=True),
                 reads=[Btm, xdtw], writes=[Pc])
            hv = hf.t[:, g * 512:(g + 1) * 512].rearrange("p (h q) -> p h q", q=64)
            T.op("dve", lambda e: e.tensor_tensor(hv, hv, cdec.t[:, g * 8:(g + 1) * 8].unsqueeze(2).to_broadcast([128, 8, 64]), ALU.mult),
                 reads=[hf, cdec], writes=[hf])
            T.op("dve", lambda e: e.tensor_tensor(hf.t[:, g * 512:(g + 1) * 512], hf.t[:, g * 512:(g + 1) * 512], Pc.t[:], ALU.add), reads=[hf, Pc], writes=[hf])
        T.op("act", lambda e: e.activation(hb.t[:], hf.t[:], AF.Copy), reads=[hf], writes=[hb])
        T.group("pe", [(lambda e, j=j: e.transpose(tr2[:, j * 128:(j + 1) * 128], ysb.t[:, j * 128:(j + 1) * 128], c["ident_b"].t[:])) for j in range(8)],
                reads=[ysb, c["ident_b"]], writes=[P[2]])
        T.op("act", lambda e: e.activation(hos.t[:], tr2[:, 0:1024].rearrange("p (k t) -> p k t", t=128), AF.Copy), reads=[P[2]], writes=[hos])
        T.dma("act", self.HF[d].t[0:1024, c0:c0 + 128].rearrange("(k p) n -> p k n", p=128), hos.t[:], reads=[hos], writes=[self.HF[d].k((0, c0 // 128))])

    def hgrn_step(self, e_, d, c0, w, sf, sb, KI, rst):
        T, c, P = self.T, self.c, self.P
        qT, fT, Vt, A, KIN, LG, BC, TMP = w["qT"], w["fT"], w["Vt"], w["A"], w["KIN"], w["LG"], w["BC"], w["TMP"]
        tot, hdec, CST, qs, keT, ketm, qloc, attm, hoh = w["tot"], w["hdec"], w["CST"], w["qs"], w["keT"], w["ketm"], w["qloc"], w["attm"], w["hoh"]
        T.dma("sp", qT.t[:], self.U_FM.t[2560:3584, c0:c0 + 128].rearrange("(k p) n -> p k n", p=128), writes=[qT])
        T.dma("sp", fT.t[:], self.U_FM.t[3584 + d * 1024:3584 + (d + 1) * 1024, c0:c0 + 128].rearrange("(k p) n -> p k n", p=128), writes=[fT])
        T.dma("sp", Vt.t[:], self.U_TM.t[c0:c0 + 128, 0:1024], writes=[Vt])
        lb = c["LB"].t[:, e_ * 8:(e_ + 1) * 8].unsqueeze(2).to_broadcast([128, 8, 128])
        lb1 = c["LB1"].t[:, e_ * 8:(e_ + 1) * 8].unsqueeze(2).to_broadcast([128, 8, 128])
        T.op("act", lambda e: e.activation(A.t[:], fT.t[:], AF.Sigmoid), reads=[fT], writes=[A])
        T.op("dve", lambda e: e.tensor_tensor(A.t[:], A.t[:], lb1, ALU.mult), reads=[A, c["LB1"]], writes=[A])
        T.op("dve", lambda e: e.tensor_tensor(A.t[:], A.t[:], lb, ALU.add), reads=[A, c["LB"]], writes=[A])
        T.op("dve", lambda e: e.tensor_scalar(KIN.t[:], A.t[:], -1.0, 1.0, ALU.mult, ALU.add), reads=[A], writes=[KIN])
        T.op("act", lambda e: e.activation(LG.t[:], A.t[:], AF.Ln), reads=[A], writes=[LG])
        fl = lambda t: t.t[:].rearrange("p h t -> p (h t)")
        T.op("dve", lambda e: e.tensor_tensor_scan(fl(BC), rst.t[:], fl(LG), 0.0, ALU.mult, ALU.add), reads=[rst, LG], writes=[BC])
        T.op("dve", lambda e: e.tensor_copy(tot.t[:].unsqueeze(2), BC.t[:, :, 127:128]), reads=[BC], writes=[tot])
        totb = tot.t[:].unsqueeze(2).to_broadcast([128, 8, 128])
        if d == 1:
            T.op("dve", lambda e: e.tensor_tensor(BC.t[:], LG.t[:], BC.t[:], ALU.subtract), reads=[LG, BC], writes=[BC])
            T.op("dve", lambda e: e.tensor_tensor(BC.t[:], BC.t[:], totb, ALU.add), reads=[BC, tot], writes=[BC])
        T.op("act", lambda e: e.activation(TMP.t[:], BC.t[:], AF.Exp), reads=[BC], writes=[TMP])
        T.op("dve", lambda e: e.tensor_tensor(qs.t[:], qT.t[:], TMP.t[:], ALU.mult), reads=[qT, TMP], writes=[qs])
        T.op("dve", lambda e: e.tensor_tensor(TMP.t[:], totb, BC.t[:], ALU.subtract), reads=[tot, BC, TMP], writes=[TMP])
        T.op("act", lambda e: e.activation(TMP.t[:], TMP.t[:], AF.Exp), reads=[TMP], writes=[TMP])
        T.op("dve", lambda e: e.tensor_tensor(keT.t[:], TMP.t[:], KIN.t[:], ALU.mult), reads=[TMP, KIN], writes=[keT])
        T.op("act", lambda e: e.activation(hdec.t[:], tot.t[:], AF.Exp), reads=[tot], writes=[hdec])
        tr2 = self.trps(2)
        T.group("pe", [(lambda e, h=h: e.transpose(tr2[:, h * 128:(h + 1) * 128], keT.t[:, h, :], c["ident_b"].t[:])) for h in range(8)],
                reads=[keT, c["ident_b"]], writes=[P[2]])
        T.op("act", lambda e: e.activation(ketm.t[:], tr2[:, 0:1024], AF.Copy), reads=[P[2]], writes=[ketm])
        if d == 0:
            T.op("pool", lambda e: e.memset(CST.t[:, :, 0:1], 0.0), writes=[CST])
            T.op("dve", lambda e: e.tensor_copy(CST.t[:, :, 1:8], BC.t[:, :, 15:127:16]), reads=[BC, CST], writes=[CST])
        else:
            T.op("pool", lambda e: e.memset(CST.t[:, :, 7:8], 0.0), writes=[CST])
            T.op("dve", lambda e: e.tensor_copy(CST.t[:, :, 0:7], BC.t[:, :, 16:128:16]), reads=[BC, CST], writes=[CST])
        v64 = lambda t: t.t[:].rearrange("p h (i u) -> p (h i) u", u=16)
        T.op("dve", lambda e: e.tensor_tensor(v64(TMP), v64(BC), CST.t[:].rearrange("p h i -> p (h i)").unsqueeze(2).to_broadcast([128, 64, 16]), ALU.subtract),
             reads=[BC, CST, TMP], writes=[TMP])
        T.op("act", lambda e: e.activation(TMP.t[:], TMP.t[:], AF.Exp), reads=[TMP], writes=[TMP])
        T.op("dve", lambda e: e.tensor_tensor(qloc.t[:], qT.t[:], TMP.t[:], ALU.mult), reads=[qT, TMP], writes=[qloc])
        for i in range(8):
            lo, hi = (0, 16 * (i + 1)) if d == 0 else (16 * i, 128)
            Ki = KI[i]
            T.op("dve", lambda e: e.tensor_tensor(TMP.t[:, :, lo:hi], CST.t[:, :, i:i + 1].to_broadcast([128, 8, hi - lo]), BC.t[:, :, lo:hi], ALU.subtract),
                 reads=[CST, BC, TMP], writes=[TMP])
            T.op("act", lambda e: e.activation(TMP.t[:, :, lo:hi], TMP.t[:, :, lo:hi], AF.Exp), reads=[TMP], writes=[TMP])
            T.op("dve", lambda e: e.tensor_tensor(Ki.t[:, :, lo:hi], TMP.t[:, :, lo:hi], KIN.t[:, :, lo:hi], ALU.mult), reads=[TMP, KIN], writes=[Ki])
        for h in range(8):
            PA = P[0]
            T.group("pe", [(lambda e, i=i: e.matmul(PA.t[:, 16 * i:16 * i + 16], KI[i].t[:, h, :], qloc.t[:, h, 16 * i:16 * i + 16], start=True, stop=True))
                           for i in range(8)], reads=KI + [qloc], writes=[PA])
            T.op("dve", lambda e: e.tensor_tensor(attm.t[:], PA.t[:, 0:128], c[f"tri{d}"].t[:], ALU.mult), reads=[PA, c[f"tri{d}"]], writes=[attm])
            PO = P[3 + h // 4]
            dst = PO.t[:, (h % 4) * 128:(h % 4 + 1) * 128]
            T.group("pe", [lambda e: e.matmul(dst, Vt.t[:, h * 128:(h + 1) * 128], attm.t[:], start=True, stop=False),
                           lambda e: e.matmul(dst, sb.t[:, h, :], qs.t[:, h, :], start=False, stop=True)],
                    reads=[Vt, attm, sb, qs], writes=[PO])
            Pc = P[5 + h % 2]
            T.op("pe", lambda e: e.matmul(Pc.t[:, 0:128], ketm.t[:, h * 128:(h + 1) * 128], Vt.t[:, h * 128:(h + 1) * 128], start=True, stop=True),
                 reads=[ketm, Vt], writes=[Pc])
            T.op("dve", lambda e: e.scalar_tensor_tensor(sf.t[:, h, :], sf.t[:, h, :], hdec.t[:, h:h + 1], Pc.t[:, 0:128], ALU.mult, ALU.add),
                 reads=[sf, hdec, Pc], writes=[sf])
        T.op("act", lambda e: e.activation(sb.t[:], sf.t[:], AF.Copy), reads=[sf], writes=[sb])
        T.op("act", lambda e: e.activation(hoh.t[:, 0:4, :], P[3].t[:].rearrange("p (h t) -> p h t", t=128), AF.Copy), reads=[P[3]], writes=[hoh])
        T.op("dve", lambda e: e.tensor_copy(hoh.t[:, 4:8, :], P[4].t[:].rearrange("p (h t) -> p h t", t=128)), reads=[P[4]], writes=[hoh])
        T.dma("act", self.HF[d].t[1024:2048, c0:c0 + 128].rearrange("(k p) n -> p k n", p=128), hoh.t[:], reads=[hoh], writes=[self.HF[d].k((1, c0 // 128))])


_W_NAMES = ["mod_w", "mod_b", "norm_w", "final_norm_w", "mlp_w1", "mlp_w2", "even_w_in", "even_w_out",
            "ssd_conv_w", "ssd_conv_b", "ssd_a_log", "ssd_dt_bias", "ssd_d", "ssd_norm_w", "hgrn_lb", "hgrn_norm_w",
            "odd_w_in", "odd_w_out", "mlstm_conv_w", "mlstm_conv_b", "mlstm_gate_b", "mlstm_norm_w"]


def make_in_maps(inputs, cores):
    f = lambda a: np.ascontiguousarray(np.asarray(a, dtype=np.float32))
    shared = {k: f(inputs[k]) for k in _W_NAMES}
    maps = []
    for ci in cores:
        m = dict(shared)
        m["x"] = f(inputs["x"][2 * ci:2 * ci + 2])
        m["ctx"] = f(inputs["ctx"][2 * ci:2 * ci + 2])
        m["cvec"] = f(np.stack([inputs["c"][2 * ci], inputs["c"][2 * ci + 1], inputs["c_ctx"]], axis=0))
        maps.append(m)
    return maps


def kernel(**inputs):
    prog = Prog()
    nc = prog.build()
    maps = make_in_maps(inputs, list(range(8)))
    res = run_bass_kernel_spmd(nc, maps, core_ids=list(range(8)))
    return np.concatenate([np.asarray(r["out"], dtype=np.float32) for r in res.results], axis=0)
```

```python
import contextlib
import math
import numpy as np
import concourse.bass as bass
import concourse.mybir as mybir
from concourse.bass_utils import run_bass_kernel_spmd

F32 = mybir.dt.float32
BF16 = mybir.dt.bfloat16
ALU = mybir.AluOpType
AF = mybir.ActivationFunctionType

D = 2048
NTOK = 4608
NBLK = 9
NCH = 36
DEPTH = 4
EVEN_IN = 7712
ODD_IN = 6160
EPS = 1e-6
BIG = 30000.0


class Res:
    __slots__ = ("name", "w", "r")

    def __init__(self, name):
        self.name = name
        self.w = None
        self.r = []


class Buf:
    def __init__(self, t, name):
        self.t = t
        self.name = name
        self.r = Res(name)
        self.subs = {}

    def k(self, key):
        s = self.subs.get(key)
        if s is None:
            s = Res(f"{self.name}.{key}")
            self.subs[key] = s
        return s


def _res(x):
    return x.r if isinstance(x, Buf) else x


class Trk:
    EPOCH = 60000

    def __init__(self, nc, es, n_dma_sems=(10, 6, 10)):
        self.nc = nc
        self.es = es
        self.engs = {"pe": nc.tensor, "act": nc.scalar, "dve": nc.vector, "pool": nc.gpsimd, "sp": nc.sync}
        self.sems = {}
        self.cnt = {}
        self.waited = {k: {} for k in self.engs}
        self.epoch = {}
        self.last = {}
        for k in ("pe", "act", "dve", "pool"):
            self.epoch[k] = 0
            self._new_sem(f"e_{k}0")
        self.dq = {}
        self.gen = 0
        for q, n in zip(("sp", "act", "pool"), n_dma_sems):
            keys = []
            for i in range(n):
                key = f"d_{q}{i}"
                self._new_sem(key)
                keys.append(key)
            self.dq[q] = {"keys": keys, "i": 0}
        self.n_instr = 0
        self.n_wait = 0

    def _new_sem(self, key):
        self.sems[key] = self.es.enter_context(self.nc.semaphore(key))
        self.cnt[key] = 0

    def _eng_key(self, k):
        key = f"e_{k}{self.epoch[k]}"
        if self.cnt[key] >= self.EPOCH:
            self.epoch[k] += 1
            key = f"e_{k}{self.epoch[k]}"
            self._new_sem(key)
        return key

    def _wait(self, ek, tok):
        if tok is None:
            return
        key, val = tok
        if self.waited[ek].get(key, 0) >= val:
            return
        self.engs[ek].wait_ge(self.sems[key], val)
        self.waited[ek][key] = val
        self.n_wait += 1

    def _deps(self, ek, reads, writes):
        toks = []
        for r in reads:
            r = _res(r)
            if r.w is not None:
                toks.append((r.w, True))
        for w in writes:
            w = _res(w)
            if w.w is not None:
                toks.append((w.w, True))
            for t in w.r:
                toks.append((t, False))
        for tok, strong in toks:
            if tok[0].startswith("e_" + ek):
                if ek == "pe" or not strong:
                    continue
            self._wait(ek, tok)

    def _commit(self, tok, reads, writes):
        for r in reads:
            r = _res(r)
            r.r.append(tok)
            if len(r.r) > 48:
                best = {}
                for k, v in r.r:
                    if best.get(k, 0) < v:
                        best[k] = v
                r.r = list(best.items())
        for w in writes:
            w = _res(w)
            w.w = tok
            w.r = []

    def op(self, ek, fn, reads=(), writes=()):
        return self.group(ek, [fn], reads, writes)

    def group(self, ek, fns, reads=(), writes=()):
        key = self._eng_key(ek)
        self._deps(ek, reads, writes)
        ins = None
        for fn in fns:
            ins = fn(self.engs[ek])
            self.n_instr += 1
        ins.then_inc(self.sems[key], 1)
        self.cnt[key] += 1
        tok = (key, self.cnt[key])
        self.last[ek] = tok
        self._commit(tok, reads, writes)
        return tok

    def dma(self, q, out, in_, reads=(), writes=(), **kw):
        d = self.dq[q]
        slot = d["i"] % len(d["keys"])
        d["i"] += 1
        key = d["keys"][slot]
        if self.cnt[key] + 16 > self.EPOCH:
            self.gen += 1
            key = f"d_{q}{slot}_{self.gen}"
            self._new_sem(key)
            d["keys"][slot] = key
        if self.cnt[key] > 0:
            self._wait(q, (key, self.cnt[key]))
        self._deps(q, reads, writes)
        ins = self.engs[q].dma_start(out=out, in_=in_, **kw)
        self.n_instr += 1
        ins.then_inc(self.sems[key], 16)
        self.cnt[key] += 16
        tok = (key, self.cnt[key])
        self._commit(tok, reads, writes)
        return tok

    def all_tokens(self):
        toks = [t for t in self.last.values()]
        for q in self.dq.values():
            for key in q["keys"]:
                if self.cnt[key]:
                    toks.append((key, self.cnt[key]))
        for key, c in self.cnt.items():
            if key.startswith("d_") and c and (key, c) not in toks:
                toks.append((key, c))
        return toks

    def barrier(self, engines=("pe", "act", "dve", "pool", "sp")):
        toks = self.all_tokens()
        for ek in engines:
            for t in toks:
                self._wait(ek, t)


def blk_cols(tb):
    return tb * 512


def chunk_col(b, kind, j):
    if kind == "lat":
        return b * 2048 + j * 128
    return 4096 + b * 256 + j * 128


def chain_chunks(b, d):
    ctx = [("ctx", 0), ("ctx", 1)]
    lat = [("lat", j) for j in range(16)]
    if d == 0:
        seq = ctx + lat
    else:
        seq = ctx[::-1] + lat[::-1]
    return [chunk_col(b, k, j) for k, j in seq]


class Prog:
    def __init__(self, n_layers=DEPTH, debug=False):
        self.n_layers = n_layers
        self.debug = debug
        self.nc = bass.Bass("TRN2", target_bir_lowering=False)
        self.uid = 0

    def kind(self, layer):
        f = getattr(self, "force_kind", None)
        if f is not None:
            return f
        return (layer % 2 == 1, layer // 2)

    def sb(self, es, name, shape, dt):
        self.uid += 1
        nm = f"{name}_{self.uid}"
        return Buf(es.enter_context(self.nc.sbuf_tensor(nm, list(shape), dt)), nm)

    def ps(self, es, name, shape, dt=F32):
        self.uid += 1
        nm = f"{name}_{self.uid}"
        return Buf(es.enter_context(self.nc.psum_tensor(nm, list(shape), dt)), nm)

    def dram(self, name, shape, dt, kind="Internal"):
        if self.debug and kind == "Internal" and name in self.debug_outs:
            kind = "ExternalOutput"
        t = self.nc.dram_tensor(name, list(shape), dt, kind=kind)
        return Buf(t.ap(), name)

    def build(self, debug_outs=()):
        self.debug_outs = set(debug_outs)
        nc = self.nc
        I = {}
        def inp(name, shape):
            I[name] = nc.dram_tensor(name, list(shape), F32, kind="ExternalInput").ap()
        inp("x", [2, 2048, D]); inp("ctx", [2, 256, D]); inp("cvec", [3, D])
        inp("mod_w", [DEPTH, D, 6 * D]); inp("mod_b", [DEPTH, 6 * D]); inp("norm_w", [DEPTH, 2, D])
        inp("final_norm_w", [D]); inp("mlp_w1", [DEPTH, D, 4 * D]); inp("mlp_w2", [DEPTH, 4 * D, D])
        inp("even_w_in", [2, D, EVEN_IN]); inp("even_w_out", [2, D, D])
        inp("ssd_conv_w", [2, 3, 1536]); inp("ssd_conv_b", [2, 1536]); inp("ssd_a_log", [2, 2, 16])
        inp("ssd_dt_bias", [2, 2, 16]); inp("ssd_d", [2, 16]); inp("ssd_norm_w", [2, 1024])
        inp("hgrn_lb", [2, 1024]); inp("hgrn_norm_w", [2, 1024])
        inp("odd_w_in", [2, D, ODD_IN]); inp("odd_w_out", [2, D, D])
        inp("mlstm_conv_w", [2, 3, 2048]); inp("mlstm_conv_b", [2, 2048]); inp("mlstm_gate_b", [2, 4, 4])
        inp("mlstm_norm_w", [2, 2048])
        self.I = I
        self.out = nc.dram_tensor("out", [2, 2048, D], F32, kind="ExternalOutput").ap()
        self.r_out = Res("out")

        self.XT = self.dram("XT", [D, NTOK], F32)
        self.U_FM = self.dram("U_FM", [6656, NTOK], BF16)
        self.U_TM = self.dram("U_TM", [NTOK, 2048], BF16)
        self.U_G = self.dram("U_G", [NTOK, 32], F32)
        self.HF = [self.dram("HF0", [D, NTOK], BF16), self.dram("HF1", [D, NTOK], BF16)]

        with contextlib.ExitStack() as es:
            self.T = Trk(nc, es)
            with contextlib.ExitStack() as ces:
                self.consts(ces)
                with contextlib.ExitStack() as pes:
                    self.prologue(pes)
                self.T.barrier()
                stop = getattr(self, "stop_after", None)
                if self.debug:
                    dbg = self.nc.dram_tensor("DBG_MOD", [128, DEPTH * 288], F32, kind="ExternalOutput").ap()
                    self.T.dma("sp", dbg, self.c["MOD"].t[:], reads=[self.c["MOD"]])
                    dbg2 = self.nc.dram_tensor("DBG_S", [128, DEPTH * 96], F32, kind="ExternalOutput").ap()
                    self.T.dma("sp", dbg2, self.c["S"].t[:], reads=[self.c["S"]])
                stopped = stop == ("prologue",)
                for layer in range(self.n_layers):
                    if stopped:
                        break
                    with contextlib.ExitStack() as des:
                        self.dense_phase(des, layer)
                    self.T.barrier()
                    if stop == ("dense", layer):
                        stopped = True
                        break
                    with contextlib.ExitStack() as mes:
                        self.mixer_phase(mes, layer)
                    self.T.barrier()
                    if stop == ("mixer", layer):
                        stopped = True
                        break
                if not stopped:
                    with contextlib.ExitStack() as des:
                        self.dense_phase(des, self.n_layers)
                self.T.barrier()
        return nc

    def consts(self, es):
        T = self.T
        c = self.c = {}

        def mk(name, shape, dt):
            c[name] = self.sb(es, name, shape, dt)
            return c[name]
        self.P = [self.ps(es, f"P{i}", [128, 512], F32) for i in range(8)]
        idf = mk("ident_f", [128, 128], F32)
        T.op("pool", lambda e: e.memset(idf.t[:], 1.0), writes=[idf])
        T.op("pool", lambda e: e.affine_select(idf.t[:], idf.t[:], pattern=[[-1, 128]], compare_op=ALU.is_equal,
                                               fill=0.0, base=0, channel_multiplier=1), reads=[idf], writes=[idf])
        idb = mk("ident_b", [128, 128], BF16)
        T.op("dve", lambda e: e.tensor_copy(idb.t[:], idf.t[:]), reads=[idf], writes=[idb])
        of = mk("ones_f", [128, 128], F32)
        T.op("pool", lambda e: e.memset(of.t[:], 1.0), writes=[of])
        ob = mk("ones_b", [128, 128], BF16)
        T.op("pool", lambda e: e.memset(ob.t[:], 1.0), writes=[ob])
        for d in range(2):
            sgn = 1 if d == 0 else -1
            tri = mk(f"tri{d}", [128, 128], F32)
            T.op("pool", lambda e: e.memset(tri.t[:], 1.0), writes=[tri])
            T.op("pool", lambda e: e.affine_select(tri.t[:], tri.t[:], pattern=[[sgn, 128]], compare_op=ALU.is_ge,
                                                   fill=0.0, base=0, channel_multiplier=-sgn), reads=[tri], writes=[tri])
            msk = mk(f"msk{d}", [128, 128], F32)
            T.op("dve", lambda e: e.tensor_scalar(msk.t[:], tri.t[:], -BIG, BIG, ALU.mult, ALU.add), reads=[tri], writes=[msk])
        mk("MOD", [128, DEPTH * 6 * 16 * 3], F32)
        mk("S", [128, DEPTH * 2 * 16 * 3], F32)
        mk("MODB", [128, DEPTH * 96], F32)
        mk("NW", [128, 128], F32)
        mk("FNW", [128, 16], F32)
        mk("SCW", [128, 72], F32); mk("SCB", [128, 24], F32)
        mk("MCW", [128, 96], F32); mk("MCB", [128, 32], F32)
        mk("SNW", [128, 16], F32); mk("LBR", [128, 16], F32); mk("HNW", [128, 16], F32); mk("MNW", [128, 32], F32)
        mk("LB", [128, 16], F32); mk("LB1", [128, 16], F32)
        mk("DSK", [128, 16], F32)
        mk("GB", [128, 32], F32)
        mk("DTB", [128, 64], F32)
        mk("EA", [128, 64], F32)
        mk("scT", [128, 16 * 3], BF16)

    def MODv(self, l, m, dc, v):
        i = ((l * 6 + m) * 16 + dc) * 3 + v
        return self.c["MOD"].t[:, i:i + 1]

    def Sv(self, l, i, dc, v):
        j = ((l * 2 + i) * 16 + dc) * 3 + v
        return self.c["S"].t[:, j:j + 1]

    def load_vecT(self, es, dst, src2d, k):
        T = self.T
        st = self.sb(es, "vst", [k, 128], F32)
        T.dma("sp", st.t[:], src2d, writes=[st])
        P = self.P[7]
        T.op("pe", lambda e: e.transpose(P.t[:, 0:k], st.t[:], self.c["ident_f"].t[0:k, 0:k]),
             reads=[st, self.c["ident_f"]], writes=[P])
        T.op("dve", lambda e: e.tensor_copy(dst.t[:, 0:k], P.t[:, 0:k]), reads=[P], writes=[dst])

    def prologue(self, es):
        T, c, I, nc = self.T, self.c, self.I, self.nc
        v128 = lambda ap, pat, **kw: ap.rearrange(pat, **kw)
        self.load_vecT(es, c["NW"], I["norm_w"].rearrange("l i (c p) -> (l i c) p", p=128), 128)
        self.load_vecT(es, c["FNW"], I["final_norm_w"].rearrange("(c p) -> c p", p=128), 16)
        self.load_vecT(es, c["SCW"], I["ssd_conv_w"].rearrange("e j (c p) -> (e j c) p", p=128), 72)
        self.load_vecT(es, c["SCB"], I["ssd_conv_b"].rearrange("e (c p) -> (e c) p", p=128), 24)
        self.load_vecT(es, c["MCW"], I["mlstm_conv_w"].rearrange("e j (c p) -> (e j c) p", p=128), 96)
        self.load_vecT(es, c["MCB"], I["mlstm_conv_b"].rearrange("e (c p) -> (e c) p", p=128), 32)
        self.load_vecT(es, c["SNW"], I["ssd_norm_w"].rearrange("e (c p) -> (e c) p", p=128), 16)
        self.load_vecT(es, c["LBR"], I["hgrn_lb"].rearrange("e (c p) -> (e c) p", p=128), 16)
        self.load_vecT(es, c["HNW"], I["hgrn_norm_w"].rearrange("e (c p) -> (e c) p", p=128), 16)
        self.load_vecT(es, c["MNW"], I["mlstm_norm_w"].rearrange("e (c p) -> (e c) p", p=128), 32)
        for l in range(DEPTH):
            tmp = self.sb(es, "mbt", [128, 96], F32)
            self.load_vecT(es, tmp, I["mod_b"][l].rearrange("(c p) -> c p", p=128), 96)
            T.op("dve", lambda e: e.tensor_copy(c["MODB"].t[:, l * 96:(l + 1) * 96], tmp.t[:]), reads=[tmp], writes=[c["MODB"]])
        LB, LB1, LBR = c["LB"], c["LB1"], c["LBR"]
        T.op("pool", lambda e: e.memset(LB.t[:, 0:8], 0.0), writes=[LB])
        T.op("dve", lambda e: e.tensor_tensor(LB.t[:, 8:16], LBR.t[:, 8:16], LBR.t[:, 0:8], ALU.subtract), reads=[LBR, LB], writes=[LB])
        T.op("act", lambda e: e.activation(LB.t[:, 8:16], LB.t[:, 8:16], AF.Sigmoid), reads=[LB], writes=[LB])
        T.op("dve", lambda e: e.tensor_scalar(LB1.t[:], LB.t[:], -1.0, 1.0, ALU.mult, ALU.add), reads=[LB], writes=[LB1])
        T.dma("sp", c["GB"].t[:], I["mlstm_gate_b"].rearrange("o a b -> (o a b)").partition_broadcast(128), writes=[c["GB"]])
        T.dma("sp", c["DTB"].t[:], I["ssd_dt_bias"].rearrange("e d h -> (e d h)").partition_broadcast(128), writes=[c["DTB"]])
        T.dma("sp", c["EA"].t[:], I["ssd_a_log"].rearrange("e d h -> (e d h)").partition_broadcast(128), writes=[c["EA"]])
        T.op("act", lambda e: e.activation(c["EA"].t[:], c["EA"].t[:], AF.Exp), reads=[c["EA"]], writes=[c["EA"]])
        for o in range(2):
            T.op("dve", lambda e: e.tensor_scalar_add(c["GB"].t[:, o * 16:o * 16 + 8], c["GB"].t[:, o * 16:o * 16 + 8], -math.log(16.0)),
                 reads=[c["GB"]], writes=[c["GB"]])
        for e_ in range(2):
            for h in range(16):
                T.dma("sp", c["DSK"].t[(h % 2) * 64:(h % 2) * 64 + 64, e_ * 8 + h // 2:e_ * 8 + h // 2 + 1],
                      I["ssd_d"][e_, h:h + 1].partition_broadcast(64), writes=[c["DSK"]])
        cT = self.sb(es, "cT", [128, 48], F32)
        self.load_vecT(es, cT, I["cvec"].rearrange("v (c p) -> (v c) p", p=128), 48)
        scv = c["scT"].t[:].rearrange("p (k v) -> p k v", v=3)
        for v in range(3):
            T.op("act", lambda e: e.activation(scv[:, :, v], cT.t[:, v * 16:(v + 1) * 16], AF.Silu), reads=[cT], writes=[c["scT"]])
        wb = [self.sb(es, f"mw{i}", [128, 16, 512], BF16) for i in range(3)]
        P = self.P
        n = 0
        for l in range(DEPTH):
            Pm = P[l % 2]
            for jb in range(24):
                w = wb[n % 3]; n += 1
                T.dma("pool", w.t[:], I["mod_w"][l][:, jb * 512:(jb + 1) * 512].rearrange("(k p) n -> p k n", p=128), writes=[w])
                for jt in range(4):
                    j = jb * 4 + jt
                    T.group("pe", [
                        (lambda e, kc=kc: e.matmul(Pm.t[:, j * 3:j * 3 + 3], w.t[:, kc, jt * 128:(jt + 1) * 128], scv[:, kc, :],
                                                   start=(kc == 0), stop=(kc == 15))) for kc in range(16)],
                        reads=[w, c["scT"]], writes=[Pm])
            MODl = c["MOD"].t[:, l * 288:(l + 1) * 288].rearrange("p (j v) -> p j v", v=3)
            T.op("dve", lambda e: e.tensor_tensor(MODl, Pm.t[:, 0:288].rearrange("p (j v) -> p j v", v=3),
                                                  c["MODB"].t[:, l * 96:(l + 1) * 96].unsqueeze(2).to_broadcast([128, 96, 3]), ALU.add),
                 reads=[Pm, c["MODB"]], writes=[c["MOD"]])
            for i in range(2):
                Sl = c["S"].t[:, (l * 2 + i) * 48:(l * 2 + i + 1) * 48].rearrange("p (k v) -> p k v", v=3)
                sc = c["MOD"].t[:, ((l * 6 + 1 + 3 * i) * 16) * 3:((l * 6 + 2 + 3 * i) * 16) * 3].rearrange("p (k v) -> p k v", v=3)
                nw = c["NW"].t[:, (l * 2 + i) * 16:(l * 2 + i + 1) * 16].unsqueeze(2).to_broadcast([128, 16, 3])
                T.op("dve", lambda e: e.tensor_scalar_add(Sl, sc, 1.0), reads=[c["MOD"]], writes=[c["S"]])
                T.op("dve", lambda e: e.tensor_tensor(Sl, Sl, nw, ALU.mult), reads=[c["S"], c["NW"]], writes=[c["S"]])

    def dense_phase(self, es, layer):
        T, c, I, P = self.T, self.c, self.I, self.P
        L = self.n_layers
        self.xT = self.sb(es, "xT", [128, 16, 512], F32)
        self.actT = self.sb(es, "actT", [128, 16, 512], BF16)
        self.wb = [self.sb(es, f"wb{i}", [128, 16 * 512], BF16) for i in range(3)]
        self.wn = 0
        self.f32t = [self.sb(es, f"f32t{i}", [128, 512], F32) for i in range(4)]
        self.fn = 0
        self.rstd = self.sb(es, "rstd", [128, 512], F32)
        self.so = [self.sb(es, f"so{i}", [128, 4, 512], BF16) for i in range(2)]
        self.son = 0
        self.pn = 0
        if layer > 0:
            self.aT = self.sb(es, "aT", [128, 64, 512], BF16)
            self.hA = self.sb(es, "hA", [128, 4, 512], BF16)
            self.hB = self.sb(es, "hB", [128, 4, 512], BF16)
            self.hC = self.so[0]
            self.hD = self.so[1]
            base = self.sb(es, "hs", [128, 2048], F32)
            self.hs = Buf(base.t[:].rearrange("p (a b) -> p a b", b=512), base.name)
            self.hs.r = base.r
            self.xin = base
        if layer == 0:
            self.xin = self.sb(es, "xin", [128, 2048], F32)
        for tb in range(NBLK):
            if layer == L and tb == 8:
                continue
            v = 2 if tb == 8 else tb // 4
            c0 = tb * 512
            xT = self.xT
            if layer == 0:
                self.load_x_input(tb)
            else:
                T.dma("sp", xT.t[:], self.XT.t[:, c0:c0 + 512].rearrange("(k p) n -> p k n", p=128),
                      reads=[self.XT.k(tb)], writes=[xT])
                self.stage_c(layer - 1, tb, v)
            if layer < L:
                self.norm_mod(layer, 0, v)
                self.stage_a(layer, tb)
                T.dma("sp", self.XT.t[:, c0:c0 + 512].rearrange("(k p) n -> p k n", p=128), xT.t[:],
                      reads=[xT], writes=[self.XT.k(tb)])
            else:
                self.final_out(tb)

    def nextP(self):
        p = self.P[self.pn % 4]
        self.pn += 1
        return p

    def nextF(self):
        f = self.f32t[self.fn % 4]
        self.fn += 1
        return f

    def load_x_input(self, tb):
        T, I, P = self.T, self.I, self.P
        idf = self.c["ident_f"]
        for tt in range(4):
            if tb < 8:
                src = I["x"][tb // 4, (tb % 4) * 512 + tt * 128:(tb % 4) * 512 + (tt + 1) * 128, :]
            else:
                src = I["ctx"][tt // 2, (tt % 2) * 128:(tt % 2 + 1) * 128, :]
            T.dma("sp", self.xin.t[:], src, writes=[self.xin])
            for g in range(4):
                Pg = self.nextP()
                T.group("pe", [(lambda e, j=j: e.transpose(Pg.t[:, j * 128:(j + 1) * 128],
                                                           self.xin.t[:, (g * 4 + j) * 128:(g * 4 + j + 1) * 128], idf.t[:])) for j in range(4)],
                        reads=[self.xin, idf], writes=[Pg])
                T.op("act" if g % 2 else "dve",
                     (lambda e: e.activation(self.xT.t[:, g * 4:(g + 1) * 4, tt * 128:(tt + 1) * 128],
                                             Pg.t[:].rearrange("p (j t) -> p j t", t=128), AF.Copy)) if g % 2 else
                     (lambda e: e.tensor_copy(self.xT.t[:, g * 4:(g + 1) * 4, tt * 128:(tt + 1) * 128],
                                              Pg.t[:].rearrange("p (j t) -> p j t", t=128))),
                     reads=[Pg], writes=[self.xT])

    def rms_rstd(self, src_fn, nchunks, nfeat, reads):
        T, P = self.T, self.P
        Pss = P[4]
        of = self.c["ones_f"]
        for j in range(nchunks):
            sq = self.nextF()
            T.op("act", lambda e: e.activation(sq.t[:], src_fn(j), AF.Square), reads=reads, writes=[sq])
            T.op("pe", lambda e: e.matmul(Pss.t[:], of.t[:], sq.t[:], start=(j == 0), stop=(j == nchunks - 1)),
                 reads=[sq, of], writes=[Pss])
        T.op("act", lambda e: e.activation(self.rstd.t[:], Pss.t[:], AF.Ln, bias=EPS, scale=1.0 / nfeat), reads=[Pss], writes=[self.rstd])
        T.op("act", lambda e: e.activation(self.rstd.t[:], self.rstd.t[:], AF.Exp, scale=-0.5), reads=[self.rstd], writes=[self.rstd])

    def norm_mod(self, l, i, v):
        T = self.T
        xT, actT = self.xT, self.actT
        self.rms_rstd(lambda j: xT.t[:, j, :], 16, D, [xT])
        for dc in range(16):
            tmp = self.nextF()
            T.op("dve", lambda e: e.tensor_tensor(tmp.t[:], xT.t[:, dc, :], self.rstd.t[:], ALU.mult), reads=[xT, self.rstd], writes=[tmp])
            T.op("act", lambda e: e.activation(actT.t[:, dc, :], tmp.t[:], AF.Identity, bias=self.MODv(l, 3 * i, dc, v), scale=self.Sv(l, i, dc, v)),
                 reads=[tmp, self.c["MOD"], self.c["S"]], writes=[actT])

    def load_w(self, src, wc):
        w = self.wb[self.wn % 3]
        self.wn += 1
        k = src.shape[0] // 128
        view = w.t[:, 0:k * wc].rearrange("p (k n) -> p k n", n=wc)
        self.T.dma("pool", view, src.rearrange("(k p) n -> p k n", p=128), writes=[w])
        return w, view

    def proj_fm(self, wsrc, col0, ncols, evac, nk=16, rhs_fn=None):
        T = self.T
        if rhs_fn is None:
            rhs_fn = lambda kc: self.actT.t[:, kc, :]
            rd = [self.actT]
        else:
            rd = [self.aT]
        wcb = 512 if nk == 16 else 128
        ct = 0
        for b0 in range(0, ncols, wcb):
            wc = min(wcb, ncols - b0)
            w, view = self.load_w(wsrc[:, col0 + b0:col0 + b0 + wc], wc)
            for t0 in range(0, wc, 128):
                Pt = self.nextP()
                T.group("pe", [(lambda e, kc=kc: e.matmul(Pt.t[:], view[:, kc, t0:t0 + 128], rhs_fn(kc), start=(kc == 0), stop=(kc == nk - 1)))
                               for kc in range(nk)], reads=[w] + rd, writes=[Pt])
                evac(ct, Pt)
                ct += 1

    def proj_tm(self, wsrc, col0, ncols, evac):
        T = self.T
        for b0 in range(0, ncols, 512):
            wc = min(512, ncols - b0)
            w, view = self.load_w(wsrc[:, col0 + b0:col0 + b0 + wc], wc)
            for tt in range(4):
                Pt = self.nextP()
                T.group("pe", [(lambda e, kc=kc: e.matmul(Pt.t[:, 0:wc], self.actT.t[:, kc, tt * 128:(tt + 1) * 128], view[:, kc, :],
                                                          start=(kc == 0), stop=(kc == 15))) for kc in range(16)],
                        reads=[w, self.actT], writes=[Pt])
                evac(b0, wc, tt, Pt)

    class FmOut:
        def __init__(self, prog, row0, tb):
            self.p = prog; self.row0 = row0; self.tb = tb; self.n = 0; self.cur = None

        def slot(self):
            if self.n % 4 == 0:
                self.cur = self.p.so[self.p.son % 2]
                self.p.son += 1
            j = self.n % 4
            self.n += 1
            return self.cur, self.cur.t[:, j, :]

        def done_tile(self, last=False):
            if self.n % 4 == 0 or last:
                cnt = (self.n - 1) % 4 + 1
                r0 = self.row0 + (self.n - cnt) * 128
                p = self.p
                c0 = self.tb * 512
                p.T.dma("act", p.U_FM.t[r0:r0 + cnt * 128, c0:c0 + 512].rearrange("(t p) n -> p t n", p=128), self.cur.t[:, 0:cnt, :],
                        reads=[self.cur], writes=[p.U_FM.k((r0 // 128, self.tb))])

    def fm_group(self, wsrc, col0, ncols, row0, tb, evac_to):
        out = Prog.FmOut(self, row0, tb)
        ntile = ncols // 128

        def ev(ct, Pt):
            buf, dst = out.slot()
            evac_to(ct, Pt, buf, dst)
            out.done_tile(last=(ct == ntile - 1))
        self.proj_fm(wsrc, col0, ncols, ev)

    def conv_evac(self, tb, CW, CB, base_w, base_b, nchan_chunks):
        T = self.T
        seg = 256 if tb == 8 else 64

        def ev(ct, Pt, buf, dst):
            cv = self.nextF()
            w0 = CW.t[:, base_w + ct:base_w + ct + 1]
            w1 = CW.t[:, base_w + nchan_chunks + ct:base_w + nchan_chunks + ct + 1]
            w2 = CW.t[:, base_w + 2 * nchan_chunks + ct:base_w + 2 * nchan_chunks + ct + 1]
            bb = CB.t[:, base_b + ct:base_b + ct + 1]
            T.op("act", lambda e: e.activation(cv.t[:], Pt.t[:], AF.Identity, bias=bb, scale=w1), reads=[Pt, CW, CB], writes=[cv])
            cvv = cv.t[:].rearrange("p (s l) -> p s l", l=seg)
            pv = Pt.t[:].rearrange("p (s l) -> p s l", l=seg)
            T.op("dve", lambda e: e.scalar_tensor_tensor(cvv[:, :, 1:seg], pv[:, :, 0:seg - 1], w0, cvv[:, :, 1:seg], ALU.mult, ALU.add),
                 reads=[Pt, cv, CW], writes=[cv])
            T.op("dve", lambda e: e.scalar_tensor_tensor(cvv[:, :, 0:seg - 1], pv[:, :, 1:seg], w2, cvv[:, :, 0:seg - 1], ALU.mult, ALU.add),
                 reads=[Pt, cv, CW], writes=[cv])
            T.op("act", lambda e: e.activation(dst, cv.t[:], AF.Silu), reads=[cv], writes=[buf])
        return ev

    def act_evac(self, func):
        T = self.T

        def ev(ct, Pt, buf, dst):
            T.op("act", lambda e: e.activation(dst, Pt.t[:], func), reads=[Pt], writes=[buf])
        return ev

    def copy_evac(self):
        T = self.T

        def ev(ct, Pt, buf, dst):
            T.op("dve", lambda e: e.tensor_copy(dst, Pt.t[:]), reads=[Pt], writes=[buf])
        return ev

    def tm_evac(self, tb, dst_buf, col_off, dt):
        T = self.T
        c0 = tb * 512

        def ev(b0, wc, tt, Pt):
            if dt == BF16:
                st = self.so[self.son % 2]; self.son += 1
                sv = st.t[:, 0, 0:wc]
            else:
                st = self.nextF()
                sv = st.t[:, 0:wc]
            T.op("dve", lambda e: e.tensor_copy(sv, Pt.t[:, 0:wc]), reads=[Pt], writes=[st])
            r0 = c0 + tt * 128
            T.dma("act", dst_buf.t[r0:r0 + 128, col_off + b0:col_off + b0 + wc], sv, reads=[st],
                  writes=[dst_buf.k((r0 // 128, (col_off + b0) // 512))])
        return ev

    def stage_a(self, layer, tb):
        I, c = self.I, self.c
        odd, idx = self.kind(layer)
        if not odd:
            e_ = idx
            W = I["even_w_in"][e_]
            self.fm_group(W, 0, 1024, 0, tb, self.act_evac(AF.Silu))
            self.fm_group(W, 1024, 1536, 1024, tb, self.conv_evac(tb, c["SCW"], c["SCB"], e_ * 36, e_ * 12, 12))
            self.fm_group(W, 2592, 1024, 2560, tb, self.act_evac(AF.Silu))
            self.fm_group(W, 3616, 2048, 3584, tb, self.copy_evac())
            self.fm_group(W, 6688, 1024, 5632, tb, self.act_evac(AF.Silu))
            self.proj_tm(W, 5664, 1024, self.tm_evac(tb, self.U_TM, 0, BF16))
            self.proj_tm(W, 2560, 32, self.tm_evac(tb, self.U_G, 0, F32))
        else:
            o_ = idx
            W = I["odd_w_in"][o_]
            self.fm_group(W, 0, 2048, 0, tb, self.conv_evac(tb, c["MCW"], c["MCB"], o_ * 48, o_ * 16, 16))
            self.fm_group(W, 4096, 2048, 2048, tb, self.act_evac(AF.Sigmoid))
            self.proj_tm(W, 2048, 2048, self.tm_evac(tb, self.U_TM, 0, BF16))
            self.proj_tm(W, 6144, 16, self.tm_evac(tb, self.U_G, 0, F32))

    def xk(self, lst=None):
        return [self.xT.k(i) for i in (range(16) if lst is None else lst)]

    def stage_c(self, l, tb, v):
        T, c, I = self.T, self.c, self.I
        xT, aT = self.xT, self.aT
        self.finalize_mixer(l, tb)
        odd, idx = self.kind(l)
        W = I["odd_w_out"][idx] if odd else I["even_w_out"][idx]

        def ev_res(m):
            def ev(ct, Pt):
                T.op("dve", lambda e: e.scalar_tensor_tensor(xT.t[:, ct, :], Pt.t[:], self.MODv(l, m, ct, v), xT.t[:, ct, :], ALU.mult, ALU.add),
                     reads=[Pt, c["MOD"], xT], writes=[xT])
            return ev
        self.proj_fm(W, 0, 2048, ev_res(2))
        self.norm_mod(l, 1, v)

        def ev1(ct, Pt):
            tmp = self.nextF()
            T.op("act", lambda e: e.activation(tmp.t[:], Pt.t[:], AF.Relu), reads=[Pt], writes=[tmp])
            T.op("dve", lambda e: e.tensor_tensor(aT.t[:, ct, :], tmp.t[:], tmp.t[:], ALU.mult), reads=[tmp], writes=[aT])
        self.proj_fm(I["mlp_w1"][l], 0, 8192, ev1)
        self.proj_fm(I["mlp_w2"][l], 0, 2048, ev_res(5), nk=64, rhs_fn=lambda kc: aT.t[:, kc, :])

    def ld_blk(self, dst, src_buf, r0, tb):
        c0 = tb * 512
        self.T.dma("sp", dst.t[:], src_buf.t[r0:r0 + 512, c0:c0 + 512].rearrange("(k p) n -> p k n", p=128), writes=[dst])

    def finalize_mixer(self, l, tb):
        T, c = self.T, self.c
        hA, hB, hC, hD, hs, actT = self.hA, self.hB, self.hC, self.hD, self.hs, self.actT
        odd, idx = self.kind(l)
        for gI in range(4):
            r0 = gI * 512
            self.ld_blk(hA, self.HF[0], r0, tb)
            self.ld_blk(hB, self.HF[1], r0, tb)
            T.op("dve", lambda e: e.tensor_tensor(hs.t[:], hA.t[:], hB.t[:], ALU.add), reads=[hA, hB], writes=[hs])
            if odd:
                self.ld_blk(hC, self.U_FM, 2048 + r0, tb)
                self.rms_rstd(lambda j: hs.t[:, j, :], 4, 512, [hs])
                for j in range(4):
                    dc = gI * 4 + j
                    tmp = self.nextF()
                    T.op("dve", lambda e: e.tensor_tensor(tmp.t[:], hs.t[:, j, :], self.rstd.t[:], ALU.mult), reads=[hs, self.rstd], writes=[tmp])
                    T.op("dve", lambda e: e.scalar_tensor_tensor(actT.t[:, dc, :], tmp.t[:], c["MNW"].t[:, idx * 16 + dc:idx * 16 + dc + 1],
                                                                 hC.t[:, j, :], ALU.mult, ALU.mult), reads=[tmp, hC, c["MNW"]], writes=[actT])
            elif gI < 2:
                e_ = idx
                self.ld_blk(hC, self.U_FM, 1024 + r0, tb)
                self.ld_blk(hD, self.U_FM, r0, tb)
                for j in range(4):
                    dc = gI * 4 + j
                    T.op("dve", lambda e: e.scalar_tensor_tensor(hs.t[:, j, :], hC.t[:, j, :], c["DSK"].t[:, e_ * 8 + dc:e_ * 8 + dc + 1],
                                                                 hs.t[:, j, :], ALU.mult, ALU.add), reads=[hC, hs, c["DSK"]], writes=[hs])
                T.op("dve", lambda e: e.tensor_tensor(hs.t[:], hs.t[:], hD.t[:], ALU.mult), reads=[hs, hD], writes=[hs])
                self.rms_rstd(lambda j: hs.t[:, j, :], 4, 512, [hs])
                for j in range(4):
                    dc = gI * 4 + j
                    tmp = self.nextF()
                    T.op("dve", lambda e: e.tensor_tensor(tmp.t[:], hs.t[:, j, :], self.rstd.t[:], ALU.mult), reads=[hs, self.rstd], writes=[tmp])
                    T.op("act", lambda e: e.activation(actT.t[:, dc, :], tmp.t[:], AF.Copy, scale=c["SNW"].t[:, e_ * 8 + dc:e_ * 8 + dc + 1]),
                         reads=[tmp, c["SNW"]], writes=[actT])
            else:
                e_ = idx
                self.ld_blk(hD, self.U_FM, 5632 + (gI - 2) * 512, tb)
                for j in range(4):
                    dc = gI * 4 + j
                    self.rms_rstd(lambda _: hs.t[:, j, :], 1, 128, [hs])
                    tmp = self.nextF()
                    T.op("dve", lambda e: e.tensor_tensor(tmp.t[:], hs.t[:, j, :], self.rstd.t[:], ALU.mult), reads=[hs, self.rstd], writes=[tmp])
                    T.op("dve", lambda e: e.scalar_tensor_tensor(actT.t[:, dc, :], tmp.t[:], c["HNW"].t[:, e_ * 8 + dc - 8:e_ * 8 + dc - 7],
                                                                 hD.t[:, j, :], ALU.mult, ALU.mult), reads=[tmp, hD, c["HNW"]], writes=[actT])

    def final_out(self, tb):
        T, c = self.T, self.c
        xT = self.xT
        idf = c["ident_f"]
        self.rms_rstd(lambda j: xT.t[:, j, :], 16, D, [xT])
        for dc in range(16):
            tmp = self.nextF()
            T.op("dve", lambda e: e.tensor_tensor(tmp.t[:], xT.t[:, dc, :], self.rstd.t[:], ALU.mult), reads=[xT, self.rstd], writes=[tmp])
            T.op("act", lambda e: e.activation(xT.t[:, dc, :], tmp.t[:], AF.Copy, scale=c["FNW"].t[:, dc:dc + 1]), reads=[tmp, c["FNW"]], writes=[xT])
        for tt in range(4):
            for g in range(4):
                Pg = self.nextP()
                T.group("pe", [(lambda e, j=j: e.transpose(Pg.t[:, j * 128:(j + 1) * 128], xT.t[:, g * 4 + j, tt * 128:(tt + 1) * 128], idf.t[:]))
                               for j in range(4)], reads=[xT, idf], writes=[Pg])
                if g % 2:
                    T.op("act", lambda e: e.activation(self.xin.t[:, g * 512:(g + 1) * 512], Pg.t[:], AF.Copy), reads=[Pg], writes=[self.xin])
                else:
                    T.op("dve", lambda e: e.tensor_copy(self.xin.t[:, g * 512:(g + 1) * 512], Pg.t[:]), reads=[Pg], writes=[self.xin])
            r0 = (tb % 4) * 512 + tt * 128
            T.dma("sp", self.out[tb // 4, r0:r0 + 128, :], self.xin.t[:], reads=[self.xin], writes=[self.r_out])

    def mixer_phase(self, es, layer):
        self.dslot = 0
        odd, idx = self.kind(layer)
        if odd:
            self.mlstm_phase(es, idx)
        else:
            self.even_phase(es, idx)

    def trps(self, i):
        return self.P[i].t[:].bitcast(BF16)

    def gate_cums(self, d, src, n, ws, pres=None):
        T, c, P = self.T, self.c, self.P
        Pg = P[7]
        pr = Pg if pres is None else pres
        T.op("pe", lambda e: e.matmul(Pg.t[:, 256:256 + n], c[f"tri{d}"].t[:], src.t[:, 0:n], start=True, stop=True),
             reads=[src, c[f"tri{d}"]], writes=[pr])
        T.op("pe", lambda e: e.matmul(Pg.t[:, 256 + n:256 + 2 * n], c["ones_f"].t[:], src.t[:, 0:n], start=True, stop=True),
             reads=[src, c["ones_f"]], writes=[pr])
        T.op("dve", lambda e: e.tensor_copy(ws["cs"].t[:, 0:2 * n], Pg.t[:, 256:256 + 2 * n]), reads=[pr], writes=[ws["cs"]])
        return ws["cs"]

    def decay_mats(self, d, src, h, bias_ap, ws, need_e=True):
        T, c, P = self.T, self.c, self.P
        slot = self.dslot % 2
        self.dslot += 1
        PA = P[1]
        pr = PA.k(slot)
        b0 = slot * 256
        Dm, E = ws["Dm"][slot], ws["E"][slot]
        bc = src.t[:, h:h + 1].to_broadcast([128, 128])
        fns = [lambda e: e.matmul(PA.t[:, b0 + 128:b0 + 256], bc, c[f"tri{d}"].t[:], start=True, stop=False),
               lambda e: e.matmul(PA.t[:, b0 + 128:b0 + 256], c["ident_f"].t[:], c[f"msk{d}"].t[:], start=False, stop=True)]
        if need_e:
            fns.insert(0, lambda e: e.matmul(PA.t[:, b0:b0 + 128], bc, c[f"tri{d}"].t[:], start=True, stop=True))
        T.group("pe", fns, reads=[src, c[f"tri{d}"], c[f"msk{d}"], c["ident_f"]], writes=[pr])
        T.op("act", lambda e: e.activation(Dm.t[:], PA.t[:, b0 + 128:b0 + 256], AF.Exp, bias=bias_ap, scale=-1.0),
             reads=[pr, ws["bias"]], writes=[Dm])
        if need_e:
            T.op("act", lambda e: e.activation(E.t[:], PA.t[:, b0:b0 + 128], AF.Exp, scale=-1.0), reads=[pr], writes=[E])
        return Dm, E

    def mlstm_phase(self, es, o_):
        T, c, P = self.T, self.c, self.P
        chains = [(b, d) for b in range(2) for d in range(2)]
        Cf = {ch: self.sb(es, "Cf", [128, 4, 2, 640], F32) for ch in chains}
        Cb = {ch: self.sb(es, "Cb", [128, 4, 2, 640], BF16) for ch in chains}
        for ch in chains:
            T.op("pool", lambda e: e.memset(Cf[ch].t[:], 0.0), writes=[Cf[ch]])
            T.op("pool", lambda e: e.memset(Cb[ch].t[:], 0.0), writes=[Cb[ch]])
        sets = []
        for i in range(2):
            w = {}
            w["qT"] = self.sb(es, "qT", [128, 8, 128], BF16)
            w["kT"] = self.sb(es, "kT", [128, 8, 128], BF16)
            w["Vt"] = self.sb(es, "Vt", [128, 2048], BF16)
            w["g"] = self.sb(es, "g", [128, 16], F32)
            w["li"] = self.sb(es, "li", [128, 4], F32)
            w["sp"] = self.sb(es, "sp", [128, 4], F32)
            w["cs"] = self.sb(es, "cs", [128, 8], F32)
            w["bias"] = self.sb(es, "bias", [128, 4], F32)
            w["wsx"] = self.sb(es, "wsx", [128, 4], F32)
            w["cdec"] = self.sb(es, "cdec", [128, 4], F32)
            w["Dm"] = [self.sb(es, "Dm", [128, 128], F32) for _ in range(2)]
            w["E"] = [self.sb(es, "E", [128, 128], F32) for _ in range(2)]
            w["WT"] = [self.sb(es, "WT", [128, 128], BF16) for _ in range(2)]
            w["qs"] = [self.sb(es, "qs", [128, 2, 128], BF16) for _ in range(2)]
            w["ke"] = [self.sb(es, "ke", [128, 256], BF16) for _ in range(2)]
            w["dd"] = [self.sb(es, "dd", [128, 128], F32) for _ in range(2)]
            w["ho"] = self.sb(es, "ho", [128, 16, 128], BF16)
            sets.append(w)
        GB = c["GB"]
        n = 0
        self.hslot = 0
        for step in range(18):
            for ch in chains:
                b, d = ch
                c0 = chain_chunks(b, d)[step]
                w = sets[n % 2]; n += 1
                T.dma("sp", w["qT"].t[:], self.U_FM.t[0:1024, c0:c0 + 128].rearrange("(k p) n -> p k n", p=128), writes=[w["qT"]])
                T.dma("sp", w["kT"].t[:], self.U_FM.t[1024:2048, c0:c0 + 128].rearrange("(k p) n -> p k n", p=128), writes=[w["kT"]])
                T.dma("sp", w["Vt"].t[:], self.U_TM.t[c0:c0 + 128, 0:2048], writes=[w["Vt"]])
                T.dma("sp", w["g"].t[:], self.U_G.t[c0:c0 + 128, 0:16], writes=[w["g"]])
                g, li, sp = w["g"], w["li"], w["sp"]
                T.op("dve", lambda e: e.tensor_tensor(li.t[:], g.t[:, d * 4:d * 4 + 4], GB.t[:, o_ * 16 + d * 4:o_ * 16 + d * 4 + 4], ALU.add),
                     reads=[g, GB], writes=[li])
                T.op("dve", lambda e: e.tensor_tensor(sp.t[:], g.t[:, 8 + d * 4:12 + d * 4], GB.t[:, o_ * 16 + 8 + d * 4:o_ * 16 + 12 + d * 4], ALU.add),
                     reads=[g, GB], writes=[sp])
                T.op("act", lambda e: e.activation(sp.t[:], sp.t[:], AF.Exp, scale=-1.0), reads=[sp], writes=[sp])
                T.op("act", lambda e: e.activation(sp.t[:], sp.t[:], AF.Ln, bias=1.0), reads=[sp], writes=[sp])
                cs = self.gate_cums(d, sp, 4, w, pres=P[7].k("g"))
                bias, wsx, cdec = w["bias"], w["wsx"], w["cdec"]
                T.op("dve", lambda e: e.tensor_tensor(bias.t[:], li.t[:], cs.t[:, 0:4], ALU.add), reads=[li, cs], writes=[bias])
                T.op("dve", lambda e: e.tensor_tensor(wsx.t[:], bias.t[:], cs.t[:, 4:8], ALU.subtract), reads=[bias, cs], writes=[wsx])
                T.op("act", lambda e: e.activation(wsx.t[:], wsx.t[:], AF.Exp), reads=[wsx], writes=[wsx])
                T.op("act", lambda e: e.activation(cdec.t[:], cs.t[:, 4:8], AF.Exp, scale=-1.0), reads=[cs], writes=[cdec])
                qT, kT, Vt, ho = w["qT"], w["kT"], w["Vt"], w["ho"]
                cf, cb = Cf[ch], Cb[ch]
                for h in range(4):
                    hs_ = self.hslot % 2
                    self.hslot += 1
                    WT, qs, ke, dd = w["WT"][hs_], w["qs"][hs_], w["ke"][hs_], w["dd"][hs_]
                    PS = P[0]
                    prS = PS.k(hs_)
                    sS = hs_ * 128
                    T.group("pe", [(lambda e, j=j: e.matmul(PS.t[:, sS:sS + 128], kT.t[:, h * 2 + j, :], qT.t[:, h * 2 + j, :], start=(j == 0), stop=(j == 1)))
                                   for j in range(2)], reads=[kT, qT], writes=[prS])
                    Dm, E = self.decay_mats(d, sp, h, bias.t[:, h:h + 1], w)
                    T.op("dve", lambda e: e.tensor_tensor(WT.t[:], PS.t[:, sS:sS + 128], Dm.t[:], ALU.mult), reads=[prS, Dm], writes=[WT])
                    T.op("dve", lambda e: e.tensor_tensor(qs.t[:], qT.t[:, h * 2:h * 2 + 2, :], E.t[:].unsqueeze(1).to_broadcast([128, 2, 128]), ALU.mult),
                         reads=[qT, E], writes=[qs])
                    PT = P[2]
                    prT = PT.k(hs_)
                    trv = self.trps(2)[:, hs_ * 256:(hs_ + 1) * 256]
                    T.group("pe", [(lambda e, j=j: e.transpose(trv[:, j * 128:(j + 1) * 128], kT.t[:, h * 2 + j, :], c["ident_b"].t[:])) for j in range(2)],
                            reads=[kT, c["ident_b"]], writes=[prT])
                    T.op("act", lambda e: e.activation(ke.t[:], trv[:, 0:256], AF.Copy, scale=wsx.t[:, h:h + 1]), reads=[prT, wsx], writes=[ke])
                    PN = P[3 + hs_]
                    PD = P[7]
                    prD = PD.k(hs_)
                    sD = hs_ * 128
                    fns = []
                    for vc in range(5):
                        if vc < 4:
                            dst = PN.t[:, vc * 128:(vc + 1) * 128]
                            l0 = Vt.t[:, h * 512 + vc * 128:h * 512 + (vc + 1) * 128]
                        else:
                            dst = PD.t[:, sD:sD + 128]
                            l0 = c["ones_b"].t[:]
                        fns.append(lambda e, dst=dst, l0=l0: e.matmul(dst, l0, WT.t[:], start=True, stop=False))
                        for j in range(2):
                            fns.append(lambda e, dst=dst, j=j, vc=vc: e.matmul(dst, cb.t[:, h, j, vc * 128:(vc + 1) * 128], qs.t[:, j, :], start=False, stop=(j == 1)))
                    T.group("pe", fns, reads=[Vt, WT, cb, qs, c["ones_b"]], writes=[PN, prD])
                    T.op("act", lambda e: e.activation(dd.t[:], PD.t[:, sD:sD + 128], AF.Abs), reads=[prD], writes=[dd])
                    T.op("dve", lambda e: e.tensor_scalar_max(dd.t[:], dd.t[:], 1.0), reads=[dd], writes=[dd])
                    T.op("dve", lambda e: e.reciprocal(dd.t[:], dd.t[:]), reads=[dd], writes=[dd])
                    T.op("dve", lambda e: e.tensor_tensor(ho.t[:, h * 4:(h + 1) * 4, :], PN.t[:].rearrange("p (v t) -> p v t", t=128),
                                                          dd.t[:].unsqueeze(1).to_broadcast([128, 4, 128]), ALU.mult), reads=[PN, dd], writes=[ho])
                    for j in range(2):
                        Pc = P[5 + j]
                        T.op("pe", lambda e: e.matmul(Pc.t[:], ke.t[:, j * 128:(j + 1) * 128], Vt.t[:, h * 512:(h + 1) * 512], start=True, stop=True),
                             reads=[ke, Vt], writes=[Pc])
                        T.op("dve", lambda e: e.scalar_tensor_tensor(cf.t[:, h, j, 0:512], cf.t[:, h, j, 0:512], cdec.t[:, h:h + 1], Pc.t[:], ALU.mult, ALU.add),
                             reads=[cf, cdec, Pc], writes=[cf])
                    Pn = P[0]
                    prN = Pn.k("n")
                    T.group("pe", [(lambda e, j=j: e.matmul(Pn.t[:, 256 + j * 128:256 + (j + 1) * 128], ke.t[:, j * 128:(j + 1) * 128], c["ones_b"].t[:], start=True, stop=True))
                                   for j in range(2)], reads=[ke, c["ones_b"]], writes=[prN])
                    T.op("dve", lambda e: e.scalar_tensor_tensor(cf.t[:, h, :, 512:640], cf.t[:, h, :, 512:640], cdec.t[:, h:h + 1],
                                                                 Pn.t[:, 256:512].rearrange("p (j n) -> p j n", n=128), ALU.mult, ALU.add),
                         reads=[cf, cdec, prN], writes=[cf])
                    T.op("act", lambda e: e.activation(cb.t[:, h], cf.t[:, h], AF.Copy), reads=[cf], writes=[cb])
                T.dma("act", self.HF[d].t[:, c0:c0 + 128].rearrange("(k p) n -> p k n", p=128), ho.t[:], reads=[ho], writes=[self.HF[d].k(c0 // 128)])

    def even_phase(self, es, e_):
        T, c, P = self.T, self.c, self.P
        chains = [(b, d) for b in range(2) for d in range(2)]
        rst = self.sb(es, "rst", [128, 1024], F32)
        T.op("pool", lambda e: e.memset(rst.t[:], 1.0), writes=[rst])
        T.op("pool", lambda e: e.memset(rst.t[:].rearrange("p (c t) -> p c t", t=128)[:, :, 0:1], 0.0), reads=[rst], writes=[rst])
        Hf = {ch: self.sb(es, "Hf", [128, 1024], F32) for ch in chains}
        Hb = {ch: self.sb(es, "Hb", [128, 1024], BF16) for ch in chains}
        Sf = {ch: self.sb(es, "Sf", [128, 8, 128], F32) for ch in chains}
        Sb = {ch: self.sb(es, "Sb", [128, 8, 128], BF16) for ch in chains}
        for ch in chains:
            for t in (Hf[ch], Hb[ch], Sf[ch], Sb[ch]):
                T.op("pool", lambda e: e.memset(t.t[:], 0.0), writes=[t])
        KI = {}
        for d in range(2):
            KI[d] = [self.sb(es, f"KI{d}_{i}", [128, 8, 128], BF16) for i in range(8)]
            for t in KI[d]:
                T.op("pool", lambda e: e.memset(t.t[:], 0.0), writes=[t])
        sets = []
        for i in range(2):
            w = {}
            def mk(name, shape, dt):
                w[name] = self.sb(es, name, shape, dt)
            mk("BCT", [128, 4, 128], BF16); mk("xsT", [128, 8, 128], BF16); mk("dtr", [128, 32], F32)
            mk("dt", [128, 16], F32); mk("na", [128, 16], F32); mk("cs", [128, 32], F32); mk("wsx", [128, 16], F32)
            mk("cdec", [128, 16], F32); mk("xdt", [128, 1024], BF16); mk("xdtw", [128, 1024], BF16); mk("Btm", [128, 256], BF16)
            mk("G", [128, 256], F32); mk("ysb", [128, 1024], BF16); mk("hos", [128, 8, 128], BF16)
            for nm, dt_ in (("Dm", F32), ("E", F32), ("MT", BF16), ("CsT", BF16), ("attm", BF16)):
                w[nm] = [self.sb(es, nm, [128, 128], dt_) for _ in range(2)]
            w["bias"] = w["cs"]
            mk("qT", [128, 8, 128], BF16); mk("fT", [128, 8, 128], BF16); mk("Vt", [128, 1024], BF16)
            mk("A", [128, 8, 128], F32); mk("KIN", [128, 8, 128], F32); mk("LG", [128, 8, 128], F32); mk("BC", [128, 8, 128], F32)
            mk("TMP", [128, 8, 128], F32); mk("tot", [128, 8], F32); mk("hdec", [128, 8], F32); mk("CST", [128, 8, 8], F32)
            mk("qs", [128, 8, 128], BF16); mk("keT", [128, 8, 128], BF16); mk("ketm", [128, 1024], BF16); mk("qloc", [128, 8, 128], BF16)
            mk("hoh", [128, 8, 128], BF16)
            sets.append(w)
        n = 0
        for step in range(18):
            for ch in chains:
                b, d = ch
                c0 = chain_chunks(b, d)[step]
                w = sets[n % 2]; n += 1
                self.ssd_step(e_, d, c0, w, Hf[ch], Hb[ch])
                self.hgrn_step(e_, d, c0, w, Sf[ch], Sb[ch], KI[d], rst)

    def ssd_step(self, e_, d, c0, w, hf, hb):
        T, c, P = self.T, self.c, self.P
        BCT, xsT, dtr, dt, na, wsx, cdec = w["BCT"], w["xsT"], w["dtr"], w["dt"], w["na"], w["wsx"], w["cdec"]
        xdt, xdtw, Btm, G, ysb, hos = w["xdt"], w["xdtw"], w["Btm"], w["G"], w["ysb"], w["hos"]
        T.dma("sp", BCT.t[:], self.U_FM.t[2048:2560, c0:c0 + 128].rearrange("(k p) n -> p k n", p=128), writes=[BCT])
        T.dma("sp", xsT.t[:], self.U_FM.t[1024:2048, c0:c0 + 128].rearrange("(k p) n -> p k n", p=128), writes=[xsT])
        T.dma("sp", dtr.t[:], self.U_G.t[c0:c0 + 128, 0:32], writes=[dtr])
        o = e_ * 32 + d * 16
        T.op("dve", lambda e: e.tensor_tensor(dt.t[:], dtr.t[:, d * 16:(d + 1) * 16], c["DTB"].t[:, o:o + 16], ALU.add), reads=[dtr, c["DTB"]], writes=[dt])
        T.op("act", lambda e: e.activation(dt.t[:], dt.t[:], AF.Exp), reads=[dt], writes=[dt])
        T.op("act", lambda e: e.activation(dt.t[:], dt.t[:], AF.Ln, bias=1.0), reads=[dt], writes=[dt])
        T.op("dve", lambda e: e.tensor_tensor(na.t[:], dt.t[:], c["EA"].t[:, o:o + 16], ALU.mult), reads=[dt, c["EA"]], writes=[na])
        cs = self.gate_cums(d, na, 16, w)
        T.op("dve", lambda e: e.tensor_tensor(wsx.t[:], cs.t[:, 16:32], cs.t[:, 0:16], ALU.subtract), reads=[cs], writes=[wsx])
        T.op("act", lambda e: e.activation(wsx.t[:], wsx.t[:], AF.Exp, scale=-1.0), reads=[wsx], writes=[wsx])
        T.op("act", lambda e: e.activation(cdec.t[:], cs.t[:, 16:32], AF.Exp, scale=-1.0), reads=[cs], writes=[cdec])
        tr2 = self.trps(2)
        T.group("pe", [(lambda e, j=j: e.transpose(tr2[:, j * 128:(j + 1) * 128], xsT.t[:, j, :], c["ident_b"].t[:])) for j in range(8)],
                reads=[xsT, c["ident_b"]], writes=[P[2]])
        T.op("dve", lambda e: e.tensor_tensor(xdt.t[:].rearrange("p (h q) -> p h q", q=64), tr2[:, 0:1024].rearrange("p (h q) -> p h q", q=64),
                                              dt.t[:].unsqueeze(2).to_broadcast([128, 16, 64]), ALU.mult), reads=[P[2], dt], writes=[xdt])
        T.op("dve", lambda e: e.tensor_tensor(xdtw.t[:].rearrange("p (h q) -> p h q", q=64), xdt.t[:].rearrange("p (h q) -> p h q", q=64),
                                              wsx.t[:].unsqueeze(2).to_broadcast([128, 16, 64]), ALU.mult), reads=[xdt, wsx], writes=[xdtw])
        tr7 = self.trps(7)
        T.group("pe", [(lambda e, g=g: e.transpose(tr7[:, g * 128:(g + 1) * 128], BCT.t[:, g, :], c["ident_b"].t[:])) for g in range(2)],
                reads=[BCT, c["ident_b"]], writes=[P[7]])
        T.op("act", lambda e: e.activation(Btm.t[:], tr7[:, 0:256], AF.Copy), reads=[P[7]], writes=[Btm])
        T.group("pe", [(lambda e, g=g: e.matmul(P[0].t[:, g * 128:(g + 1) * 128], BCT.t[:, g, :], BCT.t[:, 2 + g, :], start=True, stop=True)) for g in range(2)],
                reads=[BCT], writes=[P[0].k(0), P[0].k(1)])
        T.op("act", lambda e: e.activation(G.t[:], P[0].t[:, 0:256], AF.Copy), reads=[P[0].k(0), P[0].k(1)], writes=[G])
        for h in range(16):
            g = h // 8
            Dm, E = self.decay_mats(d, na, h, cs.t[:, h:h + 1], w)
            MT, CsT = w["MT"][h % 2], w["CsT"][h % 2]
            T.op("dve", lambda e: e.tensor_tensor(MT.t[:], G.t[:, g * 128:(g + 1) * 128], Dm.t[:], ALU.mult), reads=[G, Dm], writes=[MT])
            T.op("dve", lambda e: e.tensor_tensor(CsT.t[:], BCT.t[:, 2 + g, :], E.t[:], ALU.mult), reads=[BCT, E], writes=[CsT])
            PY = P[3 + h // 8]
            dst = PY.t[:, (h % 8) * 64:(h % 8 + 1) * 64]
            T.group("pe", [lambda e: e.matmul(dst, MT.t[:], xdt.t[:, h * 64:(h + 1) * 64], start=True, stop=False),
                           lambda e: e.matmul(dst, CsT.t[:], hb.t[:, h * 64:(h + 1) * 64], start=False, stop=True)],
                    reads=[MT, xdt, CsT, hb], writes=[PY])
        T.op("act", lambda e: e.activation(ysb.t[:, 0:512], P[3].t[:], AF.Copy), reads=[P[3]], writes=[ysb])
        T.op("dve", lambda e: e.tensor_copy(ysb.t[:, 512:1024], P[4].t[:]), reads=[P[4]], writes=[ysb])
        for g in range(2):
            Pc = P[5 + g]
            T.op("pe", lambda e: e.matmul(Pc.t[:], Btm.t[:, g * 128:(g + 1) * 128], xdtw.t[:, g * 512:(g + 1) * 512], start=True, stop=True),
                 reads=[Btm, xdtw], writes=[Pc])
            hv = hf.t[:, g * 512:(g + 1) * 512].rearrange("p (h q) -> p h q", q=64)
            T.op("dve", lambda e: e.tensor_tensor(hv, hv, cdec.t[:, g * 8:(g + 1) * 8].unsqueeze(2).to_broadcast([128, 8, 64]), ALU.mult),
                 reads=[hf, cdec], writes=[hf])
            T.op("dve", lambda e: e.tensor_tensor(hf.t[:, g * 512:(g + 1) * 512], hf.t[:, g * 512:(g + 1) * 512], Pc.t[:], ALU.add), reads=[hf, Pc], writes=[hf])
        T.op("act", lambda e: e.activation(hb.t[:], hf.t[:], AF.Copy), reads=[hf], writes=[hb])
        T.group("pe", [(lambda e, j=j: e.transpose(tr2[:, j * 128:(j + 1) * 128], ysb.t[:, j * 128:(j + 1) * 128], c["ident_b"].t[:])) for j in range(8)],
                reads=[ysb, c["ident_b"]], writes=[P[2]])
        T.op("act", lambda e: e.activation(hos.t[:], tr2[:, 0:1024].rearrange("p (k t) -> p k t", t=128), AF.Copy), reads=[P[2]], writes=[hos])
        T.dma("act", self.HF[d].t[0:1024, c0:c0 + 128].rearrange("(k p) n -> p k n", p=128), hos.t[:], reads=[hos], writes=[self.HF[d].k((0, c0 // 128))])

    def hgrn_step(self, e_, d, c0, w, sf, sb, KI, rst):
        T, c, P = self.T, self.c, self.P
        qT, fT, Vt, A, KIN, LG, BC, TMP = w["qT"], w["fT"], w["Vt"], w["A"], w["KIN"], w["LG"], w["BC"], w["TMP"]
        tot, hdec, CST, qs, keT, ketm, qloc, hoh = w["tot"], w["hdec"], w["CST"], w["qs"], w["keT"], w["ketm"], w["qloc"], w["hoh"]
        T.dma("sp", qT.t[:], self.U_FM.t[2560:3584, c0:c0 + 128].rearrange("(k p) n -> p k n", p=128), writes=[qT])
        T.dma("sp", fT.t[:], self.U_FM.t[3584 + d * 1024:3584 + (d + 1) * 1024, c0:c0 + 128].rearrange("(k p) n -> p k n", p=128), writes=[fT])
        T.dma("sp", Vt.t[:], self.U_TM.t[c0:c0 + 128, 0:1024], writes=[Vt])
        lb = c["LB"].t[:, e_ * 8:(e_ + 1) * 8].unsqueeze(2).to_broadcast([128, 8, 128])
        lb1 = c["LB1"].t[:, e_ * 8:(e_ + 1) * 8].unsqueeze(2).to_broadcast([128, 8, 128])
        T.op("act", lambda e: e.activation(A.t[:], fT.t[:], AF.Sigmoid), reads=[fT], writes=[A])
        T.op("dve", lambda e: e.tensor_tensor(A.t[:], A.t[:], lb1, ALU.mult), reads=[A, c["LB1"]], writes=[A])
        T.op("dve", lambda e: e.tensor_tensor(A.t[:], A.t[:], lb, ALU.add), reads=[A, c["LB"]], writes=[A])
        T.op("dve", lambda e: e.tensor_scalar(KIN.t[:], A.t[:], -1.0, 1.0, ALU.mult, ALU.add), reads=[A], writes=[KIN])
        T.op("act", lambda e: e.activation(LG.t[:], A.t[:], AF.Ln), reads=[A], writes=[LG])
        fl = lambda t: t.t[:].rearrange("p h t -> p (h t)")
        T.op("dve", lambda e: e.tensor_tensor_scan(fl(BC), rst.t[:], fl(LG), 0.0, ALU.mult, ALU.add), reads=[rst, LG], writes=[BC])
        T.op("dve", lambda e: e.tensor_copy(tot.t[:].unsqueeze(2), BC.t[:, :, 127:128]), reads=[BC], writes=[tot])
        totb = tot.t[:].unsqueeze(2).to_broadcast([128, 8, 128])
        if d == 1:
            T.op("dve", lambda e: e.tensor_tensor(BC.t[:], LG.t[:], BC.t[:], ALU.subtract), reads=[LG, BC], writes=[BC])
            T.op("dve", lambda e: e.tensor_tensor(BC.t[:], BC.t[:], totb, ALU.add), reads=[BC, tot], writes=[BC])
        T.op("act", lambda e: e.activation(TMP.t[:], BC.t[:], AF.Exp), reads=[BC], writes=[TMP])
        T.op("dve", lambda e: e.tensor_tensor(qs.t[:], qT.t[:], TMP.t[:], ALU.mult), reads=[qT, TMP], writes=[qs])
        T.op("dve", lambda e: e.tensor_tensor(TMP.t[:], totb, BC.t[:], ALU.subtract), reads=[tot, BC, TMP], writes=[TMP])
        T.op("act", lambda e: e.activation(TMP.t[:], TMP.t[:], AF.Exp), reads=[TMP], writes=[TMP])
        T.op("dve", lambda e: e.tensor_tensor(keT.t[:], TMP.t[:], KIN.t[:], ALU.mult), reads=[TMP, KIN], writes=[keT])
        T.op("act", lambda e: e.activation(hdec.t[:], tot.t[:], AF.Exp), reads=[tot], writes=[hdec])
        tr2 = self.trps(2)
        T.group("pe", [(lambda e, h=h: e.transpose(tr2[:, h * 128:(h + 1) * 128], keT.t[:, h, :], c["ident_b"].t[:])) for h in range(8)],
                reads=[keT, c["ident_b"]], writes=[P[2]])
        T.op("act", lambda e: e.activation(ketm.t[:], tr2[:, 0:1024], AF.Copy), reads=[P[2]], writes=[ketm])
        if d == 0:
            T.op("pool", lambda e: e.memset(CST.t[:, :, 0:1], 0.0), writes=[CST])
            T.op("dve", lambda e: e.tensor_copy(CST.t[:, :, 1:8], BC.t[:, :, 15:127:16]), reads=[BC, CST], writes=[CST])
        else:
            T.op("pool", lambda e: e.memset(CST.t[:, :, 7:8], 0.0), writes=[CST])
            T.op("dve", lambda e: e.tensor_copy(CST.t[:, :, 0:7], BC.t[:, :, 16:128:16]), reads=[BC, CST], writes=[CST])
        v64 = lambda t: t.t[:].rearrange("p h (i u) -> p (h i) u", u=16)
        T.op("dve", lambda e: e.tensor_tensor(v64(TMP), v64(BC), CST.t[:].rearrange("p h i -> p (h i)").unsqueeze(2).to_broadcast([128, 64, 16]), ALU.subtract),
             reads=[BC, CST, TMP], writes=[TMP])
        T.op("act", lambda e: e.activation(TMP.t[:], TMP.t[:], AF.Exp), reads=[TMP], writes=[TMP])
        T.op("dve", lambda e: e.tensor_tensor(qloc.t[:], qT.t[:], TMP.t[:], ALU.mult), reads=[qT, TMP], writes=[qloc])
        for i in range(8):
            lo, hi = (0, 16 * (i + 1)) if d == 0 else (16 * i, 128)
            Ki = KI[i]
            T.op("dve", lambda e: e.tensor_tensor(TMP.t[:, :, lo:hi], CST.t[:, :, i:i + 1].to_broadcast([128, 8, hi - lo]), BC.t[:, :, lo:hi], ALU.subtract),
                 reads=[CST, BC, TMP], writes=[TMP])
            T.op("act", lambda e: e.activation(TMP.t[:, :, lo:hi], TMP.t[:, :, lo:hi], AF.Exp), reads=[TMP], writes=[TMP])
            T.op("dve", lambda e: e.tensor_tensor(Ki.t[:, :, lo:hi], TMP.t[:, :, lo:hi], KIN.t[:, :, lo:hi], ALU.mult), reads=[TMP, KIN], writes=[Ki])
        for h in range(8):
            PA = P[0]
            sl = h % 4
            prA = PA.k(sl)
            a0 = sl * 128
            attm = w["attm"][h % 2]
            T.group("pe", [(lambda e, i=i: e.matmul(PA.t[:, a0 + 16 * i:a0 + 16 * i + 16], KI[i].t[:, h, :], qloc.t[:, h, 16 * i:16 * i + 16], start=True, stop=True))
                           for i in range(8)], reads=KI + [qloc], writes=[prA])
            T.op("dve", lambda e: e.tensor_tensor(attm.t[:], PA.t[:, a0:a0 + 128], c[f"tri{d}"].t[:], ALU.mult), reads=[prA, c[f"tri{d}"]], writes=[attm])
            PO = P[3 + h // 4]
            dst = PO.t[:, (h % 4) * 128:(h % 4 + 1) * 128]
            T.group("pe", [lambda e: e.matmul(dst, Vt.t[:, h * 128:(h + 1) * 128], attm.t[:], start=True, stop=False),
                           lambda e: e.matmul(dst, sb.t[:, h, :], qs.t[:, h, :], start=False, stop=True)],
                    reads=[Vt, attm, sb, qs], writes=[PO])
            Pc = P[5 + h % 2]
            T.op("pe", lambda e: e.matmul(Pc.t[:, 0:128], ketm.t[:, h * 128:(h + 1) * 128], Vt.t[:, h * 128:(h + 1) * 128], start=True, stop=True),
                 reads=[ketm, Vt], writes=[Pc])
            T.op("dve", lambda e: e.scalar_tensor_tensor(sf.t[:, h, :], sf.t[:, h, :], hdec.t[:, h:h + 1], Pc.t[:, 0:128], ALU.mult, ALU.add),
                 reads=[sf, hdec, Pc], writes=[sf])
        T.op("act", lambda e: e.activation(sb.t[:], sf.t[:], AF.Copy), reads=[sf], writes=[sb])
        T.op("act", lambda e: e.activation(hoh.t[:, 0:4, :], P[3].t[:].rearrange("p (h t) -> p h t", t=128), AF.Copy), reads=[P[3]], writes=[hoh])
        T.op("dve", lambda e: e.tensor_copy(hoh.t[:, 4:8, :], P[4].t[:].rearrange("p (h t) -> p h t", t=128)), reads=[P[4]], writes=[hoh])
        T.dma("act", self.HF[d].t[1024:2048, c0:c0 + 128].rearrange("(k p) n -> p k n", p=128), hoh.t[:], reads=[hoh], writes=[self.HF[d].k((1, c0 // 128))])


_W_NAMES = ["mod_w", "mod_b", "norm_w", "final_norm_w", "mlp_w1", "mlp_w2", "even_w_in", "even_w_out",
            "ssd_conv_w", "ssd_conv_b", "ssd_a_log", "ssd_dt_bias", "ssd_d", "ssd_norm_w", "hgrn_lb", "hgrn_norm_w",
            "odd_w_in", "odd_w_out", "mlstm_conv_w", "mlstm_conv_b", "mlstm_gate_b", "mlstm_norm_w"]


def make_in_maps(inputs, cores):
    f = lambda a: np.ascontiguousarray(np.asarray(a, dtype=np.float32))
    shared = {k: f(inputs[k]) for k in _W_NAMES}
    maps = []
    for ci in cores:
        m = dict(shared)
        m["x"] = f(inputs["x"][2 * ci:2 * ci + 2])
        m["ctx"] = f(inputs["ctx"][2 * ci:2 * ci + 2])
        m["cvec"] = f(np.stack([inputs["c"][2 * ci], inputs["c"][2 * ci + 1], inputs["c_ctx"]], axis=0))
        maps.append(m)
    return maps


def kernel(**inputs):
    prog = Prog()
    nc = prog.build()
    maps = make_in_maps(inputs, list(range(8)))
    res = run_bass_kernel_spmd(nc, maps, core_ids=list(range(8)))
    return np.concatenate([np.asarray(r["out"], dtype=np.float32) for r in res.results], axis=0)
```

```python
import contextlib
import math
import numpy as np
import concourse.bass as bass
import concourse.mybir as mybir
from concourse.bass_utils import run_bass_kernel_spmd

F32 = mybir.dt.float32
BF16 = mybir.dt.bfloat16
ALU = mybir.AluOpType
AF = mybir.ActivationFunctionType

D = 2048
NTOK = 4608
NBLK = 9
NCH = 36
DEPTH = 4
EVEN_IN = 7712
ODD_IN = 6160
EPS = 1e-6
BIG = 30000.0


class Res:
    __slots__ = ("name", "w", "r")

    def __init__(self, name):
        self.name = name
        self.w = None
        self.r = []


class Buf:
    def __init__(self, t, name):
        self.t = t
        self.name = name
        self.r = Res(name)
        self.subs = {}

    def k(self, key):
        s = self.subs.get(key)
        if s is None:
            s = Res(f"{self.name}.{key}")
            self.subs[key] = s
        return s


def _res(x):
    return x.r if isinstance(x, Buf) else x


class Trk:
    EPOCH = 60000

    def __init__(self, nc, es, n_dma_sems=(10, 6, 10)):
        self.nc = nc
        self.es = es
        self.engs = {"pe": nc.tensor, "act": nc.scalar, "dve": nc.vector, "pool": nc.gpsimd, "sp": nc.sync}
        self.sems = {}
        self.cnt = {}
        self.waited = {k: {} for k in self.engs}
        self.epoch = {}
        self.last = {}
        for k in ("pe", "act", "dve", "pool"):
            self.epoch[k] = 0
            self._new_sem(f"e_{k}0")
        self.dq = {}
        self.gen = 0
        for q, n in zip(("sp", "act", "pool"), n_dma_sems):
            keys = []
            for i in range(n):
                key = f"d_{q}{i}"
                self._new_sem(key)
                keys.append(key)
            self.dq[q] = {"keys": keys, "i": 0}
        self.n_instr = 0
        self.n_wait = 0

    def _new_sem(self, key):
        self.sems[key] = self.es.enter_context(self.nc.semaphore(key))
        self.cnt[key] = 0

    def _eng_key(self, k):
        key = f"e_{k}{self.epoch[k]}"
        if self.cnt[key] >= self.EPOCH:
            self.epoch[k] += 1
            key = f"e_{k}{self.epoch[k]}"
            self._new_sem(key)
        return key

    def _wait(self, ek, tok):
        if tok is None:
            return
        key, val = tok
        if self.waited[ek].get(key, 0) >= val:
            return
        self.engs[ek].wait_ge(self.sems[key], val)
        self.waited[ek][key] = val
        self.n_wait += 1

    def _deps(self, ek, reads, writes):
        toks = []
        for r in reads:
            r = _res(r)
            if r.w is not None:
                toks.append((r.w, True))
        for w in writes:
            w = _res(w)
            if w.w is not None:
                toks.append((w.w, True))
            for t in w.r:
                toks.append((t, False))
        for tok, strong in toks:
            if tok[0].startswith("e_" + ek):
                if ek == "pe" or not strong:
                    continue
            self._wait(ek, tok)

    def _commit(self, tok, reads, writes):
        for r in reads:
            r = _res(r)
            r.r.append(tok)
            if len(r.r) > 48:
                best = {}
                for k, v in r.r:
                    if best.get(k, 0) < v:
                        best[k] = v
                r.r = list(best.items())
        for w in writes:
            w = _res(w)
            w.w = tok
            w.r = []

    def op(self, ek, fn, reads=(), writes=()):
        return self.group(ek, [fn], reads, writes)

    def group(self, ek, fns, reads=(), writes=()):
        key = self._eng_key(ek)
        self._deps(ek, reads, writes)
        ins = None
        for fn in fns:
            ins = fn(self.engs[ek])
            self.n_instr += 1
        ins.then_inc(self.sems[key], 1)
        self.cnt[key] += 1
        tok = (key, self.cnt[key])
        self.last[ek] = tok
        self._commit(tok, reads, writes)
        return tok

    def dma(self, q, out, in_, reads=(), writes=(), **kw):
        d = self.dq[q]
        slot = d["i"] % len(d["keys"])
        d["i"] += 1
        key = d["keys"][slot]
        if self.cnt[key] + 16 > self.EPOCH:
            self.gen += 1
            key = f"d_{q}{slot}_{self.gen}"
            self._new_sem(key)
            d["keys"][slot] = key
        if self.cnt[key] > 0:
            self._wait(q, (key, self.cnt[key]))
        self._deps(q, reads, writes)
        ins = self.engs[q].dma_start(out=out, in_=in_, **kw)
        self.n_instr += 1
        ins.then_inc(self.sems[key], 16)
        self.cnt[key] += 16
        tok = (key, self.cnt[key])
        self._commit(tok, reads, writes)
        return tok

    def all_tokens(self):
        toks = [t for t in self.last.values()]
        for q in self.dq.values():
            for key in q["keys"]:
                if self.cnt[key]:
                    toks.append((key, self.cnt[key]))
        for key, c in self.cnt.items():
            if key.startswith("d_") and c and (key, c) not in toks:
                toks.append((key, c))
        return toks

    def barrier(self, engines=("pe", "act", "dve", "pool", "sp")):
        toks = self.all_tokens()
        for ek in engines:
            for t in toks:
                self._wait(ek, t)


def blk_cols(tb):
    return tb * 512


def chunk_col(b, kind, j):
    if kind == "lat":
        return b * 2048 + j * 128
    return 4096 + b * 256 + j * 128


def chain_chunks(b, d):
    ctx = [("ctx", 0), ("ctx", 1)]
    lat = [("lat", j) for j in range(16)]
    if d == 0:
        seq = ctx + lat
    else:
        seq = ctx[::-1] + lat[::-1]
    return [chunk_col(b, k, j) for k, j in seq]


class Prog:
    def __init__(self, n_layers=DEPTH, debug=False):
        self.n_layers = n_layers
        self.debug = debug
        self.nc = bass.Bass("TRN2", target_bir_lowering=False)
        self.uid = 0

    def kind(self, layer):
        f = getattr(self, "force_kind", None)
        if f is not None:
            return f
        return (layer % 2 == 1, layer // 2)

    def sb(self, es, name, shape, dt):
        self.uid += 1
        nm = f"{name}_{self.uid}"
        return Buf(es.enter_context(self.nc.sbuf_tensor(nm, list(shape), dt)), nm)

    def ps(self, es, name, shape, dt=F32):
        self.uid += 1
        nm = f"{name}_{self.uid}"
        return Buf(es.enter_context(self.nc.psum_tensor(nm, list(shape), dt)), nm)

    def dram(self, name, shape, dt, kind="Internal"):
        if self.debug and kind == "Internal" and name in self.debug_outs:
            kind = "ExternalOutput"
        t = self.nc.dram_tensor(name, list(shape), dt, kind=kind)
        return Buf(t.ap(), name)

    def build(self, debug_outs=()):
        self.debug_outs = set(debug_outs)
        nc = self.nc
        I = {}
        def inp(name, shape):
            I[name] = nc.dram_tensor(name, list(shape), F32, kind="ExternalInput").ap()
        inp("x", [2, 2048, D]); inp("ctx", [2, 256, D]); inp("cvec", [3, D])
        inp("mod_w", [DEPTH, D, 6 * D]); inp("mod_b", [DEPTH, 6 * D]); inp("norm_w", [DEPTH, 2, D])
        inp("final_norm_w", [D]); inp("mlp_w1", [DEPTH, D, 4 * D]); inp("mlp_w2", [DEPTH, 4 * D, D])
        inp("even_w_in", [2, D, EVEN_IN]); inp("even_w_out", [2, D, D])
        inp("ssd_conv_w", [2, 3, 1536]); inp("ssd_conv_b", [2, 1536]); inp("ssd_a_log", [2, 2, 16])
        inp("ssd_dt_bias", [2, 2, 16]); inp("ssd_d", [2, 16]); inp("ssd_norm_w", [2, 1024])
        inp("hgrn_lb", [2, 1024]); inp("hgrn_norm_w", [2, 1024])
        inp("odd_w_in", [2, D, ODD_IN]); inp("odd_w_out", [2, D, D])
        inp("mlstm_conv_w", [2, 3, 2048]); inp("mlstm_conv_b", [2, 2048]); inp("mlstm_gate_b", [2, 4, 4])
        inp("mlstm_norm_w", [2, 2048])
        self.I = I
        self.out = nc.dram_tensor("out", [2, 2048, D], F32, kind="ExternalOutput").ap()
        self.r_out = Res("out")

        self.XT = self.dram("XT", [D, NTOK], F32)
        self.U_FM = self.dram("U_FM", [6656, NTOK], BF16)
        self.U_TM = self.dram("U_TM", [NTOK, 2048], BF16)
        self.U_G = self.dram("U_G", [NTOK, 32], F32)
        self.HF = [self.dram("HF0", [D, NTOK], BF16), self.dram("HF1", [D, NTOK], BF16)]

        with contextlib.ExitStack() as es:
            self.T = Trk(nc, es)
            with contextlib.ExitStack() as ces:
                self.consts(ces)
                with contextlib.ExitStack() as pes:
                    self.prologue(pes)
                self.T.barrier()
                stop = getattr(self, "stop_after", None)
                if self.debug:
                    dbg = self.nc.dram_tensor("DBG_MOD", [128, DEPTH * 288], F32, kind="ExternalOutput").ap()
                    self.T.dma("sp", dbg, self.c["MOD"].t[:], reads=[self.c["MOD"]])
                    dbg2 = self.nc.dram_tensor("DBG_S", [128, DEPTH * 96], F32, kind="ExternalOutput").ap()
                    self.T.dma("sp", dbg2, self.c["S"].t[:], reads=[self.c["S"]])
                stopped = stop == ("prologue",)
                for layer in range(self.n_layers):
                    if stopped:
                        break
                    with contextlib.ExitStack() as des:
                        self.dense_phase(des, layer)
                    self.T.barrier()
                    if stop == ("dense", layer):
                        stopped = True
                        break
                    with contextlib.ExitStack() as mes:
                        self.mixer_phase(mes, layer)
                    self.T.barrier()
                    if stop == ("mixer", layer):
                        stopped = True
                        break
                if not stopped:
                    with contextlib.ExitStack() as des:
                        self.dense_phase(des, self.n_layers)
                self.T.barrier()
        return nc

    def consts(self, es):
        T = self.T
        c = self.c = {}

        def mk(name, shape, dt):
            c[name] = self.sb(es, name, shape, dt)
            return c[name]
        self.P = [self.ps(es, f"P{i}", [128, 512], F32) for i in range(8)]
        idf = mk("ident_f", [128, 128], F32)
        T.op("pool", lambda e: e.memset(idf.t[:], 1.0), writes=[idf])
        T.op("pool", lambda e: e.affine_select(idf.t[:], idf.t[:], pattern=[[-1, 128]], compare_op=ALU.is_equal,
                                               fill=0.0, base=0, channel_multiplier=1), reads=[idf], writes=[idf])
        idb = mk("ident_b", [128, 128], BF16)
        T.op("dve", lambda e: e.tensor_copy(idb.t[:], idf.t[:]), reads=[idf], writes=[idb])
        of = mk("ones_f", [128, 128], F32)
        T.op("pool", lambda e: e.memset(of.t[:], 1.0), writes=[of])
        ob = mk("ones_b", [128, 128], BF16)
        T.op("pool", lambda e: e.memset(ob.t[:], 1.0), writes=[ob])
        for d in range(2):
            sgn = 1 if d == 0 else -1
            tri = mk(f"tri{d}", [128, 128], F32)
            T.op("pool", lambda e: e.memset(tri.t[:], 1.0), writes=[tri])
            T.op("pool", lambda e: e.affine_select(tri.t[:], tri.t[:], pattern=[[sgn, 128]], compare_op=ALU.is_ge,
                                                   fill=0.0, base=0, channel_multiplier=-sgn), reads=[tri], writes=[tri])
            msk = mk(f"msk{d}", [128, 128], F32)
            T.op("dve", lambda e: e.tensor_scalar(msk.t[:], tri.t[:], -BIG, BIG, ALU.mult, ALU.add), reads=[tri], writes=[msk])
        mk("MOD", [128, DEPTH * 6 * 16 * 3], F32)
        mk("S", [128, DEPTH * 2 * 16 * 3], F32)
        mk("MODB", [128, DEPTH * 96], F32)
        mk("NW", [128, 128], F32)
        mk("FNW", [128, 16], F32)
        mk("SCW", [128, 72], F32); mk("SCB", [128, 24], F32)
        mk("MCW", [128, 96], F32); mk("MCB", [128, 32], F32)
        mk("SNW", [128, 16], F32); mk("LBR", [128, 16], F32); mk("HNW", [128, 16], F32); mk("MNW", [128, 32], F32)
        mk("LB", [128, 16], F32); mk("LB1", [128, 16], F32)
        mk("DSK", [128, 16], F32)
        mk("GB", [128, 32], F32)
        mk("DTB", [128, 64], F32)
        mk("EA", [128, 64], F32)
        mk("scT", [128, 16 * 3], BF16)

    def MODv(self, l, m, dc, v):
        i = ((l * 6 + m) * 16 + dc) * 3 + v
        return self.c["MOD"].t[:, i:i + 1]

    def Sv(self, l, i, dc, v):
        j = ((l * 2 + i) * 16 + dc) * 3 + v
        return self.c["S"].t[:, j:j + 1]

    def load_vecT(self, es, dst, src2d, k):
        T = self.T
        st = self.sb(es, "vst", [k, 128], F32)
        T.dma("sp", st.t[:], src2d, writes=[st])
        P = self.P[7]
        T.op("pe", lambda e: e.transpose(P.t[:, 0:k], st.t[:], self.c["ident_f"].t[0:k, 0:k]),
             reads=[st, self.c["ident_f"]], writes=[P])
        T.op("dve", lambda e: e.tensor_copy(dst.t[:, 0:k], P.t[:, 0:k]), reads=[P], writes=[dst])

    def prologue(self, es):
        T, c, I, nc = self.T, self.c, self.I, self.nc
        v128 = lambda ap, pat, **kw: ap.rearrange(pat, **kw)
        self.load_vecT(es, c["NW"], I["norm_w"].rearrange("l i (c p) -> (l i c) p", p=128), 128)
        self.load_vecT(es, c["FNW"], I["final_norm_w"].rearrange("(c p) -> c p", p=128), 16)
        self.load_vecT(es, c["SCW"], I["ssd_conv_w"].rearrange("e j (c p) -> (e j c) p", p=128), 72)
        self.load_vecT(es, c["SCB"], I["ssd_conv_b"].rearrange("e (c p) -> (e c) p", p=128), 24)
        self.load_vecT(es, c["MCW"], I["mlstm_conv_w"].rearrange("e j (c p) -> (e j c) p", p=128), 96)
        self.load_vecT(es, c["MCB"], I["mlstm_conv_b"].rearrange("e (c p) -> (e c) p", p=128), 32)
        self.load_vecT(es, c["SNW"], I["ssd_norm_w"].rearrange("e (c p) -> (e c) p", p=128), 16)
        self.load_vecT(es, c["LBR"], I["hgrn_lb"].rearrange("e (c p) -> (e c) p", p=128), 16)
        self.load_vecT(es, c["HNW"], I["hgrn_norm_w"].rearrange("e (c p) -> (e c) p", p=128), 16)
        self.load_vecT(es, c["MNW"], I["mlstm_norm_w"].rearrange("e (c p) -> (e c) p", p=128), 32)
        for l in range(DEPTH):
            tmp = self.sb(es, "mbt", [128, 96], F32)
            self.load_vecT(es, tmp, I["mod_b"][l].rearrange("(c p) -> c p", p=128), 96)
            T.op("dve", lambda e: e.tensor_copy(c["MODB"].t[:, l * 96:(l + 1) * 96], tmp.t[:]), reads=[tmp], writes=[c["MODB"]])
        LB, LB1, LBR = c["LB"], c["LB1"], c["LBR"]
        T.op("pool", lambda e: e.memset(LB.t[:, 0:8], 0.0), writes=[LB])
        T.op("dve", lambda e: e.tensor_tensor(LB.t[:, 8:16], LBR.t[:, 8:16], LBR.t[:, 0:8], ALU.subtract), reads=[LBR, LB], writes=[LB])
        T.op("act", lambda e: e.activation(LB.t[:, 8:16], LB.t[:, 8:16], AF.Sigmoid), reads=[LB], writes=[LB])
        T.op("dve", lambda e: e.tensor_scalar(LB1.t[:], LB.t[:], -1.0, 1.0, ALU.mult, ALU.add), reads=[LB], writes=[LB1])
        T.dma("sp", c["GB"].t[:], I["mlstm_gate_b"].rearrange("o a b -> (o a b)").partition_broadcast(128), writes=[c["GB"]])
        T.dma("sp", c["DTB"].t[:], I["ssd_dt_bias"].rearrange("e d h -> (e d h)").partition_broadcast(128), writes=[c["DTB"]])
        T.dma("sp", c["EA"].t[:], I["ssd_a_log"].rearrange("e d h -> (e d h)").partition_broadcast(128), writes=[c["EA"]])
        T.op("act", lambda e: e.activation(c["EA"].t[:], c["EA"].t[:], AF.Exp), reads=[c["EA"]], writes=[c["EA"]])
        for o in range(2):
            T.op("dve", lambda e: e.tensor_scalar_add(c["GB"].t[:, o * 16:o * 16 + 8], c["GB"].t[:, o * 16:o * 16 + 8], -math.log(16.0)),
                 reads=[c["GB"]], writes=[c["GB"]])
        for e_ in range(2):
            for h in range(16):
                T.dma("sp", c["DSK"].t[(h % 2) * 64:(h % 2) * 64 + 64, e_ * 8 + h // 2:e_ * 8 + h // 2 + 1],
                      I["ssd_d"][e_, h:h + 1].partition_broadcast(64), writes=[c["DSK"]])
        cT = self.sb(es, "cT", [128, 48], F32)
        self.load_vecT(es, cT, I["cvec"].rearrange("v (c p) -> (v c) p", p=128), 48)
        scv = c["scT"].t[:].rearrange("p (k v) -> p k v", v=3)
        for v in range(3):
            T.op("act", lambda e: e.activation(scv[:, :, v], cT.t[:, v * 16:(v + 1) * 16], AF.Silu), reads=[cT], writes=[c["scT"]])
        wb = [self.sb(es, f"mw{i}", [128, 16, 512], BF16) for i in range(3)]
        P = self.P
        n = 0
        for l in range(DEPTH):
            Pm = P[l % 2]
            for jb in range(24):
                w = wb[n % 3]; n += 1
                T.dma("pool", w.t[:], I["mod_w"][l][:, jb * 512:(jb + 1) * 512].rearrange("(k p) n -> p k n", p=128), writes=[w])
                for jt in range(4):
                    j = jb * 4 + jt
                    T.group("pe", [
                        (lambda e, kc=kc: e.matmul(Pm.t[:, j * 3:j * 3 + 3], w.t[:, kc, jt * 128:(jt + 1) * 128], scv[:, kc, :],
                                                   start=(kc == 0), stop=(kc == 15))) for kc in range(16)],
                        reads=[w, c["scT"]], writes=[Pm])
            MODl = c["MOD"].t[:, l * 288:(l + 1) * 288].rearrange("p (j v) -> p j v", v=3)
            T.op("dve", lambda e: e.tensor_tensor(MODl, Pm.t[:, 0:288].rearrange("p (j v) -> p j v", v=3),
                                                  c["MODB"].t[:, l * 96:(l + 1) * 96].unsqueeze(2).to_broadcast([128, 96, 3]), ALU.add),
                 reads=[Pm, c["MODB"]], writes=[c["MOD"]])
            for i in range(2):
                Sl = c["S"].t[:, (l * 2 + i) * 48:(l * 2 + i + 1) * 48].rearrange("p (k v) -> p k v", v=3)
                sc = c["MOD"].t[:, ((l * 6 + 1 + 3 * i) * 16) * 3:((l * 6 + 2 + 3 * i) * 16) * 3].rearrange("p (k v) -> p k v", v=3)
                nw = c["NW"].t[:, (l * 2 + i) * 16:(l * 2 + i + 1) * 16].unsqueeze(2).to_broadcast([128, 16, 3])
                T.op("dve", lambda e: e.tensor_scalar_add(Sl, sc, 1.0), reads=[c["MOD"]], writes=[c["S"]])
                T.op("dve", lambda e: e.tensor_tensor(Sl, Sl, nw, ALU.mult), reads=[c["S"], c["NW"]], writes=[c["S"]])

    def dense_phase(self, es, layer):
        T, c, I, P = self.T, self.c, self.I, self.P
        L = self.n_layers
        self.xT = self.sb(es, "xT", [128, 16, 512], F32)
        self.actT = self.sb(es, "actT", [128, 16, 512], BF16)
        self.wb = [self.sb(es, f"wb{i}", [128, 16 * 512], BF16) for i in range(3)]
        self.wn = 0
        self.f32t = [self.sb(es, f"f32t{i}", [128, 512], F32) for i in range(4)]
        self.fn = 0
        self.rstd = self.sb(es, "rstd", [128, 512], F32)
        self.so = [self.sb(es, f"so{i}", [128, 4, 512], BF16) for i in range(2)]
        self.son = 0
        self.pn = 0
        if layer > 0:
            self.aT = self.sb(es, "aT", [128, 64, 512], BF16)
            self.hA = self.sb(es, "hA", [128, 4, 512], BF16)
            self.hB = self.sb(es, "hB", [128, 4, 512], BF16)
            self.hC = self.so[0]
            self.hD = self.so[1]
            base = self.sb(es, "hs", [128, 2048], F32)
            self.hs = Buf(base.t[:].rearrange("p (a b) -> p a b", b=512), base.name)
            self.hs.r = base.r
            self.xin = base
        if layer == 0:
            self.xin = self.sb(es, "xin", [128, 2048], F32)
        for tb in range(NBLK):
            if layer == L and tb == 8:
                continue
            v = 2 if tb == 8 else tb // 4
            c0 = tb * 512
            xT = self.xT
            if layer == 0:
                self.load_x_input(tb)
            else:
                T.dma("sp", xT.t[:], self.XT.t[:, c0:c0 + 512].rearrange("(k p) n -> p k n", p=128),
                      reads=[self.XT.k(tb)], writes=[xT])
                self.stage_c(layer - 1, tb, v)
            if layer < L:
                self.norm_mod(layer, 0, v)
                self.stage_a(layer, tb)
                T.dma("sp", self.XT.t[:, c0:c0 + 512].rearrange("(k p) n -> p k n", p=128), xT.t[:],
                      reads=[xT], writes=[self.XT.k(tb)])
            else:
                self.final_out(tb)

    def nextP(self):
        p = self.P[self.pn % 4]
        self.pn += 1
        return p

    def nextF(self):
        f = self.f32t[self.fn % 4]
        self.fn += 1
        return f

    def load_x_input(self, tb):
        T, I, P = self.T, self.I, self.P
        idf = self.c["ident_f"]
        for tt in range(4):
            if tb < 8:
                src = I["x"][tb // 4, (tb % 4) * 512 + tt * 128:(tb % 4) * 512 + (tt + 1) * 128, :]
            else:
                src = I["ctx"][tt // 2, (tt % 2) * 128:(tt % 2 + 1) * 128, :]
            T.dma("sp", self.xin.t[:], src, writes=[self.xin])
            for g in range(4):
                Pg = self.nextP()
                T.group("pe", [(lambda e, j=j: e.transpose(Pg.t[:, j * 128:(j + 1) * 128],
                                                           self.xin.t[:, (g * 4 + j) * 128:(g * 4 + j + 1) * 128], idf.t[:])) for j in range(4)],
                        reads=[self.xin, idf], writes=[Pg])
                T.op("act" if g % 2 else "dve",
                     (lambda e: e.activation(self.xT.t[:, g * 4:(g + 1) * 4, tt * 128:(tt + 1) * 128],
                                             Pg.t[:].rearrange("p (j t) -> p j t", t=128), AF.Copy)) if g % 2 else
                     (lambda e: e.tensor_copy(self.xT.t[:, g * 4:(g + 1) * 4, tt * 128:(tt + 1) * 128],
                                              Pg.t[:].rearrange("p (j t) -> p j t", t=128))),
                     reads=[Pg], writes=[self.xT])

    def rms_rstd(self, src_fn, nchunks, nfeat, reads):
        T, P = self.T, self.P
        Pss = P[4]
        of = self.c["ones_f"]
        for j in range(nchunks):
            sq = self.nextF()
            T.op("act", lambda e: e.activation(sq.t[:], src_fn(j), AF.Square), reads=reads, writes=[sq])
            T.op("pe", lambda e: e.matmul(Pss.t[:], of.t[:], sq.t[:], start=(j == 0), stop=(j == nchunks - 1)),
                 reads=[sq, of], writes=[Pss])
        T.op("act", lambda e: e.activation(self.rstd.t[:], Pss.t[:], AF.Ln, bias=EPS, scale=1.0 / nfeat), reads=[Pss], writes=[self.rstd])
        T.op("act", lambda e: e.activation(self.rstd.t[:], self.rstd.t[:], AF.Exp, scale=-0.5), reads=[self.rstd], writes=[self.rstd])

    def norm_mod(self, l, i, v):
        T = self.T
        xT, actT = self.xT, self.actT
        self.rms_rstd(lambda j: xT.t[:, j, :], 16, D, [xT])
        for dc in range(16):
            tmp = self.nextF()
            T.op("dve", lambda e: e.tensor_tensor(tmp.t[:], xT.t[:, dc, :], self.rstd.t[:], ALU.mult), reads=[xT, self.rstd], writes=[tmp])
            T.op("act", lambda e: e.activation(actT.t[:, dc, :], tmp.t[:], AF.Identity, bias=self.MODv(l, 3 * i, dc, v), scale=self.Sv(l, i, dc, v)),
                 reads=[tmp, self.c["MOD"], self.c["S"]], writes=[actT])

    def load_w(self, src, wc):
        w = self.wb[self.wn % 3]
        self.wn += 1
        k = src.shape[0] // 128
        view = w.t[:, 0:k * wc].rearrange("p (k n) -> p k n", n=wc)
        self.T.dma("pool", view, src.rearrange("(k p) n -> p k n", p=128), writes=[w])
        return w, view

    def proj_fm(self, wsrc, col0, ncols, evac, nk=16, rhs_fn=None):
        T = self.T
        if rhs_fn is None:
            rhs_fn = lambda kc: self.actT.t[:, kc, :]
            rd = [self.actT]
        else:
            rd = [self.aT]
        wcb = 512 if nk == 16 else 128
        ct = 0
        for b0 in range(0, ncols, wcb):
            wc = min(wcb, ncols - b0)
            w, view = self.load_w(wsrc[:, col0 + b0:col0 + b0 + wc], wc)
            for t0 in range(0, wc, 128):
                Pt = self.nextP()
                T.group("pe", [(lambda e, kc=kc: e.matmul(Pt.t[:], view[:, kc, t0:t0 + 128], rhs_fn(kc), start=(kc == 0), stop=(kc == nk - 1)))
                               for kc in range(nk)], reads=[w] + rd, writes=[Pt])
                evac(ct, Pt)
                ct += 1

    def proj_tm(self, wsrc, col0, ncols, evac):
        T = self.T
        for b0 in range(0, ncols, 512):
            wc = min(512, ncols - b0)
            w, view = self.load_w(wsrc[:, col0 + b0:col0 + b0 + wc], wc)
            for tt in range(4):
                Pt = self.nextP()
                T.group("pe", [(lambda e, kc=kc: e.matmul(Pt.t[:, 0:wc], self.actT.t[:, kc, tt * 128:(tt + 1) * 128], view[:, kc, :],
                                                          start=(kc == 0), stop=(kc == 15))) for kc in range(16)],
                        reads=[w, self.actT], writes=[Pt])
                evac(b0, wc, tt, Pt)

    class FmOut:
        def __init__(self, prog, row0, tb):
            self.p = prog; self.row0 = row0; self.tb = tb; self.n = 0; self.cur = None

        def slot(self):
            if self.n % 4 == 0:
                self.cur = self.p.so[self.p.son % 2]
                self.p.son += 1
            j = self.n % 4
            self.n += 1
            return self.cur, self.cur.t[:, j, :]

        def done_tile(self, last=False):
            if self.n % 4 == 0 or last:
                cnt = (self.n - 1) % 4 + 1
                r0 = self.row0 + (self.n - cnt) * 128
                p = self.p
                c0 = self.tb * 512
                p.T.dma("act", p.U_FM.t[r0:r0 + cnt * 128, c0:c0 + 512].rearrange("(t p) n -> p t n", p=128), self.cur.t[:, 0:cnt, :],
                        reads=[self.cur], writes=[p.U_FM.k((r0 // 128, self.tb))])

    def fm_group(self, wsrc, col0, ncols, row0, tb, evac_to):
        out = Prog.FmOut(self, row0, tb)
        ntile = ncols // 128

        def ev(ct, Pt):
            buf, dst = out.slot()
            evac_to(ct, Pt, buf, dst)
            out.done_tile(last=(ct == ntile - 1))
        self.proj_fm(wsrc, col0, ncols, ev)

    def conv_evac(self, tb, CW, CB, base_w, base_b, nchan_chunks):
        T = self.T
        seg = 256 if tb == 8 else 64

        def ev(ct, Pt, buf, dst):
            cv = self.nextF()
            w0 = CW.t[:, base_w + ct:base_w + ct + 1]
            w1 = CW.t[:, base_w + nchan_chunks + ct:base_w + nchan_chunks + ct + 1]
            w2 = CW.t[:, base_w + 2 * nchan_chunks + ct:base_w + 2 * nchan_chunks + ct + 1]
            bb = CB.t[:, base_b + ct:base_b + ct + 1]
            T.op("act", lambda e: e.activation(cv.t[:], Pt.t[:], AF.Identity, bias=bb, scale=w1), reads=[Pt, CW, CB], writes=[cv])
            cvv = cv.t[:].rearrange("p (s l) -> p s l", l=seg)
            pv = Pt.t[:].rearrange("p (s l) -> p s l", l=seg)
            T.op("dve", lambda e: e.scalar_tensor_tensor(cvv[:, :, 1:seg], pv[:, :, 0:seg - 1], w0, cvv[:, :, 1:seg], ALU.mult, ALU.add),
                 reads=[Pt, cv, CW], writes=[cv])
            T.op("dve", lambda e: e.scalar_tensor_tensor(cvv[:, :, 0:seg - 1], pv[:, :, 1:seg], w2, cvv[:, :, 0:seg - 1], ALU.mult, ALU.add),
                 reads=[Pt, cv, CW], writes=[cv])
            T.op("act", lambda e: e.activation(dst, cv.t[:], AF.Silu), reads=[cv], writes=[buf])
        return ev

    def act_evac(self, func):
        T = self.T

        def ev(ct, Pt, buf, dst):
            T.op("act", lambda e: e.activation(dst, Pt.t[:], func), reads=[Pt], writes=[buf])
        return ev

    def copy_evac(self):
        T = self.T

        def ev(ct, Pt, buf, dst):
            T.op("dve", lambda e: e.tensor_copy(dst, Pt.t[:]), reads=[Pt], writes=[buf])
        return ev

    def tm_evac(self, tb, dst_buf, col_off, dt):
        T = self.T
        c0 = tb * 512

        def ev(b0, wc, tt, Pt):
            if dt == BF16:
                st = self.so[self.son % 2]; self.son += 1
                sv = st.t[:, 0, 0:wc]
            else:
                st = self.nextF()
                sv = st.t[:, 0:wc]
            T.op("dve", lambda e: e.tensor_copy(sv, Pt.t[:, 0:wc]), reads=[Pt], writes=[st])
            r0 = c0 + tt * 128
            T.dma("act", dst_buf.t[r0:r0 + 128, col_off + b0:col_off + b0 + wc], sv, reads=[st],
                  writes=[dst_buf.k((r0 // 128, (col_off + b0) // 512))])
        return ev

    def stage_a(self, layer, tb):
        I, c = self.I, self.c
        odd, idx = self.kind(layer)
        if not odd:
            e_ = idx
            W = I["even_w_in"][e_]
            self.fm_group(W, 0, 1024, 0, tb, self.act_evac(AF.Silu))
            self.fm_group(W, 1024, 1536, 1024, tb, self.conv_evac(tb, c["SCW"], c["SCB"], e_ * 36, e_ * 12, 12))
            self.fm_group(W, 2592, 1024, 2560, tb, self.act_evac(AF.Silu))
            self.fm_group(W, 3616, 2048, 3584, tb, self.copy_evac())
            self.fm_group(W, 6688, 1024, 5632, tb, self.act_evac(AF.Silu))
            self.proj_tm(W, 5664, 1024, self.tm_evac(tb, self.U_TM, 0, BF16))
            self.proj_tm(W, 2560, 32, self.tm_evac(tb, self.U_G, 0, F32))
        else:
            o_ = idx
            W = I["odd_w_in"][o_]
            self.fm_group(W, 0, 2048, 0, tb, self.conv_evac(tb, c["MCW"], c["MCB"], o_ * 48, o_ * 16, 16))
            self.fm_group(W, 4096, 2048, 2048, tb, self.act_evac(AF.Sigmoid))
            self.proj_tm(W, 2048, 2048, self.tm_evac(tb, self.U_TM, 0, BF16))
            self.proj_tm(W, 6144, 16, self.tm_evac(tb, self.U_G, 0, F32))

    def xk(self, lst=None):
        return [self.xT.k(i) for i in (range(16) if lst is None else lst)]

    def stage_c(self, l, tb, v):
        T, c, I = self.T, self.c, self.I
        xT, aT = self.xT, self.aT
        self.finalize_mixer(l, tb)
        odd, idx = self.kind(l)
        W = I["odd_w_out"][idx] if odd else I["even_w_out"][idx]

        def ev_res(m):
            def ev(ct, Pt):
                T.op("dve", lambda e: e.scalar_tensor_tensor(xT.t[:, ct, :], Pt.t[:], self.MODv(l, m, ct, v), xT.t[:, ct, :], ALU.mult, ALU.add),
                     reads=[Pt, c["MOD"], xT], writes=[xT])
            return ev
        self.proj_fm(W, 0, 2048, ev_res(2))
        self.norm_mod(l, 1, v)

        def ev1(ct, Pt):
            tmp = self.nextF()
            T.op("act", lambda e: e.activation(tmp.t[:], Pt.t[:], AF.Relu), reads=[Pt], writes=[tmp])
            T.op("dve", lambda e: e.tensor_tensor(aT.t[:, ct, :], tmp.t[:], tmp.t[:], ALU.mult), reads=[tmp], writes=[aT])
        self.proj_fm(I["mlp_w1"][l], 0, 8192, ev1)
        self.proj_fm(I["mlp_w2"][l], 0, 2048, ev_res(5), nk=64, rhs_fn=lambda kc: aT.t[:, kc, :])

    def ld_blk(self, dst, src_buf, r0, tb):
        c0 = tb * 512
        self.T.dma("sp", dst.t[:], src_buf.t[r0:r0 + 512, c0:c0 + 512].rearrange("(k p) n -> p k n", p=128), writes=[dst])

    def finalize_mixer(self, l, tb):
        T, c = self.T, self.c
        hA, hB, hC, hD, hs, actT = self.hA, self.hB, self.hC, self.hD, self.hs, self.actT
        odd, idx = self.kind(l)
        for gI in range(4):
            r0 = gI * 512
            self.ld_blk(hA, self.HF[0], r0, tb)
            self.ld_blk(hB, self.HF[1], r0, tb)
            T.op("dve", lambda e: e.tensor_tensor(hs.t[:], hA.t[:], hB.t[:], ALU.add), reads=[hA, hB], writes=[hs])
            if odd:
                self.ld_blk(hC, self.U_FM, 2048 + r0, tb)
                self.rms_rstd(lambda j: hs.t[:, j, :], 4, 512, [hs])
                for j in range(4):
                    dc = gI * 4 + j
                    tmp = self.nextF()
                    T.op("dve", lambda e: e.tensor_tensor(tmp.t[:], hs.t[:, j, :], self.rstd.t[:], ALU.mult), reads=[hs, self.rstd], writes=[tmp])
                    T.op("dve", lambda e: e.scalar_tensor_tensor(actT.t[:, dc, :], tmp.t[:], c["MNW"].t[:, idx * 16 + dc:idx * 16 + dc + 1],
                                                                 hC.t[:, j, :], ALU.mult, ALU.mult), reads=[tmp, hC, c["MNW"]], writes=[actT])
            elif gI < 2:
                e_ = idx
                self.ld_blk(hC, self.U_FM, 1024 + r0, tb)
                self.ld_blk(hD, self.U_FM, r0, tb)
                for j in range(4):
                    dc = gI * 4 + j
                    T.op("dve", lambda e: e.scalar_tensor_tensor(hs.t[:, j, :], hC.t[:, j, :], c["DSK"].t[:, e_ * 8 + dc:e_ * 8 + dc + 1],
                                                                 hs.t[:, j, :], ALU.mult, ALU.add), reads=[hC, hs, c["DSK"]], writes=[hs])
                T.op("dve", lambda e: e.tensor_tensor(hs.t[:], hs.t[:], hD.t[:], ALU.mult), reads=[hs, hD], writes=[hs])
                self.rms_rstd(lambda j: hs.t[:, j, :], 4, 512, [hs])
                for j in range(4):
                    dc = gI * 4 + j
                    tmp = self.nextF()
                    T.op("dve", lambda e: e.tensor_tensor(tmp.t[:], hs.t[:, j, :], self.rstd.t[:], ALU.mult), reads=[hs, self.rstd], writes=[tmp])
                    T.op("act", lambda e: e.activation(actT.t[:, dc, :], tmp.t[:], AF.Copy, scale=c["SNW"].t[:, e_ * 8 + dc:e_ * 8 + dc + 1]),
                         reads=[tmp, c["SNW"]], writes=[actT])
            else:
                e_ = idx
                self.ld_blk(hD, self.U_FM, 5632 + (gI - 2) * 512, tb)
                for j in range(4):
                    dc = gI * 4 + j
                    self.rms_rstd(lambda _: hs.t[:, j, :], 1, 128, [hs])
                    tmp = self.nextF()
                    T.op("dve", lambda e: e.tensor_tensor(tmp.t[:], hs.t[:, j, :], self.rstd.t[:], ALU.mult), reads=[hs, self.rstd], writes=[tmp])
                    T.op("dve", lambda e: e.scalar_tensor_tensor(actT.t[:, dc, :], tmp.t[:], c["HNW"].t[:, e_ * 8 + dc - 8:e_ * 8 + dc - 7],
                                                                 hD.t[:, j, :], ALU.mult, ALU.mult), reads=[tmp, hD, c["HNW"]], writes=[actT])

    def final_out(self, tb):
        T, c = self.T, self.c
        xT = self.xT
        idf = c["ident_f"]
        self.rms_rstd(lambda j: xT.t[:, j, :], 16, D, [xT])
        for dc in range(16):
            tmp = self.nextF()
            T.op("dve", lambda e: e.tensor_tensor(tmp.t[:], xT.t[:, dc, :], self.rstd.t[:], ALU.mult), reads=[xT, self.rstd], writes=[tmp])
            T.op("act", lambda e: e.activation(xT.t[:, dc, :], tmp.t[:], AF.Copy, scale=c["FNW"].t[:, dc:dc + 1]), reads=[tmp, c["FNW"]], writes=[xT])
        for tt in range(4):
            for g in range(4):
                Pg = self.nextP()
                T.group("pe", [(lambda e, j=j: e.transpose(Pg.t[:, j * 128:(j + 1) * 128], xT.t[:, g * 4 + j, tt * 128:(tt + 1) * 128], idf.t[:]))
                               for j in range(4)], reads=[xT, idf], writes=[Pg])
                if g % 2:
                    T.op("act", lambda e: e.activation(self.xin.t[:, g * 512:(g + 1) * 512], Pg.t[:], AF.Copy), reads=[Pg], writes=[self.xin])
                else:
                    T.op("dve", lambda e: e.tensor_copy(self.xin.t[:, g * 512:(g + 1) * 512], Pg.t[:]), reads=[Pg], writes=[self.xin])
            r0 = (tb % 4) * 512 + tt * 128
            T.dma("sp", self.out[tb // 4, r0:r0 + 128, :], self.xin.t[:], reads=[self.xin], writes=[self.r_out])

    def mixer_phase(self, es, layer):
        self.dslot = 0
        odd, idx = self.kind(layer)
        self.decay_banks = not odd
        if odd:
            self.mlstm_phase(es, idx)
        else:
            self.even_phase(es, idx)

    def trps(self, i):
        return self.P[i].t[:].bitcast(BF16)

    def gate_cums(self, d, src, n, ws, pres=None):
        T, c, P = self.T, self.c, self.P
        Pg = P[7]
        pr = Pg if pres is None else pres
        T.op("pe", lambda e: e.matmul(Pg.t[:, 256:256 + n], c[f"tri{d}"].t[:], src.t[:, 0:n], start=True, stop=True),
             reads=[src, c[f"tri{d}"]], writes=[pr])
        T.op("pe", lambda e: e.matmul(Pg.t[:, 256 + n:256 + 2 * n], c["ones_f"].t[:], src.t[:, 0:n], start=True, stop=True),
             reads=[src, c["ones_f"]], writes=[pr])
        T.op("dve", lambda e: e.tensor_copy(ws["cs"].t[:, 0:2 * n], Pg.t[:, 256:256 + 2 * n]), reads=[pr], writes=[ws["cs"]])
        return ws["cs"]

    def decay_mats(self, d, src, h, bias_ap, ws, need_e=True):
        T, c, P = self.T, self.c, self.P
        slot = self.dslot % 2
        self.dslot += 1
        if getattr(self, "decay_banks", False):
            PA = P[1] if slot == 0 else P[0]
            pr = PA
            b0 = 0 if slot == 0 else 256
        else:
            PA = P[1]
            pr = PA.k(slot)
            b0 = slot * 256
        Dm, E = ws["Dm"][slot], ws["E"][slot]
        bc = src.t[:, h:h + 1].to_broadcast([128, 128])
        fns = [lambda e: e.matmul(PA.t[:, b0 + 128:b0 + 256], bc, c[f"tri{d}"].t[:], start=True, stop=False),
               lambda e: e.matmul(PA.t[:, b0 + 128:b0 + 256], c["ident_f"].t[:], c[f"msk{d}"].t[:], start=False, stop=True)]
        if need_e:
            fns.insert(0, lambda e: e.matmul(PA.t[:, b0:b0 + 128], bc, c[f"tri{d}"].t[:], start=True, stop=True))
        T.group("pe", fns, reads=[src, c[f"tri{d}"], c[f"msk{d}"], c["ident_f"]], writes=[pr])
        T.op("act", lambda e: e.activation(Dm.t[:], PA.t[:, b0 + 128:b0 + 256], AF.Exp, bias=bias_ap, scale=-1.0),
             reads=[pr, ws["bias"]], writes=[Dm])
        if need_e:
            T.op("act", lambda e: e.activation(E.t[:], PA.t[:, b0:b0 + 128], AF.Exp, scale=-1.0), reads=[pr], writes=[E])
        return Dm, E

    def mlstm_phase(self, es, o_):
        T, c, P = self.T, self.c, self.P
        chains = [(b, d) for b in range(2) for d in range(2)]
        Cf = {ch: self.sb(es, "Cf", [128, 4, 2, 640], F32) for ch in chains}
        Cb = {ch: self.sb(es, "Cb", [128, 4, 2, 640], BF16) for ch in chains}
        for ch in chains:
            T.op("pool", lambda e: e.memset(Cf[ch].t[:], 0.0), writes=[Cf[ch]])
            T.op("pool", lambda e: e.memset(Cb[ch].t[:], 0.0), writes=[Cb[ch]])
        sets = []
        for i in range(2):
            w = {}
            w["qT"] = self.sb(es, "qT", [128, 8, 128], BF16)
            w["kT"] = self.sb(es, "kT", [128, 8, 128], BF16)
            w["Vt"] = self.sb(es, "Vt", [128, 2048], BF16)
            w["g"] = self.sb(es, "g", [128, 16], F32)
            w["li"] = self.sb(es, "li", [128, 4], F32)
            w["sp"] = self.sb(es, "sp", [128, 4], F32)
            w["cs"] = self.sb(es, "cs", [128, 8], F32)
            w["bias"] = self.sb(es, "bias", [128, 4], F32)
            w["wsx"] = self.sb(es, "wsx", [128, 4], F32)
            w["cdec"] = self.sb(es, "cdec", [128, 4], F32)
            w["Dm"] = [self.sb(es, "Dm", [128, 128], F32) for _ in range(2)]
            w["E"] = [self.sb(es, "E", [128, 128], F32) for _ in range(2)]
            w["WT"] = [self.sb(es, "WT", [128, 128], BF16) for _ in range(2)]
            w["qs"] = [self.sb(es, "qs", [128, 2, 128], BF16) for _ in range(2)]
            w["ke"] = [self.sb(es, "ke", [128, 256], BF16) for _ in range(2)]
            w["dd"] = [self.sb(es, "dd", [128, 128], F32) for _ in range(2)]
            w["ho"] = self.sb(es, "ho", [128, 16, 128], BF16)
            sets.append(w)
        GB = c["GB"]
        n = 0
        self.hslot = 0
        for step in range(18):
            for ch in chains:
                b, d = ch
                c0 = chain_chunks(b, d)[step]
                w = sets[n % 2]; n += 1
                T.dma("sp", w["qT"].t[:], self.U_FM.t[0:1024, c0:c0 + 128].rearrange("(k p) n -> p k n", p=128), writes=[w["qT"]])
                T.dma("sp", w["kT"].t[:], self.U_FM.t[1024:2048, c0:c0 + 128].rearrange("(k p) n -> p k n", p=128), writes=[w["kT"]])
                T.dma("sp", w["Vt"].t[:], self.U_TM.t[c0:c0 + 128, 0:2048], writes=[w["Vt"]])
                T.dma("sp", w["g"].t[:], self.U_G.t[c0:c0 + 128, 0:16], writes=[w["g"]])
                g, li, sp = w["g"], w["li"], w["sp"]
                T.op("dve", lambda e: e.tensor_tensor(li.t[:], g.t[:, d * 4:d * 4 + 4], GB.t[:, o_ * 16 + d * 4:o_ * 16 + d * 4 + 4], ALU.add),
                     reads=[g, GB], writes=[li])
                T.op("dve", lambda e: e.tensor_tensor(sp.t[:], g.t[:, 8 + d * 4:12 + d * 4], GB.t[:, o_ * 16 + 8 + d * 4:o_ * 16 + 12 + d * 4], ALU.add),
                     reads=[g, GB], writes=[sp])
                T.op("act", lambda e: e.activation(sp.t[:], sp.t[:], AF.Exp, scale=-1.0), reads=[sp], writes=[sp])
                T.op("act", lambda e: e.activation(sp.t[:], sp.t[:], AF.Ln, bias=1.0), reads=[sp], writes=[sp])
                cs = self.gate_cums(d, sp, 4, w, pres=P[7].k("g"))
                bias, wsx, cdec = w["bias"], w["wsx"], w["cdec"]
                T.op("dve", lambda e: e.tensor_tensor(bias.t[:], li.t[:], cs.t[:, 0:4], ALU.add), reads=[li, cs], writes=[bias])
                T.op("dve", lambda e: e.tensor_tensor(wsx.t[:], bias.t[:], cs.t[:, 4:8], ALU.subtract), reads=[bias, cs], writes=[wsx])
                T.op("act", lambda e: e.activation(wsx.t[:], wsx.t[:], AF.Exp), reads=[wsx], writes=[wsx])
                T.op("act", lambda e: e.activation(cdec.t[:], cs.t[:, 4:8], AF.Exp, scale=-1.0), reads=[cs], writes=[cdec])
                qT, kT, Vt, ho = w["qT"], w["kT"], w["Vt"], w["ho"]
                cf, cb = Cf[ch], Cb[ch]
                hst = {}

                def F(h):
                        hs_ = self.hslot % 2
                        self.hslot += 1
                        WT, qs, ke, dd = w["WT"][hs_], w["qs"][hs_], w["ke"][hs_], w["dd"][hs_]
                        hst[h] = (hs_, WT, qs, ke, dd)
                        PS = P[0]
                        prS = PS.k(hs_)
                        sS = hs_ * 128
                        T.group("pe", [(lambda e, j=j: e.matmul(PS.t[:, sS:sS + 128], kT.t[:, h * 2 + j, :], qT.t[:, h * 2 + j, :], start=(j == 0), stop=(j == 1)))
                                       for j in range(2)], reads=[kT, qT], writes=[prS])
                        Dm, E = self.decay_mats(d, sp, h, bias.t[:, h:h + 1], w)
                        T.op("dve", lambda e: e.tensor_tensor(WT.t[:], PS.t[:, sS:sS + 128], Dm.t[:], ALU.mult), reads=[prS, Dm], writes=[WT])
                        T.op("dve", lambda e: e.tensor_tensor(qs.t[:], qT.t[:, h * 2:h * 2 + 2, :], E.t[:].unsqueeze(1).to_broadcast([128, 2, 128]), ALU.mult),
                             reads=[qT, E], writes=[qs])
                        PT = P[2]
                        prT = PT.k(hs_)
                        trv = self.trps(2)[:, hs_ * 256:(hs_ + 1) * 256]
                        T.group("pe", [(lambda e, j=j: e.transpose(trv[:, j * 128:(j + 1) * 128], kT.t[:, h * 2 + j, :], c["ident_b"].t[:])) for j in range(2)],
                                reads=[kT, c["ident_b"]], writes=[prT])
                        T.op("act", lambda e: e.activation(ke.t[:], trv[:, 0:256], AF.Copy, scale=wsx.t[:, h:h + 1]), reads=[prT, wsx], writes=[ke])

                def B(h):
                        hs_, WT, qs, ke, dd = hst[h]
                        PN = P[3 + hs_]
                        PD = P[7]
                        prD = PD.k(hs_)
                        sD = hs_ * 128
                        fns = []
                        for vc in range(5):
                            if vc < 4:
                                dst = PN.t[:, vc * 128:(vc + 1) * 128]
                                l0 = Vt.t[:, h * 512 + vc * 128:h * 512 + (vc + 1) * 128]
                            else:
                                dst = PD.t[:, sD:sD + 128]
                                l0 = c["ones_b"].t[:]
                            fns.append(lambda e, dst=dst, l0=l0: e.matmul(dst, l0, WT.t[:], start=True, stop=False))
                            for j in range(2):
                                fns.append(lambda e, dst=dst, j=j, vc=vc: e.matmul(dst, cb.t[:, h, j, vc * 128:(vc + 1) * 128], qs.t[:, j, :], start=False, stop=(j == 1)))
                        T.group("pe", fns, reads=[Vt, WT, cb, qs, c["ones_b"]], writes=[PN, prD])
                        T.op("act", lambda e: e.activation(dd.t[:], PD.t[:, sD:sD + 128], AF.Abs), reads=[prD], writes=[dd])
                        T.op("dve", lambda e: e.tensor_scalar_max(dd.t[:], dd.t[:], 1.0), reads=[dd], writes=[dd])
                        T.op("dve", lambda e: e.reciprocal(dd.t[:], dd.t[:]), reads=[dd], writes=[dd])
                        T.op("dve", lambda e: e.tensor_tensor(ho.t[:, h * 4:(h + 1) * 4, :], PN.t[:].rearrange("p (v t) -> p v t", t=128),
                                                              dd.t[:].unsqueeze(1).to_broadcast([128, 4, 128]), ALU.mult), reads=[PN, dd], writes=[ho])
                        for j in range(2):
                            Pc = P[5 + j]
                            T.op("pe", lambda e: e.matmul(Pc.t[:], ke.t[:, j * 128:(j + 1) * 128], Vt.t[:, h * 512:(h + 1) * 512], start=True, stop=True),
                                 reads=[ke, Vt], writes=[Pc])
                            T.op("dve", lambda e: e.scalar_tensor_tensor(cf.t[:, h, j, 0:512], cf.t[:, h, j, 0:512], cdec.t[:, h:h + 1], Pc.t[:], ALU.mult, ALU.add),
                                 reads=[cf, cdec, Pc], writes=[cf])
                        Pn = P[0]
                        prN = Pn.k("n")
                        T.group("pe", [(lambda e, j=j: e.matmul(Pn.t[:, 256 + j * 128:256 + (j + 1) * 128], ke.t[:, j * 128:(j + 1) * 128], c["ones_b"].t[:], start=True, stop=True))
                                       for j in range(2)], reads=[ke, c["ones_b"]], writes=[prN])
                        T.op("dve", lambda e: e.scalar_tensor_tensor(cf.t[:, h, :, 512:640], cf.t[:, h, :, 512:640], cdec.t[:, h:h + 1],
                                                                     Pn.t[:, 256:512].rearrange("p (j n) -> p j n", n=128), ALU.mult, ALU.add),
                             reads=[cf, cdec, prN], writes=[cf])
                        T.op("act", lambda e: e.activation(cb.t[:, h], cf.t[:, h], AF.Copy), reads=[cf], writes=[cb])

                for h in range(4):
                    F(h)
                    B(h)
                T.dma("act", self.HF[d].t[:, c0:c0 + 128].rearrange("(k p) n -> p k n", p=128), ho.t[:], reads=[ho], writes=[self.HF[d].k(c0 // 128)])

    def even_phase(self, es, e_):
        T, c, P = self.T, self.c, self.P
        chains = [(b, d) for b in range(2) for d in range(2)]
        rst = self.sb(es, "rst", [128, 1024], F32)
        T.op("pool", lambda e: e.memset(rst.t[:], 1.0), writes=[rst])
        T.op("pool", lambda e: e.memset(rst.t[:].rearrange("p (c t) -> p c t", t=128)[:, :, 0:1], 0.0), reads=[rst], writes=[rst])
        Hf = {ch: self.sb(es, "Hf", [128, 1024], F32) for ch in chains}
        Hb = {ch: self.sb(es, "Hb", [128, 1024], BF16) for ch in chains}
        Sf = {ch: self.sb(es, "Sf", [128, 8, 128], F32) for ch in chains}
        Sb = {ch: self.sb(es, "Sb", [128, 8, 128], BF16) for ch in chains}
        for ch in chains:
            for t in (Hf[ch], Hb[ch], Sf[ch], Sb[ch]):
                T.op("pool", lambda e: e.memset(t.t[:], 0.0), writes=[t])
        KI = {}
        for d in range(2):
            KI[d] = [self.sb(es, f"KI{d}_{i}", [128, 8, 128], BF16) for i in range(8)]
            for t in KI[d]:
                T.op("pool", lambda e: e.memset(t.t[:], 0.0), writes=[t])
        sets = []
        for i in range(2):
            w = {}
            def mk(name, shape, dt):
                w[name] = self.sb(es, name, shape, dt)
            mk("BCT", [128, 4, 128], BF16); mk("xsT", [128, 8, 128], BF16); mk("dtr", [128, 32], F32)
            mk("dt", [128, 16], F32); mk("na", [128, 16], F32); mk("cs", [128, 32], F32); mk("wsx", [128, 16], F32)
            mk("cdec", [128, 16], F32); mk("xdt", [128, 1024], BF16); mk("xdtw", [128, 1024], BF16); mk("Btm", [128, 256], BF16)
            mk("G", [128, 256], F32); mk("ysb", [128, 1024], BF16); mk("hos", [128, 8, 128], BF16)
            for nm, dt_ in (("Dm", F32), ("E", F32), ("MT", BF16), ("CsT", BF16), ("attm", BF16)):
                w[nm] = [self.sb(es, nm, [128, 128], dt_) for _ in range(2)]
            w["bias"] = w["cs"]
            mk("qT", [128, 8, 128], BF16); mk("fT", [128, 8, 128], BF16); mk("Vt", [128, 1024], BF16)
            mk("A", [128, 8, 128], F32); mk("KIN", [128, 8, 128], F32); mk("LG", [128, 8, 128], F32); mk("BC", [128, 8, 128], F32)
            mk("TMP", [128, 8, 128], F32); mk("tot", [128, 8], F32); mk("hdec", [128, 8], F32); mk("CST", [128, 8, 8], F32)
            mk("qs", [128, 8, 128], BF16); mk("keT", [128, 8, 128], BF16); mk("ketm", [128, 1024], BF16); mk("qloc", [128, 8, 128], BF16)
            mk("hoh", [128, 8, 128], BF16)
            sets.append(w)
        n = 0
        for step in range(18):
            for ch in chains:
                b, d = ch
                c0 = chain_chunks(b, d)[step]
                w = sets[n % 2]; n += 1
                self.ssd_step(e_, d, c0, w, Hf[ch], Hb[ch])
                self.hgrn_step(e_, d, c0, w, Sf[ch], Sb[ch], KI[d], rst)

    def ssd_step(self, e_, d, c0, w, hf, hb):
        T, c, P = self.T, self.c, self.P
        BCT, xsT, dtr, dt, na, wsx, cdec = w["BCT"], w["xsT"], w["dtr"], w["dt"], w["na"], w["wsx"], w["cdec"]
        xdt, xdtw, Btm, G, ysb, hos = w["xdt"], w["xdtw"], w["Btm"], w["G"], w["ysb"], w["hos"]
        T.dma("sp", BCT.t[:], self.U_FM.t[2048:2560, c0:c0 + 128].rearrange("(k p) n -> p k n", p=128), writes=[BCT])
        T.dma("sp", xsT.t[:], self.U_FM.t[1024:2048, c0:c0 + 128].rearrange("(k p) n -> p k n", p=128), writes=[xsT])
        T.dma("sp", dtr.t[:], self.U_G.t[c0:c0 + 128, 0:32], writes=[dtr])
        o = e_ * 32 + d * 16
        T.op("dve", lambda e: e.tensor_tensor(dt.t[:], dtr.t[:, d * 16:(d + 1) * 16], c["DTB"].t[:, o:o + 16], ALU.add), reads=[dtr, c["DTB"]], writes=[dt])
        T.op("act", lambda e: e.activation(dt.t[:], dt.t[:], AF.Exp), reads=[dt], writes=[dt])
        T.op("act", lambda e: e.activation(dt.t[:], dt.t[:], AF.Ln, bias=1.0), reads=[dt], writes=[dt])
        T.op("dve", lambda e: e.tensor_tensor(na.t[:], dt.t[:], c["EA"].t[:, o:o + 16], ALU.mult), reads=[dt, c["EA"]], writes=[na])
        cs = self.gate_cums(d, na, 16, w)
        T.op("dve", lambda e: e.tensor_tensor(wsx.t[:], cs.t[:, 16:32], cs.t[:, 0:16], ALU.subtract), reads=[cs], writes=[wsx])
        T.op("act", lambda e: e.activation(wsx.t[:], wsx.t[:], AF.Exp, scale=-1.0), reads=[wsx], writes=[wsx])
        T.op("act", lambda e: e.activation(cdec.t[:], cs.t[:, 16:32], AF.Exp, scale=-1.0), reads=[cs], writes=[cdec])
        tr2 = self.trps(2)
        T.group("pe", [(lambda e, j=j: e.transpose(tr2[:, j * 128:(j + 1) * 128], xsT.t[:, j, :], c["ident_b"].t[:])) for j in range(8)],
                reads=[xsT, c["ident_b"]], writes=[P[2]])
        T.op("dve", lambda e: e.tensor_tensor(xdt.t[:].rearrange("p (h q) -> p h q", q=64), tr2[:, 0:1024].rearrange("p (h q) -> p h q", q=64),
                                              dt.t[:].unsqueeze(2).to_broadcast([128, 16, 64]), ALU.mult), reads=[P[2], dt], writes=[xdt])
        T.op("dve", lambda e: e.tensor_tensor(xdtw.t[:].rearrange("p (h q) -> p h q", q=64), xdt.t[:].rearrange("p (h q) -> p h q", q=64),
                                              wsx.t[:].unsqueeze(2).to_broadcast([128, 16, 64]), ALU.mult), reads=[xdt, wsx], writes=[xdtw])
        tr7 = self.trps(7)
        T.group("pe", [(lambda e, g=g: e.transpose(tr7[:, g * 128:(g + 1) * 128], BCT.t[:, g, :], c["ident_b"].t[:])) for g in range(2)],
                reads=[BCT, c["ident_b"]], writes=[P[7]])
        T.op("act", lambda e: e.activation(Btm.t[:], tr7[:, 0:256], AF.Copy), reads=[P[7]], writes=[Btm])
        T.group("pe", [(lambda e, g=g: e.matmul(P[0].t[:, g * 128:(g + 1) * 128], BCT.t[:, g, :], BCT.t[:, 2 + g, :], start=True, stop=True)) for g in range(2)],
                reads=[BCT], writes=[P[0]])
        T.op("act", lambda e: e.activation(G.t[:], P[0].t[:, 0:256], AF.Copy), reads=[P[0]], writes=[G])
        def F(h):
            g = h // 8
            Dm, E = self.decay_mats(d, na, h, cs.t[:, h:h + 1], w)
            MT, CsT = w["MT"][h % 2], w["CsT"][h % 2]
            T.op("dve", lambda e: e.tensor_tensor(MT.t[:], G.t[:, g * 128:(g + 1) * 128], Dm.t[:], ALU.mult), reads=[G, Dm], writes=[MT])
            T.op("dve", lambda e: e.tensor_tensor(CsT.t[:], BCT.t[:, 2 + g, :], E.t[:], ALU.mult), reads=[BCT, E], writes=[CsT])

        def B(h):
            MT, CsT = w["MT"][h % 2], w["CsT"][h % 2]
            PY = P[3 + h // 8]
            dst = PY.t[:, (h % 8) * 64:(h % 8 + 1) * 64]
            T.group("pe", [lambda e: e.matmul(dst, MT.t[:], xdt.t[:, h * 64:(h + 1) * 64], start=True, stop=False),
                           lambda e: e.matmul(dst, CsT.t[:], hb.t[:, h * 64:(h + 1) * 64], start=False, stop=True)],
                    reads=[MT, xdt, CsT, hb], writes=[PY])
        F(0)
        for h in range(16):
            if h + 1 < 16:
                F(h + 1)
            B(h)
        T.op("act", lambda e: e.activation(ysb.t[:, 0:512], P[3].t[:], AF.Copy), reads=[P[3]], writes=[ysb])
        T.op("dve", lambda e: e.tensor_copy(ysb.t[:, 512:1024], P[4].t[:]), reads=[P[4]], writes=[ysb])
        for g in range(2):
            Pc = P[5 + g]
            T.op("pe", lambda e: e.matmul(Pc.t[:], Btm.t[:, g * 128:(g + 1) * 128], xdtw.t[:, g * 512:(g + 1) * 512], start=True, stop=True),
                 reads=[Btm, xdtw], writes=[Pc])
            hv = hf.t[:, g * 512:(g + 1) * 512].rearrange("p (h q) -> p h q", q=64)
            T.op("dve", lambda e: e.tensor_tensor(hv, hv, cdec.t[:, g * 8:(g + 1) * 8].unsqueeze(2).to_broadcast([128, 8, 64]), ALU.mult),
                 reads=[hf, cdec], writes=[hf])
            T.op("dve", lambda e: e.tensor_tensor(hf.t[:, g * 512:(g + 1) * 512], hf.t[:, g * 512:(g + 1) * 512], Pc.t[:], ALU.add), reads=[hf, Pc], writes=[hf])
        T.op("act", lambda e: e.activation(hb.t[:], hf.t[:], AF.Copy), reads=[hf], writes=[hb])
        T.group("pe", [(lambda e, j=j: e.transpose(tr2[:, j * 128:(j + 1) * 128], ysb.t[:, j * 128:(j + 1) * 128], c["ident_b"].t[:])) for j in range(8)],
                reads=[ysb, c["ident_b"]], writes=[P[2]])
        T.op("act", lambda e: e.activation(hos.t[:], tr2[:, 0:1024].rearrange("p (k t) -> p k t", t=128), AF.Copy), reads=[P[2]], writes=[hos])
        T.dma("act", self.HF[d].t[0:1024, c0:c0 + 128].rearrange("(k p) n -> p k n", p=128), hos.t[:], reads=[hos], writes=[self.HF[d].k((0, c0 // 128))])

    def hgrn_step(self, e_, d, c0, w, sf, sb, KI, rst):
        T, c, P = self.T, self.c, self.P
        qT, fT, Vt, A, KIN, LG, BC, TMP = w["qT"], w["fT"], w["Vt"], w["A"], w["KIN"], w["LG"], w["BC"], w["TMP"]
        tot, hdec, CST, qs, keT, ketm, qloc, hoh = w["tot"], w["hdec"], w["CST"], w["qs"], w["keT"], w["ketm"], w["qloc"], w["hoh"]
        T.dma("sp", qT.t[:], self.U_FM.t[2560:3584, c0:c0 + 128].rearrange("(k p) n -> p k n", p=128), writes=[qT])
        T.dma("sp", fT.t[:], self.U_FM.t[3584 + d * 1024:3584 + (d + 1) * 1024, c0:c0 + 128].rearrange("(k p) n -> p k n", p=128), writes=[fT])
        T.dma("sp", Vt.t[:], self.U_TM.t[c0:c0 + 128, 0:1024], writes=[Vt])
        lb = c["LB"].t[:, e_ * 8:(e_ + 1) * 8].unsqueeze(2).to_broadcast([128, 8, 128])
        lb1 = c["LB1"].t[:, e_ * 8:(e_ + 1) * 8].unsqueeze(2).to_broadcast([128, 8, 128])
        T.op("act", lambda e: e.activation(A.t[:], fT.t[:], AF.Sigmoid), reads=[fT], writes=[A])
        T.op("dve", lambda e: e.tensor_tensor(A.t[:], A.t[:], lb1, ALU.mult), reads=[A, c["LB1"]], writes=[A])
        T.op("dve", lambda e: e.tensor_tensor(A.t[:], A.t[:], lb, ALU.add), reads=[A, c["LB"]], writes=[A])
        T.op("dve", lambda e: e.tensor_scalar(KIN.t[:], A.t[:], -1.0, 1.0, ALU.mult, ALU.add), reads=[A], writes=[KIN])
        T.op("act", lambda e: e.activation(LG.t[:], A.t[:], AF.Ln), reads=[A], writes=[LG])
        fl = lambda t: t.t[:].rearrange("p h t -> p (h t)")
        T.op("dve", lambda e: e.tensor_tensor_scan(fl(BC), rst.t[:], fl(LG), 0.0, ALU.mult, ALU.add), reads=[rst, LG], writes=[BC])
        T.op("dve", lambda e: e.tensor_copy(tot.t[:].unsqueeze(2), BC.t[:, :, 127:128]), reads=[BC], writes=[tot])
        totb = tot.t[:].unsqueeze(2).to_broadcast([128, 8, 128])
        if d == 1:
            T.op("dve", lambda e: e.tensor_tensor(BC.t[:], LG.t[:], BC.t[:], ALU.subtract), reads=[LG, BC], writes=[BC])
            T.op("dve", lambda e: e.tensor_tensor(BC.t[:], BC.t[:], totb, ALU.add), reads=[BC, tot], writes=[BC])
        T.op("act", lambda e: e.activation(TMP.t[:], BC.t[:], AF.Exp), reads=[BC], writes=[TMP])
        T.op("dve", lambda e: e.tensor_tensor(qs.t[:], qT.t[:], TMP.t[:], ALU.mult), reads=[qT, TMP], writes=[qs])
        T.op("dve", lambda e: e.tensor_tensor(TMP.t[:], totb, BC.t[:], ALU.subtract), reads=[tot, BC, TMP], writes=[TMP])
        T.op("act", lambda e: e.activation(TMP.t[:], TMP.t[:], AF.Exp), reads=[TMP], writes=[TMP])
        T.op("dve", lambda e: e.tensor_tensor(keT.t[:], TMP.t[:], KIN.t[:], ALU.mult), reads=[TMP, KIN], writes=[keT])
        T.op("act", lambda e: e.activation(hdec.t[:], tot.t[:], AF.Exp), reads=[tot], writes=[hdec])
        tr2 = self.trps(2)
        T.group("pe", [(lambda e, h=h: e.transpose(tr2[:, h * 128:(h + 1) * 128], keT.t[:, h, :], c["ident_b"].t[:])) for h in range(8)],
                reads=[keT, c["ident_b"]], writes=[P[2]])
        T.op("act", lambda e: e.activation(ketm.t[:], tr2[:, 0:1024], AF.Copy), reads=[P[2]], writes=[ketm])
        if d == 0:
            T.op("pool", lambda e: e.memset(CST.t[:, :, 0:1], 0.0), writes=[CST])
            T.op("dve", lambda e: e.tensor_copy(CST.t[:, :, 1:8], BC.t[:, :, 15:127:16]), reads=[BC, CST], writes=[CST])
        else:
            T.op("pool", lambda e: e.memset(CST.t[:, :, 7:8], 0.0), writes=[CST])
            T.op("dve", lambda e: e.tensor_copy(CST.t[:, :, 0:7], BC.t[:, :, 16:128:16]), reads=[BC, CST], writes=[CST])
        v64 = lambda t: t.t[:].rearrange("p h (i u) -> p (h i) u", u=16)
        T.op("dve", lambda e: e.tensor_tensor(v64(TMP), v64(BC), CST.t[:].rearrange("p h i -> p (h i)").unsqueeze(2).to_broadcast([128, 64, 16]), ALU.subtract),
             reads=[BC, CST, TMP], writes=[TMP])
        T.op("act", lambda e: e.activation(TMP.t[:], TMP.t[:], AF.Exp), reads=[TMP], writes=[TMP])
        T.op("dve", lambda e: e.tensor_tensor(qloc.t[:], qT.t[:], TMP.t[:], ALU.mult), reads=[qT, TMP], writes=[qloc])
        for i in range(8):
            lo, hi = (0, 16 * (i + 1)) if d == 0 else (16 * i, 128)
            Ki = KI[i]
            T.op("dve", lambda e: e.tensor_tensor(TMP.t[:, :, lo:hi], CST.t[:, :, i:i + 1].to_broadcast([128, 8, hi - lo]), BC.t[:, :, lo:hi], ALU.subtract),
                 reads=[CST, BC, TMP], writes=[TMP])
            T.op("act", lambda e: e.activation(TMP.t[:, :, lo:hi], TMP.t[:, :, lo:hi], AF.Exp), reads=[TMP], writes=[TMP])
            T.op("dve", lambda e: e.tensor_tensor(Ki.t[:, :, lo:hi], TMP.t[:, :, lo:hi], KIN.t[:, :, lo:hi], ALU.mult), reads=[TMP, KIN], writes=[Ki])
        def F(h):
            PA = P[h % 2]
            prA = PA
            a0 = 0
            attm = w["attm"][h % 2]
            T.group("pe", [(lambda e, i=i: e.matmul(PA.t[:, a0 + 16 * i:a0 + 16 * i + 16], KI[i].t[:, h, :], qloc.t[:, h, 16 * i:16 * i + 16], start=True, stop=True))
                           for i in range(8)], reads=KI + [qloc], writes=[prA])
            T.op("dve", lambda e: e.tensor_tensor(attm.t[:], PA.t[:, a0:a0 + 128], c[f"tri{d}"].t[:], ALU.mult), reads=[prA, c[f"tri{d}"]], writes=[attm])

        def B(h):
            attm = w["attm"][h % 2]
            PO = P[3 + h // 4]
            dst = PO.t[:, (h % 4) * 128:(h % 4 + 1) * 128]
            T.group("pe", [lambda e: e.matmul(dst, Vt.t[:, h * 128:(h + 1) * 128], attm.t[:], start=True, stop=False),
                           lambda e: e.matmul(dst, sb.t[:, h, :], qs.t[:, h, :], start=False, stop=True)],
                    reads=[Vt, attm, sb, qs], writes=[PO])
            Pc = P[5 + h % 2]
            T.op("pe", lambda e: e.matmul(Pc.t[:, 0:128], ketm.t[:, h * 128:(h + 1) * 128], Vt.t[:, h * 128:(h + 1) * 128], start=True, stop=True),
                 reads=[ketm, Vt], writes=[Pc])
            T.op("dve", lambda e: e.scalar_tensor_tensor(sf.t[:, h, :], sf.t[:, h, :], hdec.t[:, h:h + 1], Pc.t[:, 0:128], ALU.mult, ALU.add),
                 reads=[sf, hdec, Pc], writes=[sf])
        F(0)
        for h in range(8):
            if h + 1 < 8:
                F(h + 1)
            B(h)
        T.op("act", lambda e: e.activation(sb.t[:], sf.t[:], AF.Copy), reads=[sf], writes=[sb])
        T.op("act", lambda e: e.activation(hoh.t[:, 0:4, :], P[3].t[:].rearrange("p (h t) -> p h t", t=128), AF.Copy), reads=[P[3]], writes=[hoh])
        T.op("dve", lambda e: e.tensor_copy(hoh.t[:, 4:8, :], P[4].t[:].rearrange("p (h t) -> p h t", t=128)), reads=[P[4]], writes=[hoh])
        T.dma("act", self.HF[d].t[1024:2048, c0:c0 + 128].rearrange("(k p) n -> p k n", p=128), hoh.t[:], reads=[hoh], writes=[self.HF[d].k((1, c0 // 128))])


_W_NAMES = ["mod_w", "mod_b", "norm_w", "final_norm_w", "mlp_w1", "mlp_w2", "even_w_in", "even_w_out",
            "ssd_conv_w", "ssd_conv_b", "ssd_a_log", "ssd_dt_bias", "ssd_d", "ssd_norm_w", "hgrn_lb", "hgrn_norm_w",
            "odd_w_in", "odd_w_out", "mlstm_conv_w", "mlstm_conv_b", "mlstm_gate_b", "mlstm_norm_w"]


def make_in_maps(inputs, cores):
    f = lambda a: np.ascontiguousarray(np.asarray(a, dtype=np.float32))
    shared = {k: f(inputs[k]) for k in _W_NAMES}
    maps = []
    for ci in cores:
        m = dict(shared)
        m["x"] = f(inputs["x"][2 * ci:2 * ci + 2])
        m["ctx"] = f(inputs["ctx"][2 * ci:2 * ci + 2])
        m["cvec"] = f(np.stack([inputs["c"][2 * ci], inputs["c"][2 * ci + 1], inputs["c_ctx"]], axis=0))
        maps.append(m)
    return maps


def kernel(**inputs):
    prog = Prog()
    nc = prog.build()
    maps = make_in_maps(inputs, list(range(8)))
    res = run_bass_kernel_spmd(nc, maps, core_ids=list(range(8)))
    return np.concatenate([np.asarray(r["out"], dtype=np.float32) for r in res.results], axis=0)
```

```python
import contextlib
import math
import numpy as np
import concourse.bass as bass
import concourse.mybir as mybir
from concourse.bass_utils import run_bass_kernel_spmd

F32 = mybir.dt.float32
BF16 = mybir.dt.bfloat16
ALU = mybir.AluOpType
AF = mybir.ActivationFunctionType

D = 2048
NTOK = 4608
NBLK = 9
NCH = 36
DEPTH = 4
EVEN_IN = 7712
ODD_IN = 6160
EPS = 1e-6
BIG = 30000.0


class Res:
    __slots__ = ("name", "w", "r")

    def __init__(self, name):
        self.name = name
        self.w = None
        self.r = []


class Buf:
    def __init__(self, t, name):
        self.t = t
        self.name = name
        self.r = Res(name)
        self.subs = {}

    def k(self, key):
        s = self.subs.get(key)
        if s is None:
            s = Res(f"{self.name}.{key}")
            self.subs[key] = s
        return s


def _res(x):
    return x.r if isinstance(x, Buf) else x


class Trk:
    EPOCH = 60000

    def __init__(self, nc, es, n_dma_sems=(10, 6, 10)):
        self.nc = nc
        self.es = es
        self.engs = {"pe": nc.tensor, "act": nc.scalar, "dve": nc.vector, "pool": nc.gpsimd, "sp": nc.sync}
        self.sems = {}
        self.cnt = {}
        self.waited = {k: {} for k in self.engs}
        self.epoch = {}
        self.last = {}
        for k in ("pe", "act", "dve", "pool"):
            self.epoch[k] = 0
            self._new_sem(f"e_{k}0")
        self.dq = {}
        self.gen = 0
        for q, n in zip(("sp", "act", "pool"), n_dma_sems):
            keys = []
            for i in range(n):
                key = f"d_{q}{i}"
                self._new_sem(key)
                keys.append(key)
            self.dq[q] = {"keys": keys, "i": 0}
        self.n_instr = 0
        self.n_wait = 0

    def _new_sem(self, key):
        self.sems[key] = self.es.enter_context(self.nc.semaphore(key))
        self.cnt[key] = 0

    def _eng_key(self, k):
        key = f"e_{k}{self.epoch[k]}"
        if self.cnt[key] >= self.EPOCH:
            self.epoch[k] += 1
            key = f"e_{k}{self.epoch[k]}"
            self._new_sem(key)
        return key

    def _wait(self, ek, tok):
        if tok is None:
            return
        key, val = tok
        if self.waited[ek].get(key, 0) >= val:
            return
        self.engs[ek].wait_ge(self.sems[key], val)
        self.waited[ek][key] = val
        self.n_wait += 1

    def _deps(self, ek, reads, writes):
        toks = []
        for r in reads:
            r = _res(r)
            if r.w is not None:
                toks.append((r.w, True))
        for w in writes:
            w = _res(w)
            if w.w is not None:
                toks.append((w.w, True))
            for t in w.r:
                toks.append((t, False))
        for tok, strong in toks:
            if tok[0].startswith("e_" + ek):
                if ek == "pe" or not strong:
                    continue
            self._wait(ek, tok)

    def _commit(self, tok, reads, writes):
        for r in reads:
            r = _res(r)
            r.r.append(tok)
            if len(r.r) > 48:
                best = {}
                for k, v in r.r:
                    if best.get(k, 0) < v:
                        best[k] = v
                r.r = list(best.items())
        for w in writes:
            w = _res(w)
            w.w = tok
            w.r = []

    def op(self, ek, fn, reads=(), writes=()):
        return self.group(ek, [fn], reads, writes)

    def group(self, ek, fns, reads=(), writes=()):
        key = self._eng_key(ek)
        self._deps(ek, reads, writes)
        ins = None
        for fn in fns:
            ins = fn(self.engs[ek])
            self.n_instr += 1
        ins.then_inc(self.sems[key], 1)
        self.cnt[key] += 1
        tok = (key, self.cnt[key])
        self.last[ek] = tok
        self._commit(tok, reads, writes)
        return tok

    def dma(self, q, out, in_, reads=(), writes=(), **kw):
        d = self.dq[q]
        slot = d["i"] % len(d["keys"])
        d["i"] += 1
        key = d["keys"][slot]
        if self.cnt[key] + 16 > self.EPOCH:
            self.gen += 1
            key = f"d_{q}{slot}_{self.gen}"
            self._new_sem(key)
            d["keys"][slot] = key
        if self.cnt[key] > 0:
            self._wait(q, (key, self.cnt[key]))
        self._deps(q, reads, writes)
        ins = self.engs[q].dma_start(out=out, in_=in_, **kw)
        self.n_instr += 1
        ins.then_inc(self.sems[key], 16)
        self.cnt[key] += 16
        tok = (key, self.cnt[key])
        self._commit(tok, reads, writes)
        return tok

    def all_tokens(self):
        toks = [t for t in self.last.values()]
        for q in self.dq.values():
            for key in q["keys"]:
                if self.cnt[key]:
                    toks.append((key, self.cnt[key]))
        for key, c in self.cnt.items():
            if key.startswith("d_") and c and (key, c) not in toks:
                toks.append((key, c))
        return toks

    def barrier(self, engines=("pe", "act", "dve", "pool", "sp")):
        toks = self.all_tokens()
        for ek in engines:
            for t in toks:
                self._wait(ek, t)


def blk_cols(tb):
    return tb * 512


def chunk_col(b, kind, j):
    if kind == "lat":
        return b * 2048 + j * 128
    return 4096 + b * 256 + j * 128


def chain_chunks(b, d):
    ctx = [("ctx", 0), ("ctx", 1)]
    lat = [("lat", j) for j in range(16)]
    if d == 0:
        seq = ctx + lat
    else:
        seq = ctx[::-1] + lat[::-1]
    return [chunk_col(b, k, j) for k, j in seq]


class Prog:
    def __init__(self, n_layers=DEPTH, debug=False):
        self.n_layers = n_layers
        self.debug = debug
        self.nc = bass.Bass("TRN2", target_bir_lowering=False)
        self.uid = 0

    def kind(self, layer):
        f = getattr(self, "force_kind", None)
        if f is not None:
            return f
        return (layer % 2 == 1, layer // 2)

    def sb(self, es, name, shape, dt):
        self.uid += 1
        nm = f"{name}_{self.uid}"
        return Buf(es.enter_context(self.nc.sbuf_tensor(nm, list(shape), dt)), nm)

    def ps(self, es, name, shape, dt=F32):
        self.uid += 1
        nm = f"{name}_{self.uid}"
        return Buf(es.enter_context(self.nc.psum_tensor(nm, list(shape), dt)), nm)

    def dram(self, name, shape, dt, kind="Internal"):
        if self.debug and kind == "Internal" and name in self.debug_outs:
            kind = "ExternalOutput"
        t = self.nc.dram_tensor(name, list(shape), dt, kind=kind)
        return Buf(t.ap(), name)

    def build(self, debug_outs=()):
        self.debug_outs = set(debug_outs)
        nc = self.nc
        I = {}
        def inp(name, shape):
            I[name] = nc.dram_tensor(name, list(shape), F32, kind="ExternalInput").ap()
        inp("x", [2, 2048, D]); inp("ctx", [2, 256, D]); inp("cvec", [3, D])
        inp("mod_w", [DEPTH, D, 6 * D]); inp("mod_b", [DEPTH, 6 * D]); inp("norm_w", [DEPTH, 2, D])
        inp("final_norm_w", [D]); inp("mlp_w1", [DEPTH, D, 4 * D]); inp("mlp_w2", [DEPTH, 4 * D, D])
        inp("even_w_in", [2, D, EVEN_IN]); inp("even_w_out", [2, D, D])
        inp("ssd_conv_w", [2, 3, 1536]); inp("ssd_conv_b", [2, 1536]); inp("ssd_a_log", [2, 2, 16])
        inp("ssd_dt_bias", [2, 2, 16]); inp("ssd_d", [2, 16]); inp("ssd_norm_w", [2, 1024])
        inp("hgrn_lb", [2, 1024]); inp("hgrn_norm_w", [2, 1024])
        inp("odd_w_in", [2, D, ODD_IN]); inp("odd_w_out", [2, D, D])
        inp("mlstm_conv_w", [2, 3, 2048]); inp("mlstm_conv_b", [2, 2048]); inp("mlstm_gate_b", [2, 4, 4])
        inp("mlstm_norm_w", [2, 2048])
        self.I = I
        self.out = nc.dram_tensor("out", [2, 2048, D], F32, kind="ExternalOutput").ap()
        self.r_out = Res("out")

        self.XT = self.dram("XT", [D, NTOK], F32)
        self.U_FM = self.dram("U_FM", [6656, NTOK], BF16)
        self.U_TM = self.dram("U_TM", [NTOK, 2048], BF16)
        self.U_G = self.dram("U_G", [NTOK, 32], F32)
        self.HF = [self.dram("HF0", [D, NTOK], BF16), self.dram("HF1", [D, NTOK], BF16)]

        with contextlib.ExitStack() as es:
            self.T = Trk(nc, es)
            with contextlib.ExitStack() as ces:
                self.consts(ces)
                with contextlib.ExitStack() as pes:
                    self.prologue(pes)
                self.T.barrier()
                stop = getattr(self, "stop_after", None)
                if self.debug:
                    dbg = self.nc.dram_tensor("DBG_MOD", [128, DEPTH * 288], F32, kind="ExternalOutput").ap()
                    self.T.dma("sp", dbg, self.c["MOD"].t[:], reads=[self.c["MOD"]])
                    dbg2 = self.nc.dram_tensor("DBG_S", [128, DEPTH * 96], F32, kind="ExternalOutput").ap()
                    self.T.dma("sp", dbg2, self.c["S"].t[:], reads=[self.c["S"]])
                stopped = stop == ("prologue",)
                for layer in range(self.n_layers):
                    if stopped:
                        break
                    with contextlib.ExitStack() as des:
                        self.dense_phase(des, layer)
                    self.T.barrier()
                    if stop == ("dense", layer):
                        stopped = True
                        break
                    with contextlib.ExitStack() as mes:
                        self.mixer_phase(mes, layer)
                    self.T.barrier()
                    if stop == ("mixer", layer):
                        stopped = True
                        break
                if not stopped:
                    with contextlib.ExitStack() as des:
                        self.dense_phase(des, self.n_layers)
                self.T.barrier()
        return nc

    def consts(self, es):
        T = self.T
        c = self.c = {}

        def mk(name, shape, dt):
            c[name] = self.sb(es, name, shape, dt)
            return c[name]
        self.P = [self.ps(es, f"P{i}", [128, 512], F32) for i in range(8)]
        idf = mk("ident_f", [128, 128], F32)
        T.op("pool", lambda e: e.memset(idf.t[:], 1.0), writes=[idf])
        T.op("pool", lambda e: e.affine_select(idf.t[:], idf.t[:], pattern=[[-1, 128]], compare_op=ALU.is_equal,
                                               fill=0.0, base=0, channel_multiplier=1), reads=[idf], writes=[idf])
        idb = mk("ident_b", [128, 128], BF16)
        T.op("dve", lambda e: e.tensor_copy(idb.t[:], idf.t[:]), reads=[idf], writes=[idb])
        of = mk("ones_f", [128, 128], F32)
        T.op("pool", lambda e: e.memset(of.t[:], 1.0), writes=[of])
        ob = mk("ones_b", [128, 128], BF16)
        T.op("pool", lambda e: e.memset(ob.t[:], 1.0), writes=[ob])
        for d in range(2):
            sgn = 1 if d == 0 else -1
            tri = mk(f"tri{d}", [128, 128], F32)
            T.op("pool", lambda e: e.memset(tri.t[:], 1.0), writes=[tri])
            T.op("pool", lambda e: e.affine_select(tri.t[:], tri.t[:], pattern=[[sgn, 128]], compare_op=ALU.is_ge,
                                                   fill=0.0, base=0, channel_multiplier=-sgn), reads=[tri], writes=[tri])
            msk = mk(f"msk{d}", [128, 128], F32)
            T.op("dve", lambda e: e.tensor_scalar(msk.t[:], tri.t[:], -BIG, BIG, ALU.mult, ALU.add), reads=[tri], writes=[msk])
        mk("MOD", [128, DEPTH * 6 * 16 * 3], F32)
        mk("S", [128, DEPTH * 2 * 16 * 3], F32)
        mk("MODB", [128, DEPTH * 96], F32)
        mk("NW", [128, 128], F32)
        mk("FNW", [128, 16], F32)
        mk("SCW", [128, 72], F32); mk("SCB", [128, 24], F32)
        mk("MCW", [128, 96], F32); mk("MCB", [128, 32], F32)
        mk("SNW", [128, 16], F32); mk("LBR", [128, 16], F32); mk("HNW", [128, 16], F32); mk("MNW", [128, 32], F32)
        mk("LB", [128, 16], F32); mk("LB1", [128, 16], F32)
        mk("DSK", [128, 16], F32)
        mk("GB", [128, 32], F32)
        mk("DTB", [128, 64], F32)
        mk("EA", [128, 64], F32)
        mk("scT", [128, 16 * 3], BF16)

    def MODv(self, l, m, dc, v):
        i = ((l * 6 + m) * 16 + dc) * 3 + v
        return self.c["MOD"].t[:, i:i + 1]

    def Sv(self, l, i, dc, v):
        j = ((l * 2 + i) * 16 + dc) * 3 + v
        return self.c["S"].t[:, j:j + 1]

    def load_vecT(self, es, dst, src2d, k):
        T = self.T
        st = self.sb(es, "vst", [k, 128], F32)
        T.dma("sp", st.t[:], src2d, writes=[st])
        P = self.P[7]
        T.op("pe", lambda e: e.transpose(P.t[:, 0:k], st.t[:], self.c["ident_f"].t[0:k, 0:k]),
             reads=[st, self.c["ident_f"]], writes=[P])
        T.op("dve", lambda e: e.tensor_copy(dst.t[:, 0:k], P.t[:, 0:k]), reads=[P], writes=[dst])

    def prologue(self, es):
        T, c, I, nc = self.T, self.c, self.I, self.nc
        v128 = lambda ap, pat, **kw: ap.rearrange(pat, **kw)
        self.load_vecT(es, c["NW"], I["norm_w"].rearrange("l i (c p) -> (l i c) p", p=128), 128)
        self.load_vecT(es, c["FNW"], I["final_norm_w"].rearrange("(c p) -> c p", p=128), 16)
        self.load_vecT(es, c["SCW"], I["ssd_conv_w"].rearrange("e j (c p) -> (e j c) p", p=128), 72)
        self.load_vecT(es, c["SCB"], I["ssd_conv_b"].rearrange("e (c p) -> (e c) p", p=128), 24)
        self.load_vecT(es, c["MCW"], I["mlstm_conv_w"].rearrange("e j (c p) -> (e j c) p", p=128), 96)
        self.load_vecT(es, c["MCB"], I["mlstm_conv_b"].rearrange("e (c p) -> (e c) p", p=128), 32)
        self.load_vecT(es, c["SNW"], I["ssd_norm_w"].rearrange("e (c p) -> (e c) p", p=128), 16)
        self.load_vecT(es, c["LBR"], I["hgrn_lb"].rearrange("e (c p) -> (e c) p", p=128), 16)
        self.load_vecT(es, c["HNW"], I["hgrn_norm_w"].rearrange("e (c p) -> (e c) p", p=128), 16)
        self.load_vecT(es, c["MNW"], I["mlstm_norm_w"].rearrange("e (c p) -> (e c) p", p=128), 32)
        for l in range(DEPTH):
            tmp = self.sb(es, "mbt", [128, 96], F32)
            self.load_vecT(es, tmp, I["mod_b"][l].rearrange("(c p) -> c p", p=128), 96)
            T.op("dve", lambda e: e.tensor_copy(c["MODB"].t[:, l * 96:(l + 1) * 96], tmp.t[:]), reads=[tmp], writes=[c["MODB"]])
        LB, LB1, LBR = c["LB"], c["LB1"], c["LBR"]
        T.op("pool", lambda e: e.memset(LB.t[:, 0:8], 0.0), writes=[LB])
        T.op("dve", lambda e: e.tensor_tensor(LB.t[:, 8:16], LBR.t[:, 8:16], LBR.t[:, 0:8], ALU.subtract), reads=[LBR, LB], writes=[LB])
        T.op("act", lambda e: e.activation(LB.t[:, 8:16], LB.t[:, 8:16], AF.Sigmoid), reads=[LB], writes=[LB])
        T.op("dve", lambda e: e.tensor_scalar(LB1.t[:], LB.t[:], -1.0, 1.0, ALU.mult, ALU.add), reads=[LB], writes=[LB1])
        T.dma("sp", c["GB"].t[:], I["mlstm_gate_b"].rearrange("o a b -> (o a b)").partition_broadcast(128), writes=[c["GB"]])
        T.dma("sp", c["DTB"].t[:], I["ssd_dt_bias"].rearrange("e d h -> (e d h)").partition_broadcast(128), writes=[c["DTB"]])
        T.dma("sp", c["EA"].t[:], I["ssd_a_log"].rearrange("e d h -> (e d h)").partition_broadcast(128), writes=[c["EA"]])
        T.op("act", lambda e: e.activation(c["EA"].t[:], c["EA"].t[:], AF.Exp), reads=[c["EA"]], writes=[c["EA"]])
        for o in range(2):
            T.op("dve", lambda e: e.tensor_scalar_add(c["GB"].t[:, o * 16:o * 16 + 8], c["GB"].t[:, o * 16:o * 16 + 8], -math.log(16.0)),
                 reads=[c["GB"]], writes=[c["GB"]])
        for e_ in range(2):
            for h in range(16):
                T.dma("sp", c["DSK"].t[(h % 2) * 64:(h % 2) * 64 + 64, e_ * 8 + h // 2:e_ * 8 + h // 2 + 1],
                      I["ssd_d"][e_, h:h + 1].partition_broadcast(64), writes=[c["DSK"]])
        cT = self.sb(es, "cT", [128, 48], F32)
        self.load_vecT(es, cT, I["cvec"].rearrange("v (c p) -> (v c) p", p=128), 48)
        scv = c["scT"].t[:].rearrange("p (k v) -> p k v", v=3)
        for v in range(3):
            T.op("act", lambda e: e.activation(scv[:, :, v], cT.t[:, v * 16:(v + 1) * 16], AF.Silu), reads=[cT], writes=[c["scT"]])
        wb = [self.sb(es, f"mw{i}", [128, 16, 512], BF16) for i in range(3)]
        P = self.P
        n = 0
        for l in range(DEPTH):
            Pm = P[l % 2]
            for jb in range(24):
                w = wb[n % 3]; n += 1
                T.dma("pool", w.t[:], I["mod_w"][l][:, jb * 512:(jb + 1) * 512].rearrange("(k p) n -> p k n", p=128), writes=[w])
                for jt in range(4):
                    j = jb * 4 + jt
                    T.group("pe", [
                        (lambda e, kc=kc: e.matmul(Pm.t[:, j * 3:j * 3 + 3], w.t[:, kc, jt * 128:(jt + 1) * 128], scv[:, kc, :],
                                                   start=(kc == 0), stop=(kc == 15))) for kc in range(16)],
                        reads=[w, c["scT"]], writes=[Pm])
            MODl = c["MOD"].t[:, l * 288:(l + 1) * 288].rearrange("p (j v) -> p j v", v=3)
            T.op("dve", lambda e: e.tensor_tensor(MODl, Pm.t[:, 0:288].rearrange("p (j v) -> p j v", v=3),
                                                  c["MODB"].t[:, l * 96:(l + 1) * 96].unsqueeze(2).to_broadcast([128, 96, 3]), ALU.add),
                 reads=[Pm, c["MODB"]], writes=[c["MOD"]])
            for i in range(2):
                Sl = c["S"].t[:, (l * 2 + i) * 48:(l * 2 + i + 1) * 48].rearrange("p (k v) -> p k v", v=3)
                sc = c["MOD"].t[:, ((l * 6 + 1 + 3 * i) * 16) * 3:((l * 6 + 2 + 3 * i) * 16) * 3].rearrange("p (k v) -> p k v", v=3)
                nw = c["NW"].t[:, (l * 2 + i) * 16:(l * 2 + i + 1) * 16].unsqueeze(2).to_broadcast([128, 16, 3])
                T.op("dve", lambda e: e.tensor_scalar_add(Sl, sc, 1.0), reads=[c["MOD"]], writes=[c["S"]])
                T.op("dve", lambda e: e.tensor_tensor(Sl, Sl, nw, ALU.mult), reads=[c["S"], c["NW"]], writes=[c["S"]])

    def dense_phase(self, es, layer):
        T, c, I, P = self.T, self.c, self.I, self.P
        L = self.n_layers
        self.xT = self.sb(es, "xT", [128, 16, 512], F32)
        self.actT = self.sb(es, "actT", [128, 16, 512], BF16)
        self.wb = [self.sb(es, f"wb{i}", [128, 16 * 512], BF16) for i in range(3)]
        self.wn = 0
        self.f32t = [self.sb(es, f"f32t{i}", [128, 512], F32) for i in range(4)]
        self.fn = 0
        self.rstd = self.sb(es, "rstd", [128, 512], F32)
        self.so = [self.sb(es, f"so{i}", [128, 4, 512], BF16) for i in range(2)]
        self.son = 0
        self.pn = 0
        if layer > 0:
            self.aT = self.sb(es, "aT", [128, 64, 512], BF16)
            self.hA = self.sb(es, "hA", [128, 4, 512], BF16)
            self.hB = self.sb(es, "hB", [128, 4, 512], BF16)
            self.hC = self.so[0]
            self.hD = self.so[1]
            base = self.sb(es, "hs", [128, 2048], F32)
            self.hs = Buf(base.t[:].rearrange("p (a b) -> p a b", b=512), base.name)
            self.hs.r = base.r
            self.xin = base
        if layer == 0:
            self.xin = self.sb(es, "xin", [128, 2048], F32)
        for tb in range(NBLK):
            if layer == L and tb == 8:
                continue
            v = 2 if tb == 8 else tb // 4
            c0 = tb * 512
            xT = self.xT
            if layer == 0:
                self.load_x_input(tb)
            else:
                T.dma("sp", xT.t[:], self.XT.t[:, c0:c0 + 512].rearrange("(k p) n -> p k n", p=128),
                      reads=[self.XT.k(tb)], writes=[xT])
                self.stage_c(layer - 1, tb, v)
            if layer < L:
                self.norm_mod(layer, 0, v)
                self.stage_a(layer, tb)
                T.dma("sp", self.XT.t[:, c0:c0 + 512].rearrange("(k p) n -> p k n", p=128), xT.t[:],
                      reads=[xT], writes=[self.XT.k(tb)])
            else:
                self.final_out(tb)

    def nextP(self):
        p = self.P[self.pn % 4]
        self.pn += 1
        return p

    def nextF(self):
        f = self.f32t[self.fn % 4]
        self.fn += 1
        return f

    def load_x_input(self, tb):
        T, I, P = self.T, self.I, self.P
        idf = self.c["ident_f"]
        for tt in range(4):
            if tb < 8:
                src = I["x"][tb // 4, (tb % 4) * 512 + tt * 128:(tb % 4) * 512 + (tt + 1) * 128, :]
            else:
                src = I["ctx"][tt // 2, (tt % 2) * 128:(tt % 2 + 1) * 128, :]
            T.dma("sp", self.xin.t[:], src, writes=[self.xin])
            for g in range(4):
                Pg = self.nextP()
                T.group("pe", [(lambda e, j=j: e.transpose(Pg.t[:, j * 128:(j + 1) * 128],
                                                           self.xin.t[:, (g * 4 + j) * 128:(g * 4 + j + 1) * 128], idf.t[:])) for j in range(4)],
                        reads=[self.xin, idf], writes=[Pg])
                T.op("act" if g % 2 else "dve",
                     (lambda e: e.activation(self.xT.t[:, g * 4:(g + 1) * 4, tt * 128:(tt + 1) * 128],
                                             Pg.t[:].rearrange("p (j t) -> p j t", t=128), AF.Copy)) if g % 2 else
                     (lambda e: e.tensor_copy(self.xT.t[:, g * 4:(g + 1) * 4, tt * 128:(tt + 1) * 128],
                                              Pg.t[:].rearrange("p (j t) -> p j t", t=128))),
                     reads=[Pg], writes=[self.xT])

    def rms_rstd(self, src_fn, nchunks, nfeat, reads):
        T, P = self.T, self.P
        Pss = P[4]
        of = self.c["ones_f"]
        for j in range(nchunks):
            sq = self.nextF()
            T.op("act", lambda e: e.activation(sq.t[:], src_fn(j), AF.Square), reads=reads, writes=[sq])
            T.op("pe", lambda e: e.matmul(Pss.t[:], of.t[:], sq.t[:], start=(j == 0), stop=(j == nchunks - 1)),
                 reads=[sq, of], writes=[Pss])
        T.op("act", lambda e: e.activation(self.rstd.t[:], Pss.t[:], AF.Ln, bias=EPS, scale=1.0 / nfeat), reads=[Pss], writes=[self.rstd])
        T.op("act", lambda e: e.activation(self.rstd.t[:], self.rstd.t[:], AF.Exp, scale=-0.5), reads=[self.rstd], writes=[self.rstd])

    def norm_mod(self, l, i, v):
        T = self.T
        xT, actT = self.xT, self.actT
        self.rms_rstd(lambda j: xT.t[:, j, :], 16, D, [xT])
        for dc in range(16):
            tmp = self.nextF()
            T.op("dve", lambda e: e.tensor_tensor(tmp.t[:], xT.t[:, dc, :], self.rstd.t[:], ALU.mult), reads=[xT, self.rstd], writes=[tmp])
            T.op("act", lambda e: e.activation(actT.t[:, dc, :], tmp.t[:], AF.Identity, bias=self.MODv(l, 3 * i, dc, v), scale=self.Sv(l, i, dc, v)),
                 reads=[tmp, self.c["MOD"], self.c["S"]], writes=[actT])

    def load_w(self, src, wc):
        w = self.wb[self.wn % 3]
        self.wn += 1
        k = src.shape[0] // 128
        view = w.t[:, 0:k * wc].rearrange("p (k n) -> p k n", n=wc)
        self.T.dma("pool", view, src.rearrange("(k p) n -> p k n", p=128), writes=[w])
        return w, view

    def proj_fm(self, wsrc, col0, ncols, evac, nk=16, rhs_fn=None):
        T = self.T
        if rhs_fn is None:
            rhs_fn = lambda kc: self.actT.t[:, kc, :]
            rd = [self.actT]
        else:
            rd = [self.aT]
        wcb = 512 if nk == 16 else 128
        ct = 0
        for b0 in range(0, ncols, wcb):
            wc = min(wcb, ncols - b0)
            w, view = self.load_w(wsrc[:, col0 + b0:col0 + b0 + wc], wc)
            for t0 in range(0, wc, 128):
                Pt = self.nextP()
                T.group("pe", [(lambda e, kc=kc: e.matmul(Pt.t[:], view[:, kc, t0:t0 + 128], rhs_fn(kc), start=(kc == 0), stop=(kc == nk - 1)))
                               for kc in range(nk)], reads=[w] + rd, writes=[Pt])
                evac(ct, Pt)
                ct += 1

    def proj_tm(self, wsrc, col0, ncols, evac):
        T = self.T
        for b0 in range(0, ncols, 512):
            wc = min(512, ncols - b0)
            w, view = self.load_w(wsrc[:, col0 + b0:col0 + b0 + wc], wc)
            for tt in range(4):
                Pt = self.nextP()
                T.group("pe", [(lambda e, kc=kc: e.matmul(Pt.t[:, 0:wc], self.actT.t[:, kc, tt * 128:(tt + 1) * 128], view[:, kc, :],
                                                          start=(kc == 0), stop=(kc == 15))) for kc in range(16)],
                        reads=[w, self.actT], writes=[Pt])
                evac(b0, wc, tt, Pt)

    class FmOut:
        def __init__(self, prog, row0, tb):
            self.p = prog; self.row0 = row0; self.tb = tb; self.n = 0; self.cur = None

        def slot(self):
            if self.n % 4 == 0:
                self.cur = self.p.so[self.p.son % 2]
                self.p.son += 1
            j = self.n % 4
            self.n += 1
            return self.cur, self.cur.t[:, j, :]

        def done_tile(self, last=False):
            if self.n % 4 == 0 or last:
                cnt = (self.n - 1) % 4 + 1
                r0 = self.row0 + (self.n - cnt) * 128
                p = self.p
                c0 = self.tb * 512
                p.T.dma("act", p.U_FM.t[r0:r0 + cnt * 128, c0:c0 + 512].rearrange("(t p) n -> p t n", p=128), self.cur.t[:, 0:cnt, :],
                        reads=[self.cur], writes=[p.U_FM.k((r0 // 128, self.tb))])

    def fm_group(self, wsrc, col0, ncols, row0, tb, evac_to):
        out = Prog.FmOut(self, row0, tb)
        ntile = ncols // 128

        def ev(ct, Pt):
            buf, dst = out.slot()
            evac_to(ct, Pt, buf, dst)
            out.done_tile(last=(ct == ntile - 1))
        self.proj_fm(wsrc, col0, ncols, ev)

    def conv_evac(self, tb, CW, CB, base_w, base_b, nchan_chunks):
        T = self.T
        seg = 256 if tb == 8 else 64

        def ev(ct, Pt, buf, dst):
            cv = self.nextF()
            w0 = CW.t[:, base_w + ct:base_w + ct + 1]
            w1 = CW.t[:, base_w + nchan_chunks + ct:base_w + nchan_chunks + ct + 1]
            w2 = CW.t[:, base_w + 2 * nchan_chunks + ct:base_w + 2 * nchan_chunks + ct + 1]
            bb = CB.t[:, base_b + ct:base_b + ct + 1]
            T.op("act", lambda e: e.activation(cv.t[:], Pt.t[:], AF.Identity, bias=bb, scale=w1), reads=[Pt, CW, CB], writes=[cv])
            cvv = cv.t[:].rearrange("p (s l) -> p s l", l=seg)
            pv = Pt.t[:].rearrange("p (s l) -> p s l", l=seg)
            T.op("dve", lambda e: e.scalar_tensor_tensor(cvv[:, :, 1:seg], pv[:, :, 0:seg - 1], w0, cvv[:, :, 1:seg], ALU.mult, ALU.add),
                 reads=[Pt, cv, CW], writes=[cv])
            T.op("dve", lambda e: e.scalar_tensor_tensor(cvv[:, :, 0:seg - 1], pv[:, :, 1:seg], w2, cvv[:, :, 0:seg - 1], ALU.mult, ALU.add),
                 reads=[Pt, cv, CW], writes=[cv])
            T.op("act", lambda e: e.activation(dst, cv.t[:], AF.Silu), reads=[cv], writes=[buf])
        return ev

    def act_evac(self, func):
        T = self.T

        def ev(ct, Pt, buf, dst):
            T.op("act", lambda e: e.activation(dst, Pt.t[:], func), reads=[Pt], writes=[buf])
        return ev

    def copy_evac(self):
        T = self.T

        def ev(ct, Pt, buf, dst):
            T.op("dve", lambda e: e.tensor_copy(dst, Pt.t[:]), reads=[Pt], writes=[buf])
        return ev

    def tm_evac(self, tb, dst_buf, col_off, dt):
        T = self.T
        c0 = tb * 512

        def ev(b0, wc, tt, Pt):
            if dt == BF16:
                st = self.so[self.son % 2]; self.son += 1
                sv = st.t[:, 0, 0:wc]
            else:
                st = self.nextF()
                sv = st.t[:, 0:wc]
            T.op("dve", lambda e: e.tensor_copy(sv, Pt.t[:, 0:wc]), reads=[Pt], writes=[st])
            r0 = c0 + tt * 128
            T.dma("act", dst_buf.t[r0:r0 + 128, col_off + b0:col_off + b0 + wc], sv, reads=[st],
                  writes=[dst_buf.k((r0 // 128, (col_off + b0) // 512))])
        return ev

    def stage_a(self, layer, tb):
        I, c = self.I, self.c
        odd, idx = self.kind(layer)
        if not odd:
            e_ = idx
            W = I["even_w_in"][e_]
            self.fm_group(W, 0, 1024, 0, tb, self.act_evac(AF.Silu))
            self.fm_group(W, 1024, 1536, 1024, tb, self.conv_evac(tb, c["SCW"], c["SCB"], e_ * 36, e_ * 12, 12))
            self.fm_group(W, 2592, 1024, 2560, tb, self.act_evac(AF.Silu))
            self.fm_group(W, 3616, 2048, 3584, tb, self.copy_evac())
            self.fm_group(W, 6688, 1024, 5632, tb, self.act_evac(AF.Silu))
            self.proj_tm(W, 5664, 1024, self.tm_evac(tb, self.U_TM, 0, BF16))
            self.proj_tm(W, 2560, 32, self.tm_evac(tb, self.U_G, 0, F32))
        else:
            o_ = idx
            W = I["odd_w_in"][o_]
            self.fm_group(W, 0, 2048, 0, tb, self.conv_evac(tb, c["MCW"], c["MCB"], o_ * 48, o_ * 16, 16))
            self.fm_group(W, 4096, 2048, 2048, tb, self.act_evac(AF.Sigmoid))
            self.proj_tm(W, 2048, 2048, self.tm_evac(tb, self.U_TM, 0, BF16))
            self.proj_tm(W, 6144, 16, self.tm_evac(tb, self.U_G, 0, F32))

    def xk(self, lst=None):
        return [self.xT.k(i) for i in (range(16) if lst is None else lst)]

    def stage_c(self, l, tb, v):
        T, c, I = self.T, self.c, self.I
        xT, aT = self.xT, self.aT
        self.finalize_mixer(l, tb)
        odd, idx = self.kind(l)
        W = I["odd_w_out"][idx] if odd else I["even_w_out"][idx]

        def ev_res(m):
            def ev(ct, Pt):
                T.op("dve", lambda e: e.scalar_tensor_tensor(xT.t[:, ct, :], Pt.t[:], self.MODv(l, m, ct, v), xT.t[:, ct, :], ALU.mult, ALU.add),
                     reads=[Pt, c["MOD"], xT], writes=[xT])
            return ev
        self.proj_fm(W, 0, 2048, ev_res(2))
        self.norm_mod(l, 1, v)

        def ev1(ct, Pt):
            tmp = self.nextF()
            T.op("act", lambda e: e.activation(tmp.t[:], Pt.t[:], AF.Relu), reads=[Pt], writes=[tmp])
            T.op("dve", lambda e: e.tensor_tensor(aT.t[:, ct, :], tmp.t[:], tmp.t[:], ALU.mult), reads=[tmp], writes=[aT])
        self.proj_fm(I["mlp_w1"][l], 0, 8192, ev1)
        self.proj_fm(I["mlp_w2"][l], 0, 2048, ev_res(5), nk=64, rhs_fn=lambda kc: aT.t[:, kc, :])

    def ld_blk(self, dst, src_buf, r0, tb):
        c0 = tb * 512
        self.T.dma("sp", dst.t[:], src_buf.t[r0:r0 + 512, c0:c0 + 512].rearrange("(k p) n -> p k n", p=128), writes=[dst])

    def finalize_mixer(self, l, tb):
        T, c = self.T, self.c
        hA, hB, hC, hD, hs, actT = self.hA, self.hB, self.hC, self.hD, self.hs, self.actT
        odd, idx = self.kind(l)
        for gI in range(4):
            r0 = gI * 512
            self.ld_blk(hA, self.HF[0], r0, tb)
            self.ld_blk(hB, self.HF[1], r0, tb)
            T.op("dve", lambda e: e.tensor_tensor(hs.t[:], hA.t[:], hB.t[:], ALU.add), reads=[hA, hB], writes=[hs])
            if odd:
                self.ld_blk(hC, self.U_FM, 2048 + r0, tb)
                self.rms_rstd(lambda j: hs.t[:, j, :], 4, 512, [hs])
                for j in range(4):
                    dc = gI * 4 + j
                    tmp = self.nextF()
                    T.op("dve", lambda e: e.tensor_tensor(tmp.t[:], hs.t[:, j, :], self.rstd.t[:], ALU.mult), reads=[hs, self.rstd], writes=[tmp])
                    T.op("dve", lambda e: e.scalar_tensor_tensor(actT.t[:, dc, :], tmp.t[:], c["MNW"].t[:, idx * 16 + dc:idx * 16 + dc + 1],
                                                                 hC.t[:, j, :], ALU.mult, ALU.mult), reads=[tmp, hC, c["MNW"]], writes=[actT])
            elif gI < 2:
                e_ = idx
                self.ld_blk(hC, self.U_FM, 1024 + r0, tb)
                self.ld_blk(hD, self.U_FM, r0, tb)
                for j in range(4):
                    dc = gI * 4 + j
                    T.op("dve", lambda e: e.scalar_tensor_tensor(hs.t[:, j, :], hC.t[:, j, :], c["DSK"].t[:, e_ * 8 + dc:e_ * 8 + dc + 1],
                                                                 hs.t[:, j, :], ALU.mult, ALU.add), reads=[hC, hs, c["DSK"]], writes=[hs])
                T.op("dve", lambda e: e.tensor_tensor(hs.t[:], hs.t[:], hD.t[:], ALU.mult), reads=[hs, hD], writes=[hs])
                self.rms_rstd(lambda j: hs.t[:, j, :], 4, 512, [hs])
                for j in range(4):
                    dc = gI * 4 + j
                    tmp = self.nextF()
                    T.op("dve", lambda e: e.tensor_tensor(tmp.t[:], hs.t[:, j, :], self.rstd.t[:], ALU.mult), reads=[hs, self.rstd], writes=[tmp])
                    T.op("act", lambda e: e.activation(actT.t[:, dc, :], tmp.t[:], AF.Copy, scale=c["SNW"].t[:, e_ * 8 + dc:e_ * 8 + dc + 1]),
                         reads=[tmp, c["SNW"]], writes=[actT])
            else:
                e_ = idx
                self.ld_blk(hD, self.U_FM, 5632 + (gI - 2) * 512, tb)
                for j in range(4):
                    dc = gI * 4 + j
                    self.rms_rstd(lambda _: hs.t[:, j, :], 1, 128, [hs])
                    tmp = self.nextF()
                    T.op("dve", lambda e: e.tensor_tensor(tmp.t[:], hs.t[:, j, :], self.rstd.t[:], ALU.mult), reads=[hs, self.rstd], writes=[tmp])
                    T.op("dve", lambda e: e.scalar_tensor_tensor(actT.t[:, dc, :], tmp.t[:], c["HNW"].t[:, e_ * 8 + dc - 8:e_ * 8 + dc - 7],
                                                                 hD.t[:, j, :], ALU.mult, ALU.mult), reads=[tmp, hD, c["HNW"]], writes=[actT])

    def final_out(self, tb):
        T, c = self.T, self.c
        xT = self.xT
        idf = c["ident_f"]
        self.rms_rstd(lambda j: xT.t[:, j, :], 16, D, [xT])
        for dc in range(16):
            tmp = self.nextF()
            T.op("dve", lambda e: e.tensor_tensor(tmp.t[:], xT.t[:, dc, :], self.rstd.t[:], ALU.mult), reads=[xT, self.rstd], writes=[tmp])
            T.op("act", lambda e: e.activation(xT.t[:, dc, :], tmp.t[:], AF.Copy, scale=c["FNW"].t[:, dc:dc + 1]), reads=[tmp, c["FNW"]], writes=[xT])
        for tt in range(4):
            for g in range(4):
                Pg = self.nextP()
                T.group("pe", [(lambda e, j=j: e.transpose(Pg.t[:, j * 128:(j + 1) * 128], xT.t[:, g * 4 + j, tt * 128:(tt + 1) * 128], idf.t[:]))
                               for j in range(4)], reads=[xT, idf], writes=[Pg])
                if g % 2:
                    T.op("act", lambda e: e.activation(self.xin.t[:, g * 512:(g + 1) * 512], Pg.t[:], AF.Copy), reads=[Pg], writes=[self.xin])
                else:
                    T.op("dve", lambda e: e.tensor_copy(self.xin.t[:, g * 512:(g + 1) * 512], Pg.t[:]), reads=[Pg], writes=[self.xin])
            r0 = (tb % 4) * 512 + tt * 128
            T.dma("sp", self.out[tb // 4, r0:r0 + 128, :], self.xin.t[:], reads=[self.xin], writes=[self.r_out])

    def mixer_phase(self, es, layer):
        self.dslot = 0
        odd, idx = self.kind(layer)
        self.decay_banks = not odd
        if odd:
            self.mlstm_phase(es, idx)
        else:
            self.even_phase(es, idx)

    def trps(self, i):
        return self.P[i].t[:].bitcast(BF16)

    def gate_cums(self, d, src, n, ws, pres=None, bank=7):
        T, c, P = self.T, self.c, self.P
        Pg = P[bank]
        pr = Pg if pres is None else pres
        T.op("pe", lambda e: e.matmul(Pg.t[:, 256:256 + n], c[f"tri{d}"].t[:], src.t[:, 0:n], start=True, stop=True),
             reads=[src, c[f"tri{d}"]], writes=[pr])
        T.op("pe", lambda e: e.matmul(Pg.t[:, 256 + n:256 + 2 * n], c["ones_f"].t[:], src.t[:, 0:n], start=True, stop=True),
             reads=[src, c["ones_f"]], writes=[pr])
        T.op("dve", lambda e: e.tensor_copy(ws["cs"].t[:, 0:2 * n], Pg.t[:, 256:256 + 2 * n]), reads=[pr], writes=[ws["cs"]])
        return ws["cs"]

    def decay_mats(self, d, src, h, bias_ap, ws, need_e=True, place=None):
        T, c, P = self.T, self.c, self.P
        slot = self.dslot % 2
        if place is None:
            self.dslot += 1
        if place is not None:
            slot, PA, b0 = place
            pr = PA
        elif getattr(self, "decay_banks", False):
            PA = P[1] if slot == 0 else P[0]
            pr = PA
            b0 = 0 if slot == 0 else 256
        else:
            PA = P[1]
            pr = PA.k(slot)
            b0 = slot * 256
        Dm, E = ws["Dm"][slot], ws["E"][slot]
        bc = src.t[:, h:h + 1].to_broadcast([128, 128])
        fns = [lambda e: e.matmul(PA.t[:, b0 + 128:b0 + 256], bc, c[f"tri{d}"].t[:], start=True, stop=False),
               lambda e: e.matmul(PA.t[:, b0 + 128:b0 + 256], c["ident_f"].t[:], c[f"msk{d}"].t[:], start=False, stop=True)]
        if need_e:
            fns.insert(0, lambda e: e.matmul(PA.t[:, b0:b0 + 128], bc, c[f"tri{d}"].t[:], start=True, stop=True))
        T.group("pe", fns, reads=[src, c[f"tri{d}"], c[f"msk{d}"], c["ident_f"]], writes=[pr])
        T.op("act", lambda e: e.activation(Dm.t[:], PA.t[:, b0 + 128:b0 + 256], AF.Exp, bias=bias_ap, scale=-1.0),
             reads=[pr, ws["bias"]], writes=[Dm])
        if need_e:
            T.op("act", lambda e: e.activation(E.t[:], PA.t[:, b0:b0 + 128], AF.Exp, scale=-1.0), reads=[pr], writes=[E])
        return Dm, E

    def mlstm_phase(self, es, o_):
        T, c, P = self.T, self.c, self.P
        chains = [(b, d) for b in range(2) for d in range(2)]
        Cf = {ch: self.sb(es, "Cf", [128, 4, 2, 640], F32) for ch in chains}
        Cb = {ch: self.sb(es, "Cb", [128, 4, 2, 640], BF16) for ch in chains}
        for ch in chains:
            T.op("pool", lambda e: e.memset(Cf[ch].t[:], 0.0), writes=[Cf[ch]])
            T.op("pool", lambda e: e.memset(Cb[ch].t[:], 0.0), writes=[Cb[ch]])
        sets = []
        for i in range(2):
            w = {}
            w["qT"] = self.sb(es, "qT", [128, 8, 128], BF16)
            w["kT"] = self.sb(es, "kT", [128, 8, 128], BF16)
            w["Vt"] = self.sb(es, "Vt", [128, 2048], BF16)
            w["g"] = self.sb(es, "g", [128, 16], F32)
            w["li"] = self.sb(es, "li", [128, 4], F32)
            w["sp"] = self.sb(es, "sp", [128, 4], F32)
            w["cs"] = self.sb(es, "cs", [128, 8], F32)
            w["bias"] = self.sb(es, "bias", [128, 4], F32)
            w["wsx"] = self.sb(es, "wsx", [128, 4], F32)
            w["cdec"] = self.sb(es, "cdec", [128, 4], F32)
            w["Dm"] = [self.sb(es, "Dm", [128, 128], F32) for _ in range(2)]
            w["E"] = [self.sb(es, "E", [128, 128], F32) for _ in range(2)]
            w["WT"] = [self.sb(es, "WT", [128, 128], BF16) for _ in range(2)]
            w["qs"] = [self.sb(es, "qs", [128, 2, 128], BF16) for _ in range(2)]
            w["ke"] = [self.sb(es, "ke", [128, 256], BF16) for _ in range(2)]
            w["dd"] = [self.sb(es, "dd", [128, 128], F32) for _ in range(2)]
            w["ho"] = self.sb(es, "ho", [128, 16, 128], BF16)
            sets.append(w)
        GB = c["GB"]
        n = 0
        self.hslot = 0
        for step in range(18):
            for ch in chains:
                b, d = ch
                c0 = chain_chunks(b, d)[step]
                w = sets[n % 2]; n += 1
                T.dma("sp", w["qT"].t[:], self.U_FM.t[0:1024, c0:c0 + 128].rearrange("(k p) n -> p k n", p=128), writes=[w["qT"]])
                T.dma("sp", w["kT"].t[:], self.U_FM.t[1024:2048, c0:c0 + 128].rearrange("(k p) n -> p k n", p=128), writes=[w["kT"]])
                T.dma("sp", w["Vt"].t[:], self.U_TM.t[c0:c0 + 128, 0:2048], writes=[w["Vt"]])
                T.dma("sp", w["g"].t[:], self.U_G.t[c0:c0 + 128, 0:16], writes=[w["g"]])
                g, li, sp = w["g"], w["li"], w["sp"]
                T.op("dve", lambda e: e.tensor_tensor(li.t[:], g.t[:, d * 4:d * 4 + 4], GB.t[:, o_ * 16 + d * 4:o_ * 16 + d * 4 + 4], ALU.add),
                     reads=[g, GB], writes=[li])
                T.op("dve", lambda e: e.tensor_tensor(sp.t[:], g.t[:, 8 + d * 4:12 + d * 4], GB.t[:, o_ * 16 + 8 + d * 4:o_ * 16 + 12 + d * 4], ALU.add),
                     reads=[g, GB], writes=[sp])
                T.op("act", lambda e: e.activation(sp.t[:], sp.t[:], AF.Exp, scale=-1.0), reads=[sp], writes=[sp])
                T.op("act", lambda e: e.activation(sp.t[:], sp.t[:], AF.Ln, bias=1.0), reads=[sp], writes=[sp])
                cs = self.gate_cums(d, sp, 4, w, bank=0)
                bias, wsx, cdec = w["bias"], w["wsx"], w["cdec"]
                T.op("dve", lambda e: e.tensor_tensor(bias.t[:], li.t[:], cs.t[:, 0:4], ALU.add), reads=[li, cs], writes=[bias])
                T.op("dve", lambda e: e.tensor_tensor(wsx.t[:], bias.t[:], cs.t[:, 4:8], ALU.subtract), reads=[bias, cs], writes=[wsx])
                T.op("act", lambda e: e.activation(wsx.t[:], wsx.t[:], AF.Exp), reads=[wsx], writes=[wsx])
                T.op("act", lambda e: e.activation(cdec.t[:], cs.t[:, 4:8], AF.Exp, scale=-1.0), reads=[cs], writes=[cdec])
                qT, kT, Vt, ho = w["qT"], w["kT"], w["Vt"], w["ho"]
                cf, cb = Cf[ch], Cb[ch]
                hst = {}

                def F(h):
                        hs_ = self.hslot % 2
                        self.hslot += 1
                        WT, qs, ke, dd = w["WT"][hs_], w["qs"][hs_], w["ke"][hs_], w["dd"][hs_]
                        hst[h] = (hs_, WT, qs, ke, dd)
                        PX, PY_ = P[hs_], P[2 + hs_]
                        T.group("pe", [(lambda e, j=j: e.matmul(PX.t[:, 0:128], kT.t[:, h * 2 + j, :], qT.t[:, h * 2 + j, :], start=(j == 0), stop=(j == 1)))
                                       for j in range(2)], reads=[kT, qT], writes=[PX])
                        trv = self.trps(2 + hs_)[:, 0:256]
                        T.group("pe", [(lambda e, j=j: e.transpose(trv[:, j * 128:(j + 1) * 128], kT.t[:, h * 2 + j, :], c["ident_b"].t[:])) for j in range(2)],
                                reads=[kT, c["ident_b"]], writes=[PY_])
                        Dm, E = self.decay_mats(d, sp, h, bias.t[:, h:h + 1], w, place=(hs_, PY_, 256))
                        T.op("dve", lambda e: e.tensor_tensor(WT.t[:], PX.t[:, 0:128], Dm.t[:], ALU.mult), reads=[PX, Dm], writes=[WT])
                        T.op("dve", lambda e: e.tensor_tensor(qs.t[:], qT.t[:, h * 2:h * 2 + 2, :], E.t[:].unsqueeze(1).to_broadcast([128, 2, 128]), ALU.mult),
                             reads=[qT, E], writes=[qs])
                        T.op("act", lambda e: e.activation(ke.t[:], trv[:, 0:256], AF.Copy, scale=wsx.t[:, h:h + 1]), reads=[PY_, wsx], writes=[ke])

                def B(h):
                        hs_, WT, qs, ke, dd = hst[h]
                        PN, PD = P[4], P[5]
                        fns = []
                        for vc in range(5):
                            if vc < 4:
                                dst = PN.t[:, vc * 128:(vc + 1) * 128]
                                l0 = Vt.t[:, h * 512 + vc * 128:h * 512 + (vc + 1) * 128]
                            else:
                                dst = PD.t[:, 0:128]
                                l0 = c["ones_b"].t[:]
                            fns.append(lambda e, dst=dst, l0=l0: e.matmul(dst, l0, WT.t[:], start=True, stop=False))
                            for j in range(2):
                                fns.append(lambda e, dst=dst, j=j, vc=vc: e.matmul(dst, cb.t[:, h, j, vc * 128:(vc + 1) * 128], qs.t[:, j, :], start=False, stop=(j == 1)))
                        T.group("pe", fns, reads=[Vt, WT, cb, qs, c["ones_b"]], writes=[PN, PD])
                        T.op("act", lambda e: e.activation(dd.t[:], PD.t[:, 0:128], AF.Abs), reads=[PD], writes=[dd])
                        T.op("dve", lambda e: e.tensor_scalar_max(dd.t[:], dd.t[:], 1.0), reads=[dd], writes=[dd])
                        T.op("dve", lambda e: e.reciprocal(dd.t[:], dd.t[:]), reads=[dd], writes=[dd])
                        T.op("dve", lambda e: e.tensor_tensor(ho.t[:, h * 4:(h + 1) * 4, :], PN.t[:].rearrange("p (v t) -> p v t", t=128),
                                                              dd.t[:].unsqueeze(1).to_broadcast([128, 4, 128]), ALU.mult), reads=[PN, dd], writes=[ho])
                        for j in range(2):
                            Pc = P[6 + j]
                            T.op("pe", lambda e: e.matmul(Pc.t[:], ke.t[:, j * 128:(j + 1) * 128], Vt.t[:, h * 512:(h + 1) * 512], start=True, stop=True),
                                 reads=[ke, Vt], writes=[Pc])
                            T.op("dve", lambda e: e.scalar_tensor_tensor(cf.t[:, h, j, 0:512], cf.t[:, h, j, 0:512], cdec.t[:, h:h + 1], Pc.t[:], ALU.mult, ALU.add),
                                 reads=[cf, cdec, Pc], writes=[cf])
                        Pn = P[hs_]
                        T.group("pe", [(lambda e, j=j: e.matmul(Pn.t[:, 256 + j * 128:256 + (j + 1) * 128], ke.t[:, j * 128:(j + 1) * 128], c["ones_b"].t[:], start=True, stop=True))
                                       for j in range(2)], reads=[ke, c["ones_b"]], writes=[Pn])
                        T.op("dve", lambda e: e.scalar_tensor_tensor(cf.t[:, h, :, 512:640], cf.t[:, h, :, 512:640], cdec.t[:, h:h + 1],
                                                                     Pn.t[:, 256:512].rearrange("p (j n) -> p j n", n=128), ALU.mult, ALU.add),
                             reads=[cf, cdec, Pn], writes=[cf])
                        T.op("act", lambda e: e.activation(cb.t[:, h], cf.t[:, h], AF.Copy), reads=[cf], writes=[cb])

                F(0)
                for h in range(4):
                    if h + 1 < 4:
                        F(h + 1)
                    B(h)
                T.dma("act", self.HF[d].t[:, c0:c0 + 128].rearrange("(k p) n -> p k n", p=128), ho.t[:], reads=[ho], writes=[self.HF[d].k(c0 // 128)])

    def even_phase(self, es, e_):
        T, c, P = self.T, self.c, self.P
        chains = [(b, d) for b in range(2) for d in range(2)]
        rst = self.sb(es, "rst", [128, 1024], F32)
        T.op("pool", lambda e: e.memset(rst.t[:], 1.0), writes=[rst])
        T.op("pool", lambda e: e.memset(rst.t[:].rearrange("p (c t) -> p c t", t=128)[:, :, 0:1], 0.0), reads=[rst], writes=[rst])
        Hf = {ch: self.sb(es, "Hf", [128, 1024], F32) for ch in chains}
        Hb = {ch: self.sb(es, "Hb", [128, 1024], BF16) for ch in chains}
        Sf = {ch: self.sb(es, "Sf", [128, 8, 128], F32) for ch in chains}
        Sb = {ch: self.sb(es, "Sb", [128, 8, 128], BF16) for ch in chains}
        for ch in chains:
            for t in (Hf[ch], Hb[ch], Sf[ch], Sb[ch]):
                T.op("pool", lambda e: e.memset(t.t[:], 0.0), writes=[t])
        KI = {}
        for d in range(2):
            KI[d] = [self.sb(es, f"KI{d}_{i}", [128, 8, 128], BF16) for i in range(8)]
            for t in KI[d]:
                T.op("pool", lambda e: e.memset(t.t[:], 0.0), writes=[t])
        sets = []
        for i in range(2):
            w = {}
            def mk(name, shape, dt):
                w[name] = self.sb(es, name, shape, dt)
            mk("BCT", [128, 4, 128], BF16); mk("xsT", [128, 8, 128], BF16); mk("dtr", [128, 32], F32)
            mk("dt", [128, 16], F32); mk("na", [128, 16], F32); mk("cs", [128, 32], F32); mk("wsx", [128, 16], F32)
            mk("cdec", [128, 16], F32); mk("xdt", [128, 1024], BF16); mk("xdtw", [128, 1024], BF16); mk("Btm", [128, 256], BF16)
            mk("G", [128, 256], F32); mk("ysb", [128, 1024], BF16); mk("hos", [128, 8, 128], BF16)
            for nm, dt_ in (("Dm", F32), ("E", F32), ("MT", BF16), ("CsT", BF16), ("attm", BF16)):
                w[nm] = [self.sb(es, nm, [128, 128], dt_) for _ in range(2)]
            w["bias"] = w["cs"]
            mk("qT", [128, 8, 128], BF16); mk("fT", [128, 8, 128], BF16); mk("Vt", [128, 1024], BF16)
            mk("A", [128, 8, 128], F32); mk("KIN", [128, 8, 128], F32); mk("LG", [128, 8, 128], F32); mk("BC", [128, 8, 128], F32)
            mk("TMP", [128, 8, 128], F32); mk("tot", [128, 8], F32); mk("hdec", [128, 8], F32); mk("CST", [128, 8, 8], F32)
            mk("qs", [128, 8, 128], BF16); mk("keT", [128, 8, 128], BF16); mk("ketm", [128, 1024], BF16); mk("qloc", [128, 8, 128], BF16)
            mk("hoh", [128, 8, 128], BF16)
            sets.append(w)
        n = 0
        for step in range(18):
            for ch in chains:
                b, d = ch
                c0 = chain_chunks(b, d)[step]
                w = sets[n % 2]; n += 1
                self.ssd_step(e_, d, c0, w, Hf[ch], Hb[ch])
                self.hgrn_step(e_, d, c0, w, Sf[ch], Sb[ch], KI[d], rst)

    def ssd_step(self, e_, d, c0, w, hf, hb):
        T, c, P = self.T, self.c, self.P
        BCT, xsT, dtr, dt, na, wsx, cdec = w["BCT"], w["xsT"], w["dtr"], w["dt"], w["na"], w["wsx"], w["cdec"]
        xdt, xdtw, Btm, G, ysb, hos = w["xdt"], w["xdtw"], w["Btm"], w["G"], w["ysb"], w["hos"]
        T.dma("sp", BCT.t[:], self.U_FM.t[2048:2560, c0:c0 + 128].rearrange("(k p) n -> p k n", p=128), writes=[BCT])
        T.dma("sp", xsT.t[:], self.U_FM.t[1024:2048, c0:c0 + 128].rearrange("(k p) n -> p k n", p=128), writes=[xsT])
        T.dma("sp", dtr.t[:], self.U_G.t[c0:c0 + 128, 0:32], writes=[dtr])
        o = e_ * 32 + d * 16
        T.op("dve", lambda e: e.tensor_tensor(dt.t[:], dtr.t[:, d * 16:(d + 1) * 16], c["DTB"].t[:, o:o + 16], ALU.add), reads=[dtr, c["DTB"]], writes=[dt])
        T.op("act", lambda e: e.activation(dt.t[:], dt.t[:], AF.Exp), reads=[dt], writes=[dt])
        T.op("act", lambda e: e.activation(dt.t[:], dt.t[:], AF.Ln, bias=1.0), reads=[dt], writes=[dt])
        T.op("dve", lambda e: e.tensor_tensor(na.t[:], dt.t[:], c["EA"].t[:, o:o + 16], ALU.mult), reads=[dt, c["EA"]], writes=[na])
        cs = self.gate_cums(d, na, 16, w)
        T.op("dve", lambda e: e.tensor_tensor(wsx.t[:], cs.t[:, 16:32], cs.t[:, 0:16], ALU.subtract), reads=[cs], writes=[wsx])
        T.op("act", lambda e: e.activation(wsx.t[:], wsx.t[:], AF.Exp, scale=-1.0), reads=[wsx], writes=[wsx])
        T.op("act", lambda e: e.activation(cdec.t[:], cs.t[:, 16:32], AF.Exp, scale=-1.0), reads=[cs], writes=[cdec])
        tr2 = self.trps(2)
        T.group("pe", [(lambda e, j=j: e.transpose(tr2[:, j * 128:(j + 1) * 128], xsT.t[:, j, :], c["ident_b"].t[:])) for j in range(8)],
                reads=[xsT, c["ident_b"]], writes=[P[2]])
        T.op("dve", lambda e: e.tensor_tensor(xdt.t[:].rearrange("p (h q) -> p h q", q=64), tr2[:, 0:1024].rearrange("p (h q) -> p h q", q=64),
                                              dt.t[:].unsqueeze(2).to_broadcast([128, 16, 64]), ALU.mult), reads=[P[2], dt], writes=[xdt])
        T.op("dve", lambda e: e.tensor_tensor(xdtw.t[:].rearrange("p (h q) -> p h q", q=64), xdt.t[:].rearrange("p (h q) -> p h q", q=64),
                                              wsx.t[:].unsqueeze(2).to_broadcast([128, 16, 64]), ALU.mult), reads=[xdt, wsx], writes=[xdtw])
        tr7 = self.trps(7)
        T.group("pe", [(lambda e, g=g: e.transpose(tr7[:, g * 128:(g + 1) * 128], BCT.t[:, g, :], c["ident_b"].t[:])) for g in range(2)],
                reads=[BCT, c["ident_b"]], writes=[P[7]])
        T.op("act", lambda e: e.activation(Btm.t[:], tr7[:, 0:256], AF.Copy), reads=[P[7]], writes=[Btm])
        T.group("pe", [(lambda e, g=g: e.matmul(P[0].t[:, g * 128:(g + 1) * 128], BCT.t[:, g, :], BCT.t[:, 2 + g, :], start=True, stop=True)) for g in range(2)],
                reads=[BCT], writes=[P[0]])
        T.op("act", lambda e: e.activation(G.t[:], P[0].t[:, 0:256], AF.Copy), reads=[P[0]], writes=[G])
        def F(h):
            g = h // 8
            Dm, E = self.decay_mats(d, na, h, cs.t[:, h:h + 1], w)
            MT, CsT = w["MT"][h % 2], w["CsT"][h % 2]
            T.op("dve", lambda e: e.tensor_tensor(MT.t[:], G.t[:, g * 128:(g + 1) * 128], Dm.t[:], ALU.mult), reads=[G, Dm], writes=[MT])
            T.op("dve", lambda e: e.tensor_tensor(CsT.t[:], BCT.t[:, 2 + g, :], E.t[:], ALU.mult), reads=[BCT, E], writes=[CsT])

        def B(h):
            MT, CsT = w["MT"][h % 2], w["CsT"][h % 2]
            PY = P[3 + h // 8]
            dst = PY.t[:, (h % 8) * 64:(h % 8 + 1) * 64]
            T.group("pe", [lambda e: e.matmul(dst, MT.t[:], xdt.t[:, h * 64:(h + 1) * 64], start=True, stop=False),
                           lambda e: e.matmul(dst, CsT.t[:], hb.t[:, h * 64:(h + 1) * 64], start=False, stop=True)],
                    reads=[MT, xdt, CsT, hb], writes=[PY])
        F(0)
        for h in range(16):
            if h + 1 < 16:
                F(h + 1)
            B(h)
        T.op("act", lambda e: e.activation(ysb.t[:, 0:512], P[3].t[:], AF.Copy), reads=[P[3]], writes=[ysb])
        T.op("dve", lambda e: e.tensor_copy(ysb.t[:, 512:1024], P[4].t[:]), reads=[P[4]], writes=[ysb])
        for g in range(2):
            Pc = P[5 + g]
            T.op("pe", lambda e: e.matmul(Pc.t[:], Btm.t[:, g * 128:(g + 1) * 128], xdtw.t[:, g * 512:(g + 1) * 512], start=True, stop=True),
                 reads=[Btm, xdtw], writes=[Pc])
            hv = hf.t[:, g * 512:(g + 1) * 512].rearrange("p (h q) -> p h q", q=64)
            T.op("dve", lambda e: e.tensor_tensor(hv, hv, cdec.t[:, g * 8:(g + 1) * 8].unsqueeze(2).to_broadcast([128, 8, 64]), ALU.mult),
                 reads=[hf, cdec], writes=[hf])
            T.op("dve", lambda e: e.tensor_tensor(hf.t[:, g * 512:(g + 1) * 512], hf.t[:, g * 512:(g + 1) * 512], Pc.t[:], ALU.add), reads=[hf, Pc], writes=[hf])
        T.op("act", lambda e: e.activation(hb.t[:], hf.t[:], AF.Copy), reads=[hf], writes=[hb])
        T.group("pe", [(lambda e, j=j: e.transpose(tr2[:, j * 128:(j + 1) * 128], ysb.t[:, j * 128:(j + 1) * 128], c["ident_b"].t[:])) for j in range(8)],
                reads=[ysb, c["ident_b"]], writes=[P[2]])
        T.op("act", lambda e: e.activation(hos.t[:], tr2[:, 0:1024].rearrange("p (k t) -> p k t", t=128), AF.Copy), reads=[P[2]], writes=[hos])
        T.dma("act", self.HF[d].t[0:1024, c0:c0 + 128].rearrange("(k p) n -> p k n", p=128), hos.t[:], reads=[hos], writes=[self.HF[d].k((0, c0 // 128))])

    def hgrn_step(self, e_, d, c0, w, sf, sb, KI, rst):
        T, c, P = self.T, self.c, self.P
        qT, fT, Vt, A, KIN, LG, BC, TMP = w["qT"], w["fT"], w["Vt"], w["A"], w["KIN"], w["LG"], w["BC"], w["TMP"]
        tot, hdec, CST, qs, keT, ketm, qloc, hoh = w["tot"], w["hdec"], w["CST"], w["qs"], w["keT"], w["ketm"], w["qloc"], w["hoh"]
        T.dma("sp", qT.t[:], self.U_FM.t[2560:3584, c0:c0 + 128].rearrange("(k p) n -> p k n", p=128), writes=[qT])
        T.dma("sp", fT.t[:], self.U_FM.t[3584 + d * 1024:3584 + (d + 1) * 1024, c0:c0 + 128].rearrange("(k p) n -> p k n", p=128), writes=[fT])
        T.dma("sp", Vt.t[:], self.U_TM.t[c0:c0 + 128, 0:1024], writes=[Vt])
        lb = c["LB"].t[:, e_ * 8:(e_ + 1) * 8].unsqueeze(2).to_broadcast([128, 8, 128])
        lb1 = c["LB1"].t[:, e_ * 8:(e_ + 1) * 8].unsqueeze(2).to_broadcast([128, 8, 128])
        T.op("act", lambda e: e.activation(A.t[:], fT.t[:], AF.Sigmoid), reads=[fT], writes=[A])
        T.op("dve", lambda e: e.tensor_tensor(A.t[:], A.t[:], lb1, ALU.mult), reads=[A, c["LB1"]], writes=[A])
        T.op("dve", lambda e: e.tensor_tensor(A.t[:], A.t[:], lb, ALU.add), reads=[A, c["LB"]], writes=[A])
        T.op("dve", lambda e: e.tensor_scalar(KIN.t[:], A.t[:], -1.0, 1.0, ALU.mult, ALU.add), reads=[A], writes=[KIN])
        T.op("act", lambda e: e.activation(LG.t[:], A.t[:], AF.Ln), reads=[A], writes=[LG])
        fl = lambda t: t.t[:].rearrange("p h t -> p (h t)")
        T.op("dve", lambda e: e.tensor_tensor_scan(fl(BC), rst.t[:], fl(LG), 0.0, ALU.mult, ALU.add), reads=[rst, LG], writes=[BC])
        T.op("dve", lambda e: e.tensor_copy(tot.t[:].unsqueeze(2), BC.t[:, :, 127:128]), reads=[BC], writes=[tot])
        totb = tot.t[:].unsqueeze(2).to_broadcast([128, 8, 128])
        if d == 1:
            T.op("dve", lambda e: e.tensor_tensor(BC.t[:], LG.t[:], BC.t[:], ALU.subtract), reads=[LG, BC], writes=[BC])
            T.op("dve", lambda e: e.tensor_tensor(BC.t[:], BC.t[:], totb, ALU.add), reads=[BC, tot], writes=[BC])
        T.op("act", lambda e: e.activation(TMP.t[:], BC.t[:], AF.Exp), reads=[BC], writes=[TMP])
        T.op("dve", lambda e: e.tensor_tensor(qs.t[:], qT.t[:], TMP.t[:], ALU.mult), reads=[qT, TMP], writes=[qs])
        T.op("dve", lambda e: e.tensor_tensor(TMP.t[:], totb, BC.t[:], ALU.subtract), reads=[tot, BC, TMP], writes=[TMP])
        T.op("act", lambda e: e.activation(TMP.t[:], TMP.t[:], AF.Exp), reads=[TMP], writes=[TMP])
        T.op("dve", lambda e: e.tensor_tensor(keT.t[:], TMP.t[:], KIN.t[:], ALU.mult), reads=[TMP, KIN], writes=[keT])
        T.op("act", lambda e: e.activation(hdec.t[:], tot.t[:], AF.Exp), reads=[tot], writes=[hdec])
        tr2 = self.trps(2)
        T.group("pe", [(lambda e, h=h: e.transpose(tr2[:, h * 128:(h + 1) * 128], keT.t[:, h, :], c["ident_b"].t[:])) for h in range(8)],
                reads=[keT, c["ident_b"]], writes=[P[2]])
        T.op("act", lambda e: e.activation(ketm.t[:], tr2[:, 0:1024], AF.Copy), reads=[P[2]], writes=[ketm])
        if d == 0:
            T.op("pool", lambda e: e.memset(CST.t[:, :, 0:1], 0.0), writes=[CST])
            T.op("dve", lambda e: e.tensor_copy(CST.t[:, :, 1:8], BC.t[:, :, 15:127:16]), reads=[BC, CST], writes=[CST])
        else:
            T.op("pool", lambda e: e.memset(CST.t[:, :, 7:8], 0.0), writes=[CST])
            T.op("dve", lambda e: e.tensor_copy(CST.t[:, :, 0:7], BC.t[:, :, 16:128:16]), reads=[BC, CST], writes=[CST])
        v64 = lambda t: t.t[:].rearrange("p h (i u) -> p (h i) u", u=16)
        T.op("dve", lambda e: e.tensor_tensor(v64(TMP), v64(BC), CST.t[:].rearrange("p h i -> p (h i)").unsqueeze(2).to_broadcast([128, 64, 16]), ALU.subtract),
             reads=[BC, CST, TMP], writes=[TMP])
        T.op("act", lambda e: e.activation(TMP.t[:], TMP.t[:], AF.Exp), reads=[TMP], writes=[TMP])
        T.op("dve", lambda e: e.tensor_tensor(qloc.t[:], qT.t[:], TMP.t[:], ALU.mult), reads=[qT, TMP], writes=[qloc])
        for i in range(8):
            lo, hi = (0, 16 * (i + 1)) if d == 0 else (16 * i, 128)
            Ki = KI[i]
            T.op("dve", lambda e: e.tensor_tensor(TMP.t[:, :, lo:hi], CST.t[:, :, i:i + 1].to_broadcast([128, 8, hi - lo]), BC.t[:, :, lo:hi], ALU.subtract),
                 reads=[CST, BC, TMP], writes=[TMP])
            T.op("act", lambda e: e.activation(TMP.t[:, :, lo:hi], TMP.t[:, :, lo:hi], AF.Exp), reads=[TMP], writes=[TMP])
            T.op("dve", lambda e: e.tensor_tensor(Ki.t[:, :, lo:hi], TMP.t[:, :, lo:hi], KIN.t[:, :, lo:hi], ALU.mult), reads=[TMP, KIN], writes=[Ki])
        def F(h):
            PA = P[h % 2]
            prA = PA
            a0 = 0
            attm = w["attm"][h % 2]
            T.group("pe", [(lambda e, i=i: e.matmul(PA.t[:, a0 + 16 * i:a0 + 16 * i + 16], KI[i].t[:, h, :], qloc.t[:, h, 16 * i:16 * i + 16], start=True, stop=True))
                           for i in range(8)], reads=KI + [qloc], writes=[prA])
            T.op("dve", lambda e: e.tensor_tensor(attm.t[:], PA.t[:, a0:a0 + 128], c[f"tri{d}"].t[:], ALU.mult), reads=[prA, c[f"tri{d}"]], writes=[attm])

        def B(h):
            attm = w["attm"][h % 2]
            PO = P[3 + h // 4]
            dst = PO.t[:, (h % 4) * 128:(h % 4 + 1) * 128]
            T.group("pe", [lambda e: e.matmul(dst, Vt.t[:, h * 128:(h + 1) * 128], attm.t[:], start=True, stop=False),
                           lambda e: e.matmul(dst, sb.t[:, h, :], qs.t[:, h, :], start=False, stop=True)],
                    reads=[Vt, attm, sb, qs], writes=[PO])
            Pc = P[5 + h % 2]
            T.op("pe", lambda e: e.matmul(Pc.t[:, 0:128], ketm.t[:, h * 128:(h + 1) * 128], Vt.t[:, h * 128:(h + 1) * 128], start=True, stop=True),
                 reads=[ketm, Vt], writes=[Pc])
            T.op("dve", lambda e: e.scalar_tensor_tensor(sf.t[:, h, :], sf.t[:, h, :], hdec.t[:, h:h + 1], Pc.t[:, 0:128], ALU.mult, ALU.add),
                 reads=[sf, hdec, Pc], writes=[sf])
        F(0)
        for h in range(8):
            if h + 1 < 8:
                F(h + 1)
            B(h)
        T.op("act", lambda e: e.activation(sb.t[:], sf.t[:], AF.Copy), reads=[sf], writes=[sb])
        T.op("act", lambda e: e.activation(hoh.t[:, 0:4, :], P[3].t[:].rearrange("p (h t) -> p h t", t=128), AF.Copy), reads=[P[3]], writes=[hoh])
        T.op("dve", lambda e: e.tensor_copy(hoh.t[:, 4:8, :], P[4].t[:].rearrange("p (h t) -> p h t", t=128)), reads=[P[4]], writes=[hoh])
        T.dma("act", self.HF[d].t[1024:2048, c0:c0 + 128].rearrange("(k p) n -> p k n", p=128), hoh.t[:], reads=[hoh], writes=[self.HF[d].k((1, c0 // 128))])


_W_NAMES = ["mod_w", "mod_b", "norm_w", "final_norm_w", "mlp_w1", "mlp_w2", "even_w_in", "even_w_out",
            "ssd_conv_w", "ssd_conv_b", "ssd_a_log", "ssd_dt_bias", "ssd_d", "ssd_norm_w", "hgrn_lb", "hgrn_norm_w",
            "odd_w_in", "odd_w_out", "mlstm_conv_w", "mlstm_conv_b", "mlstm_gate_b", "mlstm_norm_w"]


def make_in_maps(inputs, cores):
    f = lambda a: np.ascontiguousarray(np.asarray(a, dtype=np.float32))
    shared = {k: f(inputs[k]) for k in _W_NAMES}
    maps = []
    for ci in cores:
        m = dict(shared)
        m["x"] = f(inputs["x"][2 * ci:2 * ci + 2])
        m["ctx"] = f(inputs["ctx"][2 * ci:2 * ci + 2])
        m["cvec"] = f(np.stack([inputs["c"][2 * ci], inputs["c"][2 * ci + 1], inputs["c_ctx"]], axis=0))
        maps.append(m)
    return maps


def kernel(**inputs):
    prog = Prog()
    nc = prog.build()
    maps = make_in_maps(inputs, list(range(8)))
    res = run_bass_kernel_spmd(nc, maps, core_ids=list(range(8)))
    return np.concatenate([np.asarray(r["out"], dtype=np.float32) for r in res.results], axis=0)
```

```python
import contextlib
import math
import numpy as np
import concourse.bass as bass
import concourse.mybir as mybir
from concourse.bass_utils import run_bass_kernel_spmd

F32 = mybir.dt.float32
BF16 = mybir.dt.bfloat16
ALU = mybir.AluOpType
AF = mybir.ActivationFunctionType

D = 2048
NTOK = 4608
NBLK = 9
NCH = 36
DEPTH = 4
EVEN_IN = 7712
ODD_IN = 6160
EPS = 1e-6
BIG = 30000.0


class Res:
    __slots__ = ("name", "w", "r")

    def __init__(self, name):
        self.name = name
        self.w = None
        self.r = []


class Buf:
    def __init__(self, t, name):
        self.t = t
        self.name = name
        self.r = Res(name)
        self.subs = {}

    def k(self, key):
        s = self.subs.get(key)
        if s is None:
            s = Res(f"{self.name}.{key}")
            self.subs[key] = s
        return s


def _res(x):
    return x.r if isinstance(x, Buf) else x


class Trk:
    EPOCH = 60000

    def __init__(self, nc, es, n_dma_sems=(10, 6, 10)):
        self.nc = nc
        self.es = es
        self.engs = {"pe": nc.tensor, "act": nc.scalar, "dve": nc.vector, "pool": nc.gpsimd, "sp": nc.sync}
        self.sems = {}
        self.cnt = {}
        self.waited = {k: {} for k in self.engs}
        self.epoch = {}
        self.last = {}
        for k in ("pe", "act", "dve", "pool"):
            self.epoch[k] = 0
            self._new_sem(f"e_{k}0")
        self.dq = {}
        self.gen = 0
        for q, n in zip(("sp", "act", "pool"), n_dma_sems):
            keys = []
            for i in range(n):
                key = f"d_{q}{i}"
                self._new_sem(key)
                keys.append(key)
            self.dq[q] = {"keys": keys, "i": 0}
        self.n_instr = 0
        self.n_wait = 0

    def _new_sem(self, key):
        self.sems[key] = self.es.enter_context(self.nc.semaphore(key))
        self.cnt[key] = 0

    def _eng_key(self, k):
        key = f"e_{k}{self.epoch[k]}"
        if self.cnt[key] >= self.EPOCH:
            self.epoch[k] += 1
            key = f"e_{k}{self.epoch[k]}"
            self._new_sem(key)
        return key

    def _wait(self, ek, tok):
        if tok is None:
            return
        key, val = tok
        if self.waited[ek].get(key, 0) >= val:
            return
        self.engs[ek].wait_ge(self.sems[key], val)
        self.waited[ek][key] = val
        self.n_wait += 1

    def _deps(self, ek, reads, writes):
        toks = []
        for r in reads:
            r = _res(r)
            if r.w is not None:
                toks.append((r.w, True))
        for w in writes:
            w = _res(w)
            if w.w is not None:
                toks.append((w.w, True))
            for t in w.r:
                toks.append((t, False))
        for tok, strong in toks:
            if tok[0].startswith("e_" + ek):
                if ek == "pe" or not strong:
                    continue
            self._wait(ek, tok)

    def _commit(self, tok, reads, writes):
        for r in reads:
            r = _res(r)
            r.r.append(tok)
            if len(r.r) > 48:
                best = {}
                for k, v in r.r:
                    if best.get(k, 0) < v:
                        best[k] = v
                r.r = list(best.items())
        for w in writes:
            w = _res(w)
            w.w = tok
            w.r = []

    def op(self, ek, fn, reads=(), writes=()):
        return self.group(ek, [fn], reads, writes)

    def group(self, ek, fns, reads=(), writes=()):
        key = self._eng_key(ek)
        self._deps(ek, reads, writes)
        ins = None
        for fn in fns:
            ins = fn(self.engs[ek])
            self.n_instr += 1
        ins.then_inc(self.sems[key], 1)
        self.cnt[key] += 1
        tok = (key, self.cnt[key])
        self.last[ek] = tok
        self._commit(tok, reads, writes)
        return tok

    def dma(self, q, out, in_, reads=(), writes=(), **kw):
        d = self.dq[q]
        slot = d["i"] % len(d["keys"])
        d["i"] += 1
        key = d["keys"][slot]
        if self.cnt[key] + 16 > self.EPOCH:
            self.gen += 1
            key = f"d_{q}{slot}_{self.gen}"
            self._new_sem(key)
            d["keys"][slot] = key
        if self.cnt[key] > 0:
            self._wait(q, (key, self.cnt[key]))
        self._deps(q, reads, writes)
        ins = self.engs[q].dma_start(out=out, in_=in_, **kw)
        self.n_instr += 1
        ins.then_inc(self.sems[key], 16)
        self.cnt[key] += 16
        tok = (key, self.cnt[key])
        self._commit(tok, reads, writes)
        return tok

    def all_tokens(self):
        toks = [t for t in self.last.values()]
        for q in self.dq.values():
            for key in q["keys"]:
                if self.cnt[key]:
                    toks.append((key, self.cnt[key]))
        for key, c in self.cnt.items():
            if key.startswith("d_") and c and (key, c) not in toks:
                toks.append((key, c))
        return toks

    def barrier(self, engines=("pe", "act", "dve", "pool", "sp")):
        toks = self.all_tokens()
        for ek in engines:
            for t in toks:
                self._wait(ek, t)


def blk_cols(tb):
    return tb * 512


def chunk_col(b, kind, j):
    if kind == "lat":
        return b * 2048 + j * 128
    return 4096 + b * 256 + j * 128


def chain_chunks(b, d):
    ctx = [("ctx", 0), ("ctx", 1)]
    lat = [("lat", j) for j in range(16)]
    if d == 0:
        seq = ctx + lat
    else:
        seq = ctx[::-1] + lat[::-1]
    return [chunk_col(b, k, j) for k, j in seq]


class Prog:
    def __init__(self, n_layers=DEPTH, debug=False):
        self.n_layers = n_layers
        self.debug = debug
        self.nc = bass.Bass("TRN2", target_bir_lowering=False)
        self.uid = 0

    def kind(self, layer):
        f = getattr(self, "force_kind", None)
        if f is not None:
            return f
        return (layer % 2 == 1, layer // 2)

    def sb(self, es, name, shape, dt):
        self.uid += 1
        nm = f"{name}_{self.uid}"
        return Buf(es.enter_context(self.nc.sbuf_tensor(nm, list(shape), dt)), nm)

    def ps(self, es, name, shape, dt=F32):
        self.uid += 1
        nm = f"{name}_{self.uid}"
        return Buf(es.enter_context(self.nc.psum_tensor(nm, list(shape), dt)), nm)

    def dram(self, name, shape, dt, kind="Internal"):
        if self.debug and kind == "Internal" and name in self.debug_outs:
            kind = "ExternalOutput"
        t = self.nc.dram_tensor(name, list(shape), dt, kind=kind)
        return Buf(t.ap(), name)

    def build(self, debug_outs=()):
        self.debug_outs = set(debug_outs)
        nc = self.nc
        I = {}
        def inp(name, shape):
            I[name] = nc.dram_tensor(name, list(shape), F32, kind="ExternalInput").ap()
        inp("x", [2, 2048, D]); inp("ctx", [2, 256, D]); inp("cvec", [3, D])
        inp("mod_w", [DEPTH, D, 6 * D]); inp("mod_b", [DEPTH, 6 * D]); inp("norm_w", [DEPTH, 2, D])
        inp("final_norm_w", [D]); inp("mlp_w1", [DEPTH, D, 4 * D]); inp("mlp_w2", [DEPTH, 4 * D, D])
        inp("even_w_in", [2, D, EVEN_IN]); inp("even_w_out", [2, D, D])
        inp("ssd_conv_w", [2, 3, 1536]); inp("ssd_conv_b", [2, 1536]); inp("ssd_a_log", [2, 2, 16])
        inp("ssd_dt_bias", [2, 2, 16]); inp("ssd_d", [2, 16]); inp("ssd_norm_w", [2, 1024])
        inp("hgrn_lb", [2, 1024]); inp("hgrn_norm_w", [2, 1024])
        inp("odd_w_in", [2, D, ODD_IN]); inp("odd_w_out", [2, D, D])
        inp("mlstm_conv_w", [2, 3, 2048]); inp("mlstm_conv_b", [2, 2048]); inp("mlstm_gate_b", [2, 4, 4])
        inp("mlstm_norm_w", [2, 2048])
        self.I = I
        self.out = nc.dram_tensor("out", [2, 2048, D], F32, kind="ExternalOutput").ap()
        self.r_out = Res("out")

        self.XT = self.dram("XT", [D, NTOK], F32)
        self.U_FM = self.dram("U_FM", [6656, NTOK], BF16)
        self.U_TM = self.dram("U_TM", [NTOK, 2048], BF16)
        self.U_G = self.dram("U_G", [NTOK, 32], F32)
        self.HF = [self.dram("HF0", [D, NTOK], BF16), self.dram("HF1", [D, NTOK], BF16)]

        with contextlib.ExitStack() as es:
            self.T = Trk(nc, es)
            with contextlib.ExitStack() as ces:
                self.consts(ces)
                with contextlib.ExitStack() as pes:
                    self.prologue(pes)
                self.T.barrier()
                stop = getattr(self, "stop_after", None)
                if self.debug:
                    dbg = self.nc.dram_tensor("DBG_MOD", [128, DEPTH * 288], F32, kind="ExternalOutput").ap()
                    self.T.dma("sp", dbg, self.c["MOD"].t[:], reads=[self.c["MOD"]])
                    dbg2 = self.nc.dram_tensor("DBG_S", [128, DEPTH * 96], F32, kind="ExternalOutput").ap()
                    self.T.dma("sp", dbg2, self.c["S"].t[:], reads=[self.c["S"]])
                stopped = stop == ("prologue",)
                for layer in range(self.n_layers):
                    if stopped:
                        break
                    with contextlib.ExitStack() as des:
                        self.dense_phase(des, layer)
                    self.T.barrier()
                    if stop == ("dense", layer):
                        stopped = True
                        break
                    with contextlib.ExitStack() as mes:
                        self.mixer_phase(mes, layer)
                    self.T.barrier()
                    if stop == ("mixer", layer):
                        stopped = True
                        break
                if not stopped:
                    with contextlib.ExitStack() as des:
                        self.dense_phase(des, self.n_layers)
                self.T.barrier()
        return nc

    def consts(self, es):
        T = self.T
        c = self.c = {}

        def mk(name, shape, dt):
            c[name] = self.sb(es, name, shape, dt)
            return c[name]
        self.P = [self.ps(es, f"P{i}", [128, 512], F32) for i in range(8)]
        idf = mk("ident_f", [128, 128], F32)
        T.op("pool", lambda e: e.memset(idf.t[:], 1.0), writes=[idf])
        T.op("pool", lambda e: e.affine_select(idf.t[:], idf.t[:], pattern=[[-1, 128]], compare_op=ALU.is_equal,
                                               fill=0.0, base=0, channel_multiplier=1), reads=[idf], writes=[idf])
        idb = mk("ident_b", [128, 128], BF16)
        T.op("dve", lambda e: e.tensor_copy(idb.t[:], idf.t[:]), reads=[idf], writes=[idb])
        of = mk("ones_f", [128, 128], F32)
        T.op("pool", lambda e: e.memset(of.t[:], 1.0), writes=[of])
        ob = mk("ones_b", [128, 128], BF16)
        T.op("pool", lambda e: e.memset(ob.t[:], 1.0), writes=[ob])
        for d in range(2):
            sgn = 1 if d == 0 else -1
            tri = mk(f"tri{d}", [128, 128], F32)
            T.op("pool", lambda e: e.memset(tri.t[:], 1.0), writes=[tri])
            T.op("pool", lambda e: e.affine_select(tri.t[:], tri.t[:], pattern=[[sgn, 128]], compare_op=ALU.is_ge,
                                                   fill=0.0, base=0, channel_multiplier=-sgn), reads=[tri], writes=[tri])
            msk = mk(f"msk{d}", [128, 128], F32)
            T.op("dve", lambda e: e.tensor_scalar(msk.t[:], tri.t[:], -BIG, BIG, ALU.mult, ALU.add), reads=[tri], writes=[msk])
        mk("MOD", [128, DEPTH * 6 * 16 * 3], F32)
        mk("S", [128, DEPTH * 2 * 16 * 3], F32)
        mk("MODB", [128, DEPTH * 96], F32)
        mk("NW", [128, 128], F32)
        mk("FNW", [128, 16], F32)
        mk("SCW", [128, 72], F32); mk("SCB", [128, 24], F32)
        mk("MCW", [128, 96], F32); mk("MCB", [128, 32], F32)
        mk("SNW", [128, 16], F32); mk("LBR", [128, 16], F32); mk("HNW", [128, 16], F32); mk("MNW", [128, 32], F32)
        mk("LB", [128, 16], F32); mk("LB1", [128, 16], F32)
        mk("DSK", [128, 16], F32)
        mk("GB", [128, 32], F32)
        mk("DTB", [128, 64], F32)
        mk("EA", [128, 64], F32)
        mk("scT", [128, 16 * 3], BF16)

    def MODv(self, l, m, dc, v):
        i = ((l * 6 + m) * 16 + dc) * 3 + v
        return self.c["MOD"].t[:, i:i + 1]

    def Sv(self, l, i, dc, v):
        j = ((l * 2 + i) * 16 + dc) * 3 + v
        return self.c["S"].t[:, j:j + 1]

    def load_vecT(self, es, dst, src2d, k):
        T = self.T
        st = self.sb(es, "vst", [k, 128], F32)
        T.dma("sp", st.t[:], src2d, writes=[st])
        P = self.P[7]
        T.op("pe", lambda e: e.transpose(P.t[:, 0:k], st.t[:], self.c["ident_f"].t[0:k, 0:k]),
             reads=[st, self.c["ident_f"]], writes=[P])
        T.op("dve", lambda e: e.tensor_copy(dst.t[:, 0:k], P.t[:, 0:k]), reads=[P], writes=[dst])

    def prologue(self, es):
        T, c, I, nc = self.T, self.c, self.I, self.nc
        v128 = lambda ap, pat, **kw: ap.rearrange(pat, **kw)
        self.load_vecT(es, c["NW"], I["norm_w"].rearrange("l i (c p) -> (l i c) p", p=128), 128)
        self.load_vecT(es, c["FNW"], I["final_norm_w"].rearrange("(c p) -> c p", p=128), 16)
        self.load_vecT(es, c["SCW"], I["ssd_conv_w"].rearrange("e j (c p) -> (e j c) p", p=128), 72)
        self.load_vecT(es, c["SCB"], I["ssd_conv_b"].rearrange("e (c p) -> (e c) p", p=128), 24)
        self.load_vecT(es, c["MCW"], I["mlstm_conv_w"].rearrange("e j (c p) -> (e j c) p", p=128), 96)
        self.load_vecT(es, c["MCB"], I["mlstm_conv_b"].rearrange("e (c p) -> (e c) p", p=128), 32)
        self.load_vecT(es, c["SNW"], I["ssd_norm_w"].rearrange("e (c p) -> (e c) p", p=128), 16)
        self.load_vecT(es, c["LBR"], I["hgrn_lb"].rearrange("e (c p) -> (e c) p", p=128), 16)
        self.load_vecT(es, c["HNW"], I["hgrn_norm_w"].rearrange("e (c p) -> (e c) p", p=128), 16)
        self.load_vecT(es, c["MNW"], I["mlstm_norm_w"].rearrange("e (c p) -> (e c) p", p=128), 32)
        for l in range(DEPTH):
            tmp = self.sb(es, "mbt", [128, 96], F32)
            self.load_vecT(es, tmp, I["mod_b"][l].rearrange("(c p) -> c p", p=128), 96)
            T.op("dve", lambda e: e.tensor_copy(c["MODB"].t[:, l * 96:(l + 1) * 96], tmp.t[:]), reads=[tmp], writes=[c["MODB"]])
        LB, LB1, LBR = c["LB"], c["LB1"], c["LBR"]
        T.op("pool", lambda e: e.memset(LB.t[:, 0:8], 0.0), writes=[LB])
        T.op("dve", lambda e: e.tensor_tensor(LB.t[:, 8:16], LBR.t[:, 8:16], LBR.t[:, 0:8], ALU.subtract), reads=[LBR, LB], writes=[LB])
        T.op("act", lambda e: e.activation(LB.t[:, 8:16], LB.t[:, 8:16], AF.Sigmoid), reads=[LB], writes=[LB])
        T.op("dve", lambda e: e.tensor_scalar(LB1.t[:], LB.t[:], -1.0, 1.0, ALU.mult, ALU.add), reads=[LB], writes=[LB1])
        T.dma("sp", c["GB"].t[:], I["mlstm_gate_b"].rearrange("o a b -> (o a b)").partition_broadcast(128), writes=[c["GB"]])
        T.dma("sp", c["DTB"].t[:], I["ssd_dt_bias"].rearrange("e d h -> (e d h)").partition_broadcast(128), writes=[c["DTB"]])
        T.dma("sp", c["EA"].t[:], I["ssd_a_log"].rearrange("e d h -> (e d h)").partition_broadcast(128), writes=[c["EA"]])
        T.op("act", lambda e: e.activation(c["EA"].t[:], c["EA"].t[:], AF.Exp), reads=[c["EA"]], writes=[c["EA"]])
        for o in range(2):
            T.op("dve", lambda e: e.tensor_scalar_add(c["GB"].t[:, o * 16:o * 16 + 8], c["GB"].t[:, o * 16:o * 16 + 8], -math.log(16.0)),
                 reads=[c["GB"]], writes=[c["GB"]])
        for e_ in range(2):
            for h in range(16):
                T.dma("sp", c["DSK"].t[(h % 2) * 64:(h % 2) * 64 + 64, e_ * 8 + h // 2:e_ * 8 + h // 2 + 1],
                      I["ssd_d"][e_, h:h + 1].partition_broadcast(64), writes=[c["DSK"]])
        cT = self.sb(es, "cT", [128, 48], F32)
        self.load_vecT(es, cT, I["cvec"].rearrange("v (c p) -> (v c) p", p=128), 48)
        scv = c["scT"].t[:].rearrange("p (k v) -> p k v", v=3)
        for v in range(3):
            T.op("act", lambda e: e.activation(scv[:, :, v], cT.t[:, v * 16:(v + 1) * 16], AF.Silu), reads=[cT], writes=[c["scT"]])
        wb = [self.sb(es, f"mw{i}", [128, 16, 512], BF16) for i in range(3)]
        P = self.P
        n = 0
        for l in range(DEPTH):
            Pm = P[l % 2]
            for jb in range(24):
                w = wb[n % 3]; n += 1
                T.dma("pool", w.t[:], I["mod_w"][l][:, jb * 512:(jb + 1) * 512].rearrange("(k p) n -> p k n", p=128), writes=[w])
                for jt in range(4):
                    j = jb * 4 + jt
                    T.group("pe", [
                        (lambda e, kc=kc: e.matmul(Pm.t[:, j * 3:j * 3 + 3], w.t[:, kc, jt * 128:(jt + 1) * 128], scv[:, kc, :],
                                                   start=(kc == 0), stop=(kc == 15))) for kc in range(16)],
                        reads=[w, c["scT"]], writes=[Pm])
            MODl = c["MOD"].t[:, l * 288:(l + 1) * 288].rearrange("p (j v) -> p j v", v=3)
            T.op("dve", lambda e: e.tensor_tensor(MODl, Pm.t[:, 0:288].rearrange("p (j v) -> p j v", v=3),
                                                  c["MODB"].t[:, l * 96:(l + 1) * 96].unsqueeze(2).to_broadcast([128, 96, 3]), ALU.add),
                 reads=[Pm, c["MODB"]], writes=[c["MOD"]])
            for i in range(2):
                Sl = c["S"].t[:, (l * 2 + i) * 48:(l * 2 + i + 1) * 48].rearrange("p (k v) -> p k v", v=3)
                sc = c["MOD"].t[:, ((l * 6 + 1 + 3 * i) * 16) * 3:((l * 6 + 2 + 3 * i) * 16) * 3].rearrange("p (k v) -> p k v", v=3)
                nw = c["NW"].t[:, (l * 2 + i) * 16:(l * 2 + i + 1) * 16].unsqueeze(2).to_broadcast([128, 16, 3])
                T.op("dve", lambda e: e.tensor_scalar_add(Sl, sc, 1.0), reads=[c["MOD"]], writes=[c["S"]])
                T.op("dve", lambda e: e.tensor_tensor(Sl, Sl, nw, ALU.mult), reads=[c["S"], c["NW"]], writes=[c["S"]])

    def dense_phase(self, es, layer):
        T, c, I, P = self.T, self.c, self.I, self.P
        L = self.n_layers
        self.xT = self.sb(es, "xT", [128, 16, 512], F32)
        self.actT = self.sb(es, "actT", [128, 16, 512], BF16)
        self.wb = [self.sb(es, f"wb{i}", [128, 16 * 512], BF16) for i in range(3)]
        self.wn = 0
        self.f32t = [self.sb(es, f"f32t{i}", [128, 512], F32) for i in range(4)]
        self.fn = 0
        self.rstd = self.sb(es, "rstd", [128, 512], F32)
        self.so = [self.sb(es, f"so{i}", [128, 4, 512], BF16) for i in range(2)]
        self.son = 0
        self.pn = 0
        if layer > 0:
            self.aT = self.sb(es, "aT", [128, 64, 512], BF16)
            self.hA = self.sb(es, "hA", [128, 4, 512], BF16)
            self.hB = self.sb(es, "hB", [128, 4, 512], BF16)
            self.hC = self.so[0]
            self.hD = self.so[1]
            base = self.sb(es, "hs", [128, 2048], F32)
            self.hs = Buf(base.t[:].rearrange("p (a b) -> p a b", b=512), base.name)
            self.hs.r = base.r
            self.xin = base
        if layer == 0:
            self.xin = self.sb(es, "xin", [128, 2048], F32)
        for tb in range(NBLK):
            if layer == L and tb == 8:
                continue
            v = 2 if tb == 8 else tb // 4
            c0 = tb * 512
            xT = self.xT
            if layer == 0:
                self.load_x_input(tb)
            else:
                T.dma("sp", xT.t[:], self.XT.t[:, c0:c0 + 512].rearrange("(k p) n -> p k n", p=128),
                      reads=[self.XT.k(tb)], writes=[xT])
                self.stage_c(layer - 1, tb, v)
            if layer < L:
                self.norm_mod(layer, 0, v)
                self.stage_a(layer, tb)
                T.dma("sp", self.XT.t[:, c0:c0 + 512].rearrange("(k p) n -> p k n", p=128), xT.t[:],
                      reads=[xT], writes=[self.XT.k(tb)])
            else:
                self.final_out(tb)

    def nextP(self):
        p = self.P[self.pn % 4]
        self.pn += 1
        return p

    def nextF(self):
        f = self.f32t[self.fn % 4]
        self.fn += 1
        return f

    def load_x_input(self, tb):
        T, I, P = self.T, self.I, self.P
        idf = self.c["ident_f"]
        for tt in range(4):
            if tb < 8:
                src = I["x"][tb // 4, (tb % 4) * 512 + tt * 128:(tb % 4) * 512 + (tt + 1) * 128, :]
            else:
                src = I["ctx"][tt // 2, (tt % 2) * 128:(tt % 2 + 1) * 128, :]
            T.dma("sp", self.xin.t[:], src, writes=[self.xin])
            for g in range(4):
                Pg = self.nextP()
                T.group("pe", [(lambda e, j=j: e.transpose(Pg.t[:, j * 128:(j + 1) * 128],
                                                           self.xin.t[:, (g * 4 + j) * 128:(g * 4 + j + 1) * 128], idf.t[:])) for j in range(4)],
                        reads=[self.xin, idf], writes=[Pg])
                T.op("act" if g % 2 else "dve",
                     (lambda e: e.activation(self.xT.t[:, g * 4:(g + 1) * 4, tt * 128:(tt + 1) * 128],
                                             Pg.t[:].rearrange("p (j t) -> p j t", t=128), AF.Copy)) if g % 2 else
                     (lambda e: e.tensor_copy(self.xT.t[:, g * 4:(g + 1) * 4, tt * 128:(tt + 1) * 128],
                                              Pg.t[:].rearrange("p (j t) -> p j t", t=128))),
                     reads=[Pg], writes=[self.xT])

    def rms_rstd(self, src_fn, nchunks, nfeat, reads):
        T, P = self.T, self.P
        Pss = P[4]
        of = self.c["ones_f"]
        for j in range(nchunks):
            sq = self.nextF()
            T.op("act", lambda e: e.activation(sq.t[:], src_fn(j), AF.Square), reads=reads, writes=[sq])
            T.op("pe", lambda e: e.matmul(Pss.t[:], of.t[:], sq.t[:], start=(j == 0), stop=(j == nchunks - 1)),
                 reads=[sq, of], writes=[Pss])
        T.op("act", lambda e: e.activation(self.rstd.t[:], Pss.t[:], AF.Ln, bias=EPS, scale=1.0 / nfeat), reads=[Pss], writes=[self.rstd])
        T.op("act", lambda e: e.activation(self.rstd.t[:], self.rstd.t[:], AF.Exp, scale=-0.5), reads=[self.rstd], writes=[self.rstd])

    def norm_mod(self, l, i, v):
        T = self.T
        xT, actT = self.xT, self.actT
        self.rms_rstd(lambda j: xT.t[:, j, :], 16, D, [xT])
        for dc in range(16):
            tmp = self.nextF()
            T.op("dve", lambda e: e.tensor_tensor(tmp.t[:], xT.t[:, dc, :], self.rstd.t[:], ALU.mult), reads=[xT, self.rstd], writes=[tmp])
            T.op("act", lambda e: e.activation(actT.t[:, dc, :], tmp.t[:], AF.Identity, bias=self.MODv(l, 3 * i, dc, v), scale=self.Sv(l, i, dc, v)),
                 reads=[tmp, self.c["MOD"], self.c["S"]], writes=[actT])

    def load_w(self, src, wc):
        w = self.wb[self.wn % 3]
        self.wn += 1
        k = src.shape[0] // 128
        view = w.t[:, 0:k * wc].rearrange("p (k n) -> p k n", n=wc)
        self.T.dma("pool", view, src.rearrange("(k p) n -> p k n", p=128), writes=[w])
        return w, view

    def proj_fm(self, wsrc, col0, ncols, evac, nk=16, rhs_fn=None):
        T = self.T
        if rhs_fn is None:
            rhs_fn = lambda kc: self.actT.t[:, kc, :]
            rd = [self.actT]
        else:
            rd = [self.aT]
        wcb = 512 if nk == 16 else 128
        ct = 0
        for b0 in range(0, ncols, wcb):
            wc = min(wcb, ncols - b0)
            w, view = self.load_w(wsrc[:, col0 + b0:col0 + b0 + wc], wc)
            for t0 in range(0, wc, 128):
                Pt = self.nextP()
                T.group("pe", [(lambda e, kc=kc: e.matmul(Pt.t[:], view[:, kc, t0:t0 + 128], rhs_fn(kc), start=(kc == 0), stop=(kc == nk - 1)))
                               for kc in range(nk)], reads=[w] + rd, writes=[Pt])
                evac(ct, Pt)
                ct += 1

    def proj_tm(self, wsrc, col0, ncols, evac):
        T = self.T
        for b0 in range(0, ncols, 512):
            wc = min(512, ncols - b0)
            w, view = self.load_w(wsrc[:, col0 + b0:col0 + b0 + wc], wc)
            for tt in range(4):
                Pt = self.nextP()
                T.group("pe", [(lambda e, kc=kc: e.matmul(Pt.t[:, 0:wc], self.actT.t[:, kc, tt * 128:(tt + 1) * 128], view[:, kc, :],
                                                          start=(kc == 0), stop=(kc == 15))) for kc in range(16)],
                        reads=[w, self.actT], writes=[Pt])
                evac(b0, wc, tt, Pt)

    class FmOut:
        def __init__(self, prog, row0, tb):
            self.p = prog; self.row0 = row0; self.tb = tb; self.n = 0; self.cur = None

        def slot(self):
            if self.n % 4 == 0:
                self.cur = self.p.so[self.p.son % 2]
                self.p.son += 1
            j = self.n % 4
            self.n += 1
            return self.cur, self.cur.t[:, j, :]

        def done_tile(self, last=False):
            if self.n % 4 == 0 or last:
                cnt = (self.n - 1) % 4 + 1
                r0 = self.row0 + (self.n - cnt) * 128
                p = self.p
                c0 = self.tb * 512
                p.T.dma("act", p.U_FM.t[r0:r0 + cnt * 128, c0:c0 + 512].rearrange("(t p) n -> p t n", p=128), self.cur.t[:, 0:cnt, :],
                        reads=[self.cur], writes=[p.U_FM.k((r0 // 128, self.tb))])

    def fm_group(self, wsrc, col0, ncols, row0, tb, evac_to):
        out = Prog.FmOut(self, row0, tb)
        ntile = ncols // 128

        def ev(ct, Pt):
            buf, dst = out.slot()
            evac_to(ct, Pt, buf, dst)
            out.done_tile(last=(ct == ntile - 1))
        self.proj_fm(wsrc, col0, ncols, ev)

    def conv_evac(self, tb, CW, CB, base_w, base_b, nchan_chunks):
        T = self.T
        seg = 256 if tb == 8 else 64

        def ev(ct, Pt, buf, dst):
            cv = self.nextF()
            w0 = CW.t[:, base_w + ct:base_w + ct + 1]
            w1 = CW.t[:, base_w + nchan_chunks + ct:base_w + nchan_chunks + ct + 1]
            w2 = CW.t[:, base_w + 2 * nchan_chunks + ct:base_w + 2 * nchan_chunks + ct + 1]
            bb = CB.t[:, base_b + ct:base_b + ct + 1]
            T.op("act", lambda e: e.activation(cv.t[:], Pt.t[:], AF.Identity, bias=bb, scale=w1), reads=[Pt, CW, CB], writes=[cv])
            cvv = cv.t[:].rearrange("p (s l) -> p s l", l=seg)
            pv = Pt.t[:].rearrange("p (s l) -> p s l", l=seg)
            T.op("dve", lambda e: e.scalar_tensor_tensor(cvv[:, :, 1:seg], pv[:, :, 0:seg - 1], w0, cvv[:, :, 1:seg], ALU.mult, ALU.add),
                 reads=[Pt, cv, CW], writes=[cv])
            T.op("dve", lambda e: e.scalar_tensor_tensor(cvv[:, :, 0:seg - 1], pv[:, :, 1:seg], w2, cvv[:, :, 0:seg - 1], ALU.mult, ALU.add),
                 reads=[Pt, cv, CW], writes=[cv])
            T.op("act", lambda e: e.activation(dst, cv.t[:], AF.Silu), reads=[cv], writes=[buf])
        return ev

    def act_evac(self, func):
        T = self.T

        def ev(ct, Pt, buf, dst):
            T.op("act", lambda e: e.activation(dst, Pt.t[:], func), reads=[Pt], writes=[buf])
        return ev

    def copy_evac(self):
        T = self.T

        def ev(ct, Pt, buf, dst):
            T.op("dve", lambda e: e.tensor_copy(dst, Pt.t[:]), reads=[Pt], writes=[buf])
        return ev

    def tm_evac(self, tb, dst_buf, col_off, dt):
        T = self.T
        c0 = tb * 512

        def ev(b0, wc, tt, Pt):
            if dt == BF16:
                st = self.so[self.son % 2]; self.son += 1
                sv = st.t[:, 0, 0:wc]
            else:
                st = self.nextF()
                sv = st.t[:, 0:wc]
            T.op("dve", lambda e: e.tensor_copy(sv, Pt.t[:, 0:wc]), reads=[Pt], writes=[st])
            r0 = c0 + tt * 128
            T.dma("act", dst_buf.t[r0:r0 + 128, col_off + b0:col_off + b0 + wc], sv, reads=[st],
                  writes=[dst_buf.k((r0 // 128, (col_off + b0) // 512))])
        return ev

    def stage_a(self, layer, tb):
        I, c = self.I, self.c
        odd, idx = self.kind(layer)
        if not odd:
            e_ = idx
            W = I["even_w_in"][e_]
            self.fm_group(W, 0, 1024, 0, tb, self.act_evac(AF.Silu))
            self.fm_group(W, 1024, 1536, 1024, tb, self.conv_evac(tb, c["SCW"], c["SCB"], e_ * 36, e_ * 12, 12))
            self.fm_group(W, 2592, 1024, 2560, tb, self.act_evac(AF.Silu))
            self.fm_group(W, 3616, 2048, 3584, tb, self.copy_evac())
            self.fm_group(W, 6688, 1024, 5632, tb, self.act_evac(AF.Silu))
            self.proj_tm(W, 5664, 1024, self.tm_evac(tb, self.U_TM, 0, BF16))
            self.proj_tm(W, 2560, 32, self.tm_evac(tb, self.U_G, 0, F32))
        else:
            o_ = idx
            W = I["odd_w_in"][o_]
            self.fm_group(W, 0, 2048, 0, tb, self.conv_evac(tb, c["MCW"], c["MCB"], o_ * 48, o_ * 16, 16))
            self.fm_group(W, 4096, 2048, 2048, tb, self.act_evac(AF.Sigmoid))
            self.proj_tm(W, 2048, 2048, self.tm_evac(tb, self.U_TM, 0, BF16))
            self.proj_tm(W, 6144, 16, self.tm_evac(tb, self.U_G, 0, F32))

    def xk(self, lst=None):
        return [self.xT.k(i) for i in (range(16) if lst is None else lst)]

    def stage_c(self, l, tb, v):
        T, c, I = self.T, self.c, self.I
        xT, aT = self.xT, self.aT
        self.finalize_mixer(l, tb)
        odd, idx = self.kind(l)
        W = I["odd_w_out"][idx] if odd else I["even_w_out"][idx]

        def ev_res(m):
            def ev(ct, Pt):
                T.op("dve", lambda e: e.scalar_tensor_tensor(xT.t[:, ct, :], Pt.t[:], self.MODv(l, m, ct, v), xT.t[:, ct, :], ALU.mult, ALU.add),
                     reads=[Pt, c["MOD"], xT], writes=[xT])
            return ev
        self.proj_fm(W, 0, 2048, ev_res(2))
        self.norm_mod(l, 1, v)

        def ev1(ct, Pt):
            tmp = self.nextF()
            T.op("act", lambda e: e.activation(tmp.t[:], Pt.t[:], AF.Relu), reads=[Pt], writes=[tmp])
            T.op("dve", lambda e: e.tensor_tensor(aT.t[:, ct, :], tmp.t[:], tmp.t[:], ALU.mult), reads=[tmp], writes=[aT])
        self.proj_fm(I["mlp_w1"][l], 0, 8192, ev1)
        self.proj_fm(I["mlp_w2"][l], 0, 2048, ev_res(5), nk=64, rhs_fn=lambda kc: aT.t[:, kc, :])

    def ld_blk(self, dst, src_buf, r0, tb):
        c0 = tb * 512
        self.T.dma("sp", dst.t[:], src_buf.t[r0:r0 + 512, c0:c0 + 512].rearrange("(k p) n -> p k n", p=128), writes=[dst])

    def finalize_mixer(self, l, tb):
        T, c = self.T, self.c
        hA, hB, hC, hD, hs, actT = self.hA, self.hB, self.hC, self.hD, self.hs, self.actT
        odd, idx = self.kind(l)
        for gI in range(4):
            r0 = gI * 512
            self.ld_blk(hA, self.HF[0], r0, tb)
            self.ld_blk(hB, self.HF[1], r0, tb)
            T.op("dve", lambda e: e.tensor_tensor(hs.t[:], hA.t[:], hB.t[:], ALU.add), reads=[hA, hB], writes=[hs])
            if odd:
                self.ld_blk(hC, self.U_FM, 2048 + r0, tb)
                self.rms_rstd(lambda j: hs.t[:, j, :], 4, 512, [hs])
                for j in range(4):
                    dc = gI * 4 + j
                    tmp = self.nextF()
                    T.op("dve", lambda e: e.tensor_tensor(tmp.t[:], hs.t[:, j, :], self.rstd.t[:], ALU.mult), reads=[hs, self.rstd], writes=[tmp])
                    T.op("dve", lambda e: e.scalar_tensor_tensor(actT.t[:, dc, :], tmp.t[:], c["MNW"].t[:, idx * 16 + dc:idx * 16 + dc + 1],
                                                                 hC.t[:, j, :], ALU.mult, ALU.mult), reads=[tmp, hC, c["MNW"]], writes=[actT])
            elif gI < 2:
                e_ = idx
                self.ld_blk(hC, self.U_FM, 1024 + r0, tb)
                self.ld_blk(hD, self.U_FM, r0, tb)
                for j in range(4):
                    dc = gI * 4 + j
                    T.op("dve", lambda e: e.scalar_tensor_tensor(hs.t[:, j, :], hC.t[:, j, :], c["DSK"].t[:, e_ * 8 + dc:e_ * 8 + dc + 1],
                                                                 hs.t[:, j, :], ALU.mult, ALU.add), reads=[hC, hs, c["DSK"]], writes=[hs])
                T.op("dve", lambda e: e.tensor_tensor(hs.t[:], hs.t[:], hD.t[:], ALU.mult), reads=[hs, hD], writes=[hs])
                self.rms_rstd(lambda j: hs.t[:, j, :], 4, 512, [hs])
                for j in range(4):
                    dc = gI * 4 + j
                    tmp = self.nextF()
                    T.op("dve", lambda e: e.tensor_tensor(tmp.t[:], hs.t[:, j, :], self.rstd.t[:], ALU.mult), reads=[hs, self.rstd], writes=[tmp])
                    T.op("act", lambda e: e.activation(actT.t[:, dc, :], tmp.t[:], AF.Copy, scale=c["SNW"].t[:, e_ * 8 + dc:e_ * 8 + dc + 1]),
                         reads=[tmp, c["SNW"]], writes=[actT])
            else:
                e_ = idx
                self.ld_blk(hD, self.U_FM, 5632 + (gI - 2) * 512, tb)
                for j in range(4):
                    dc = gI * 4 + j
                    self.rms_rstd(lambda _: hs.t[:, j, :], 1, 128, [hs])
                    tmp = self.nextF()
                    T.op("dve", lambda e: e.tensor_tensor(tmp.t[:], hs.t[:, j, :], self.rstd.t[:], ALU.mult), reads=[hs, self.rstd], writes=[tmp])
                    T.op("dve", lambda e: e.scalar_tensor_tensor(actT.t[:, dc, :], tmp.t[:], c["HNW"].t[:, e_ * 8 + dc - 8:e_ * 8 + dc - 7],
                                                                 hD.t[:, j, :], ALU.mult, ALU.mult), reads=[tmp, hD, c["HNW"]], writes=[actT])

    def final_out(self, tb):
        T, c = self.T, self.c
        xT = self.xT
        idf = c["ident_f"]
        self.rms_rstd(lambda j: xT.t[:, j, :], 16, D, [xT])
        for dc in range(16):
            tmp = self.nextF()
            T.op("dve", lambda e: e.tensor_tensor(tmp.t[:], xT.t[:, dc, :], self.rstd.t[:], ALU.mult), reads=[xT, self.rstd], writes=[tmp])
            T.op("act", lambda e: e.activation(xT.t[:, dc, :], tmp.t[:], AF.Copy, scale=c["FNW"].t[:, dc:dc + 1]), reads=[tmp, c["FNW"]], writes=[xT])
        for tt in range(4):
            for g in range(4):
                Pg = self.nextP()
                T.group("pe", [(lambda e, j=j: e.transpose(Pg.t[:, j * 128:(j + 1) * 128], xT.t[:, g * 4 + j, tt * 128:(tt + 1) * 128], idf.t[:]))
                               for j in range(4)], reads=[xT, idf], writes=[Pg])
                if g % 2:
                    T.op("act", lambda e: e.activation(self.xin.t[:, g * 512:(g + 1) * 512], Pg.t[:], AF.Copy), reads=[Pg], writes=[self.xin])
                else:
                    T.op("dve", lambda e: e.tensor_copy(self.xin.t[:, g * 512:(g + 1) * 512], Pg.t[:]), reads=[Pg], writes=[self.xin])
            r0 = (tb % 4) * 512 + tt * 128
            T.dma("sp", self.out[tb // 4, r0:r0 + 128, :], self.xin.t[:], reads=[self.xin], writes=[self.r_out])

    def mixer_phase(self, es, layer):
        self.dslot = 0
        odd, idx = self.kind(layer)
        self.decay_banks = not odd
        if odd:
            self.mlstm_phase(es, idx)
        else:
            self.even_phase(es, idx)

    def trps(self, i):
        return self.P[i].t[:].bitcast(BF16)

    def gate_cums(self, d, src, n, ws, pres=None, bank=7):
        T, c, P = self.T, self.c, self.P
        Pg = P[bank]
        pr = Pg if pres is None else pres
        T.op("pe", lambda e: e.matmul(Pg.t[:, 256:256 + n], c[f"tri{d}"].t[:], src.t[:, 0:n], start=True, stop=True),
             reads=[src, c[f"tri{d}"]], writes=[pr])
        T.op("pe", lambda e: e.matmul(Pg.t[:, 256 + n:256 + 2 * n], c["ones_f"].t[:], src.t[:, 0:n], start=True, stop=True),
             reads=[src, c["ones_f"]], writes=[pr])
        T.op("dve", lambda e: e.tensor_copy(ws["cs"].t[:, 0:2 * n], Pg.t[:, 256:256 + 2 * n]), reads=[pr], writes=[ws["cs"]])
        return ws["cs"]

    def decay_mats(self, d, src, h, bias_ap, ws, need_e=True, place=None):
        T, c, P = self.T, self.c, self.P
        slot = self.dslot % 2
        if place is None:
            self.dslot += 1
        if place is not None:
            slot, PA, b0 = place
            pr = PA
        elif getattr(self, "decay_banks", False):
            PA = P[1] if slot == 0 else P[0]
            pr = PA
            b0 = 0 if slot == 0 else 256
        else:
            PA = P[1]
            pr = PA.k(slot)
            b0 = slot * 256
        Dm, E = ws["Dm"][slot], ws["E"][slot]
        bc = src.t[:, h:h + 1].to_broadcast([128, 128])
        fns = [lambda e: e.matmul(PA.t[:, b0 + 128:b0 + 256], bc, c[f"tri{d}"].t[:], start=True, stop=False),
               lambda e: e.matmul(PA.t[:, b0 + 128:b0 + 256], c["ident_f"].t[:], c[f"msk{d}"].t[:], start=False, stop=True)]
        if need_e:
            fns.insert(0, lambda e: e.matmul(PA.t[:, b0:b0 + 128], bc, c[f"tri{d}"].t[:], start=True, stop=True))
        T.group("pe", fns, reads=[src, c[f"tri{d}"], c[f"msk{d}"], c["ident_f"]], writes=[pr])
        T.op("act", lambda e: e.activation(Dm.t[:], PA.t[:, b0 + 128:b0 + 256], AF.Exp, bias=bias_ap, scale=-1.0),
             reads=[pr, ws["bias"]], writes=[Dm])
        if need_e:
            T.op("act", lambda e: e.activation(E.t[:], PA.t[:, b0:b0 + 128], AF.Exp, scale=-1.0), reads=[pr], writes=[E])
        return Dm, E

    def mlstm_phase(self, es, o_):
        T, c, P = self.T, self.c, self.P
        chains = [(b, d) for b in range(2) for d in range(2)]
        Cf = {ch: self.sb(es, "Cf", [128, 4, 2, 640], F32) for ch in chains}
        Cb = {ch: self.sb(es, "Cb", [128, 4, 2, 640], BF16) for ch in chains}
        for ch in chains:
            T.op("pool", lambda e: e.memset(Cf[ch].t[:], 0.0), writes=[Cf[ch]])
            T.op("pool", lambda e: e.memset(Cb[ch].t[:], 0.0), writes=[Cb[ch]])
        sets = []
        for i in range(2):
            w = {}
            w["qT"] = self.sb(es, "qT", [128, 8, 128], BF16)
            w["kT"] = self.sb(es, "kT", [128, 8, 128], BF16)
            w["Vt"] = self.sb(es, "Vt", [128, 2048], BF16)
            w["g"] = self.sb(es, "g", [128, 16], F32)
            w["li"] = self.sb(es, "li", [128, 4], F32)
            w["sp"] = self.sb(es, "sp", [128, 4], F32)
            w["cs"] = self.sb(es, "cs", [128, 8], F32)
            w["bias"] = self.sb(es, "bias", [128, 4], F32)
            w["wsx"] = self.sb(es, "wsx", [128, 4], F32)
            w["cdec"] = self.sb(es, "cdec", [128, 4], F32)
            w["Dm"] = [self.sb(es, "Dm", [128, 128], F32) for _ in range(2)]
            w["E"] = [self.sb(es, "E", [128, 128], F32) for _ in range(2)]
            w["WT"] = [self.sb(es, "WT", [128, 128], BF16) for _ in range(2)]
            w["qs"] = [self.sb(es, "qs", [128, 2, 128], BF16) for _ in range(2)]
            w["ke"] = [self.sb(es, "ke", [128, 256], BF16) for _ in range(2)]
            w["dd"] = [self.sb(es, "dd", [128, 128], F32) for _ in range(2)]
            w["ho"] = self.sb(es, "ho", [128, 16, 128], BF16)
            sets.append(w)
        GB = c["GB"]
        n = 0
        self.hslot = 0
        for step in range(18):
            for ch in chains:
                b, d = ch
                c0 = chain_chunks(b, d)[step]
                w = sets[n % 2]; n += 1
                T.dma("sp", w["qT"].t[:], self.U_FM.t[0:1024, c0:c0 + 128].rearrange("(k p) n -> p k n", p=128), writes=[w["qT"]])
                T.dma("sp", w["kT"].t[:], self.U_FM.t[1024:2048, c0:c0 + 128].rearrange("(k p) n -> p k n", p=128), writes=[w["kT"]])
                T.dma("sp", w["Vt"].t[:], self.U_TM.t[c0:c0 + 128, 0:2048], writes=[w["Vt"]])
                T.dma("sp", w["g"].t[:], self.U_G.t[c0:c0 + 128, 0:16], writes=[w["g"]])
                g, li, sp = w["g"], w["li"], w["sp"]
                T.op("dve", lambda e: e.tensor_tensor(li.t[:], g.t[:, d * 4:d * 4 + 4], GB.t[:, o_ * 16 + d * 4:o_ * 16 + d * 4 + 4], ALU.add),
                     reads=[g, GB], writes=[li])
                T.op("dve", lambda e: e.tensor_tensor(sp.t[:], g.t[:, 8 + d * 4:12 + d * 4], GB.t[:, o_ * 16 + 8 + d * 4:o_ * 16 + 12 + d * 4], ALU.add),
                     reads=[g, GB], writes=[sp])
                T.op("act", lambda e: e.activation(sp.t[:], sp.t[:], AF.Exp, scale=-1.0), reads=[sp], writes=[sp])
                T.op("act", lambda e: e.activation(sp.t[:], sp.t[:], AF.Ln, bias=1.0), reads=[sp], writes=[sp])
                cs = self.gate_cums(d, sp, 4, w, bank=0)
                bias, wsx, cdec = w["bias"], w["wsx"], w["cdec"]
                T.op("dve", lambda e: e.tensor_tensor(bias.t[:], li.t[:], cs.t[:, 0:4], ALU.add), reads=[li, cs], writes=[bias])
                T.op("dve", lambda e: e.tensor_tensor(wsx.t[:], bias.t[:], cs.t[:, 4:8], ALU.subtract), reads=[bias, cs], writes=[wsx])
                T.op("act", lambda e: e.activation(wsx.t[:], wsx.t[:], AF.Exp), reads=[wsx], writes=[wsx])
                T.op("act", lambda e: e.activation(cdec.t[:], cs.t[:, 4:8], AF.Exp, scale=-1.0), reads=[cs], writes=[cdec])
                qT, kT, Vt, ho = w["qT"], w["kT"], w["Vt"], w["ho"]
                cf, cb = Cf[ch], Cb[ch]
                hst = {}

                def F(h):
                        hs_ = self.hslot % 2
                        self.hslot += 1
                        WT, qs, ke, dd = w["WT"][hs_], w["qs"][hs_], w["ke"][hs_], w["dd"][hs_]
                        hst[h] = (hs_, WT, qs, ke, dd)
                        PX, PY_ = P[hs_], P[2 + hs_]
                        T.group("pe", [(lambda e, j=j: e.matmul(PX.t[:, 0:128], kT.t[:, h * 2 + j, :], qT.t[:, h * 2 + j, :], start=(j == 0), stop=(j == 1)))
                                       for j in range(2)], reads=[kT, qT], writes=[PX])
                        trv = self.trps(2 + hs_)[:, 0:256]
                        T.group("pe", [(lambda e, j=j: e.transpose(trv[:, j * 128:(j + 1) * 128], kT.t[:, h * 2 + j, :], c["ident_b"].t[:])) for j in range(2)],
                                reads=[kT, c["ident_b"]], writes=[PY_])
                        Dm, E = self.decay_mats(d, sp, h, bias.t[:, h:h + 1], w, place=(hs_, PY_, 256))
                        T.op("dve", lambda e: e.tensor_tensor(WT.t[:], PX.t[:, 0:128], Dm.t[:], ALU.mult), reads=[PX, Dm], writes=[WT])
                        T.op("dve", lambda e: e.tensor_tensor(qs.t[:], qT.t[:, h * 2:h * 2 + 2, :], E.t[:].unsqueeze(1).to_broadcast([128, 2, 128]), ALU.mult),
                             reads=[qT, E], writes=[qs])
                        T.op("act", lambda e: e.activation(ke.t[:], trv[:, 0:256], AF.Copy, scale=wsx.t[:, h:h + 1]), reads=[PY_, wsx], writes=[ke])

                def B(h):
                        hs_, WT, qs, ke, dd = hst[h]
                        PN, PD = P[4], P[5]
                        fns = []
                        for vc in range(5):
                            if vc < 4:
                                dst = PN.t[:, vc * 128:(vc + 1) * 128]
                                l0 = Vt.t[:, h * 512 + vc * 128:h * 512 + (vc + 1) * 128]
                            else:
                                dst = PD.t[:, 0:128]
                                l0 = c["ones_b"].t[:]
                            fns.append(lambda e, dst=dst, l0=l0: e.matmul(dst, l0, WT.t[:], start=True, stop=False))
                            for j in range(2):
                                fns.append(lambda e, dst=dst, j=j, vc=vc: e.matmul(dst, cb.t[:, h, j, vc * 128:(vc + 1) * 128], qs.t[:, j, :], start=False, stop=(j == 1)))
                        T.group("pe", fns, reads=[Vt, WT, cb, qs, c["ones_b"]], writes=[PN, PD])
                        T.op("act", lambda e: e.activation(dd.t[:], PD.t[:, 0:128], AF.Abs), reads=[PD], writes=[dd])
                        T.op("dve", lambda e: e.tensor_scalar_max(dd.t[:], dd.t[:], 1.0), reads=[dd], writes=[dd])
                        T.op("dve", lambda e: e.reciprocal(dd.t[:], dd.t[:]), reads=[dd], writes=[dd])
                        T.op("dve", lambda e: e.tensor_tensor(ho.t[:, h * 4:(h + 1) * 4, :], PN.t[:].rearrange("p (v t) -> p v t", t=128),
                                                              dd.t[:].unsqueeze(1).to_broadcast([128, 4, 128]), ALU.mult), reads=[PN, dd], writes=[ho])
                        for j in range(2):
                            Pc = P[6 + j]
                            T.op("pe", lambda e: e.matmul(Pc.t[:], ke.t[:, j * 128:(j + 1) * 128], Vt.t[:, h * 512:(h + 1) * 512], start=True, stop=True),
                                 reads=[ke, Vt], writes=[Pc])
                            T.op("dve", lambda e: e.scalar_tensor_tensor(cf.t[:, h, j, 0:512], cf.t[:, h, j, 0:512], cdec.t[:, h:h + 1], Pc.t[:], ALU.mult, ALU.add),
                                 reads=[cf, cdec, Pc], writes=[cf])
                        Pn = P[hs_]
                        T.group("pe", [(lambda e, j=j: e.matmul(Pn.t[:, 256 + j * 128:256 + (j + 1) * 128], ke.t[:, j * 128:(j + 1) * 128], c["ones_b"].t[:], start=True, stop=True))
                                       for j in range(2)], reads=[ke, c["ones_b"]], writes=[Pn])
                        T.op("dve", lambda e: e.scalar_tensor_tensor(cf.t[:, h, :, 512:640], cf.t[:, h, :, 512:640], cdec.t[:, h:h + 1],
                                                                     Pn.t[:, 256:512].rearrange("p (j n) -> p j n", n=128), ALU.mult, ALU.add),
                             reads=[cf, cdec, Pn], writes=[cf])
                        T.op("act", lambda e: e.activation(cb.t[:, h], cf.t[:, h], AF.Copy), reads=[cf], writes=[cb])

                F(0)
                for h in range(4):
                    if h + 1 < 4:
                        F(h + 1)
                    B(h)
                T.dma("act", self.HF[d].t[:, c0:c0 + 128].rearrange("(k p) n -> p k n", p=128), ho.t[:], reads=[ho], writes=[self.HF[d].k(c0 // 128)])

    def even_phase(self, es, e_):
        T, c, P = self.T, self.c, self.P
        chains = [(b, d) for b in range(2) for d in range(2)]
        rst = self.sb(es, "rst", [128, 1024], F32)
        T.op("pool", lambda e: e.memset(rst.t[:], 1.0), writes=[rst])
        T.op("pool", lambda e: e.memset(rst.t[:].rearrange("p (c t) -> p c t", t=128)[:, :, 0:1], 0.0), reads=[rst], writes=[rst])
        Hf = {ch: self.sb(es, "Hf", [128, 1024], F32) for ch in chains}
        Hb = {ch: self.sb(es, "Hb", [128, 1024], BF16) for ch in chains}
        Sf = {ch: self.sb(es, "Sf", [128, 8, 128], F32) for ch in chains}
        Sb = {ch: self.sb(es, "Sb", [128, 8, 128], BF16) for ch in chains}
        for ch in chains:
            for t in (Hf[ch], Hb[ch], Sf[ch], Sb[ch]):
                T.op("pool", lambda e: e.memset(t.t[:], 0.0), writes=[t])
        KI = {}
        for d in range(2):
            KI[d] = [self.sb(es, f"KI{d}_{i}", [128, 8, 128], BF16) for i in range(8)]
            for t in KI[d]:
                T.op("pool", lambda e: e.memset(t.t[:], 0.0), writes=[t])
        sets = []
        for i in range(2):
            w = {}
            def mk(name, shape, dt):
                w[name] = self.sb(es, name, shape, dt)
            mk("BCT", [128, 4, 128], BF16); mk("xsT", [128, 8, 128], BF16); mk("dtr", [128, 32], F32)
            mk("dt", [128, 16], F32); mk("na", [128, 16], F32); mk("cs", [128, 32], F32); mk("wsx", [128, 16], F32)
            mk("cdec", [128, 16], F32); mk("xdt", [128, 1024], BF16); mk("xdtw", [128, 1024], BF16); mk("Btm", [128, 256], BF16)
            mk("G", [128, 256], F32); mk("ysb", [128, 1024], BF16); mk("hos", [128, 8, 128], BF16)
            for nm, dt_ in (("Dm", F32), ("E", F32), ("MT", BF16), ("CsT", BF16), ("attm", BF16)):
                w[nm] = [self.sb(es, nm, [128, 128], dt_) for _ in range(2)]
            w["bias"] = w["cs"]
            mk("qT", [128, 8, 128], BF16); mk("fT", [128, 8, 128], BF16); mk("Vt", [128, 1024], BF16)
            mk("A", [128, 8, 128], F32); mk("KIN", [128, 8, 128], F32); mk("LG", [128, 8, 128], F32); mk("BC", [128, 8, 128], F32)
            mk("TMP", [128, 8, 128], F32); mk("tot", [128, 8], F32); mk("hdec", [128, 8], F32); mk("CST", [128, 8, 8], F32)
            mk("qs", [128, 8, 128], BF16); mk("keT", [128, 8, 128], BF16); mk("ketm", [128, 1024], BF16); mk("qloc", [128, 8, 128], BF16)
            mk("hoh", [128, 8, 128], BF16)
            sets.append(w)
        n = 0
        for step in range(18):
            for ch in chains:
                b, d = ch
                c0 = chain_chunks(b, d)[step]
                w = sets[n % 2]; n += 1
                gen = self.hgrn_prep(e_, d, c0, w, KI[d], rst)
                self.ssd_step(e_, d, c0, w, Hf[ch], Hb[ch], gen)
                for _ in gen:
                    pass
                self.hgrn_heads(e_, d, c0, w, Sf[ch], Sb[ch], KI[d])

    def ssd_step(self, e_, d, c0, w, hf, hb, gen=None):
        T, c, P = self.T, self.c, self.P
        BCT, xsT, dtr, dt, na, wsx, cdec = w["BCT"], w["xsT"], w["dtr"], w["dt"], w["na"], w["wsx"], w["cdec"]
        xdt, xdtw, Btm, G, ysb, hos = w["xdt"], w["xdtw"], w["Btm"], w["G"], w["ysb"], w["hos"]
        T.dma("sp", BCT.t[:], self.U_FM.t[2048:2560, c0:c0 + 128].rearrange("(k p) n -> p k n", p=128), writes=[BCT])
        T.dma("sp", xsT.t[:], self.U_FM.t[1024:2048, c0:c0 + 128].rearrange("(k p) n -> p k n", p=128), writes=[xsT])
        T.dma("sp", dtr.t[:], self.U_G.t[c0:c0 + 128, 0:32], writes=[dtr])
        o = e_ * 32 + d * 16
        T.op("dve", lambda e: e.tensor_tensor(dt.t[:], dtr.t[:, d * 16:(d + 1) * 16], c["DTB"].t[:, o:o + 16], ALU.add), reads=[dtr, c["DTB"]], writes=[dt])
        T.op("act", lambda e: e.activation(dt.t[:], dt.t[:], AF.Exp), reads=[dt], writes=[dt])
        T.op("act", lambda e: e.activation(dt.t[:], dt.t[:], AF.Ln, bias=1.0), reads=[dt], writes=[dt])
        T.op("dve", lambda e: e.tensor_tensor(na.t[:], dt.t[:], c["EA"].t[:, o:o + 16], ALU.mult), reads=[dt, c["EA"]], writes=[na])
        cs = self.gate_cums(d, na, 16, w)
        T.op("dve", lambda e: e.tensor_tensor(wsx.t[:], cs.t[:, 16:32], cs.t[:, 0:16], ALU.subtract), reads=[cs], writes=[wsx])
        T.op("act", lambda e: e.activation(wsx.t[:], wsx.t[:], AF.Exp, scale=-1.0), reads=[wsx], writes=[wsx])
        T.op("act", lambda e: e.activation(cdec.t[:], cs.t[:, 16:32], AF.Exp, scale=-1.0), reads=[cs], writes=[cdec])
        tr2 = self.trps(2)
        T.group("pe", [(lambda e, j=j: e.transpose(tr2[:, j * 128:(j + 1) * 128], xsT.t[:, j, :], c["ident_b"].t[:])) for j in range(8)],
                reads=[xsT, c["ident_b"]], writes=[P[2]])
        T.op("dve", lambda e: e.tensor_tensor(xdt.t[:].rearrange("p (h q) -> p h q", q=64), tr2[:, 0:1024].rearrange("p (h q) -> p h q", q=64),
                                              dt.t[:].unsqueeze(2).to_broadcast([128, 16, 64]), ALU.mult), reads=[P[2], dt], writes=[xdt])
        T.op("dve", lambda e: e.tensor_tensor(xdtw.t[:].rearrange("p (h q) -> p h q", q=64), xdt.t[:].rearrange("p (h q) -> p h q", q=64),
                                              wsx.t[:].unsqueeze(2).to_broadcast([128, 16, 64]), ALU.mult), reads=[xdt, wsx], writes=[xdtw])
        tr7 = self.trps(7)
        T.group("pe", [(lambda e, g=g: e.transpose(tr7[:, g * 128:(g + 1) * 128], BCT.t[:, g, :], c["ident_b"].t[:])) for g in range(2)],
                reads=[BCT, c["ident_b"]], writes=[P[7]])
        T.op("act", lambda e: e.activation(Btm.t[:], tr7[:, 0:256], AF.Copy), reads=[P[7]], writes=[Btm])
        T.group("pe", [(lambda e, g=g: e.matmul(P[0].t[:, g * 128:(g + 1) * 128], BCT.t[:, g, :], BCT.t[:, 2 + g, :], start=True, stop=True)) for g in range(2)],
                reads=[BCT], writes=[P[0]])
        T.op("act", lambda e: e.activation(G.t[:], P[0].t[:, 0:256], AF.Copy), reads=[P[0]], writes=[G])
        def F(h):
            g = h // 8
            Dm, E = self.decay_mats(d, na, h, cs.t[:, h:h + 1], w)
            MT, CsT = w["MT"][h % 2], w["CsT"][h % 2]
            T.op("dve", lambda e: e.tensor_tensor(MT.t[:], G.t[:, g * 128:(g + 1) * 128], Dm.t[:], ALU.mult), reads=[G, Dm], writes=[MT])
            T.op("dve", lambda e: e.tensor_tensor(CsT.t[:], BCT.t[:, 2 + g, :], E.t[:], ALU.mult), reads=[BCT, E], writes=[CsT])

        def B(h):
            MT, CsT = w["MT"][h % 2], w["CsT"][h % 2]
            PY = P[3 + h // 8]
            dst = PY.t[:, (h % 8) * 64:(h % 8 + 1) * 64]
            T.group("pe", [lambda e: e.matmul(dst, MT.t[:], xdt.t[:, h * 64:(h + 1) * 64], start=True, stop=False),
                           lambda e: e.matmul(dst, CsT.t[:], hb.t[:, h * 64:(h + 1) * 64], start=False, stop=True)],
                    reads=[MT, xdt, CsT, hb], writes=[PY])
        F(0)
        for h in range(16):
            if h + 1 < 16:
                F(h + 1)
            B(h)
            if gen is not None:
                for _ in range(3):
                    next(gen, None)
        T.op("act", lambda e: e.activation(ysb.t[:, 0:512], P[3].t[:], AF.Copy), reads=[P[3]], writes=[ysb])
        T.op("dve", lambda e: e.tensor_copy(ysb.t[:, 512:1024], P[4].t[:]), reads=[P[4]], writes=[ysb])
        for g in range(2):
            Pc = P[5 + g]
            T.op("pe", lambda e: e.matmul(Pc.t[:], Btm.t[:, g * 128:(g + 1) * 128], xdtw.t[:, g * 512:(g + 1) * 512], start=True, stop=True),
                 reads=[Btm, xdtw], writes=[Pc])
            hv = hf.t[:, g * 512:(g + 1) * 512].rearrange("p (h q) -> p h q", q=64)
            T.op("dve", lambda e: e.tensor_tensor(hv, hv, cdec.t[:, g * 8:(g + 1) * 8].unsqueeze(2).to_broadcast([128, 8, 64]), ALU.mult),
                 reads=[hf, cdec], writes=[hf])
            T.op("dve", lambda e: e.tensor_tensor(hf.t[:, g * 512:(g + 1) * 512], hf.t[:, g * 512:(g + 1) * 512], Pc.t[:], ALU.add), reads=[hf, Pc], writes=[hf])
        T.op("act", lambda e: e.activation(hb.t[:], hf.t[:], AF.Copy), reads=[hf], writes=[hb])
        T.group("pe", [(lambda e, j=j: e.transpose(tr2[:, j * 128:(j + 1) * 128], ysb.t[:, j * 128:(j + 1) * 128], c["ident_b"].t[:])) for j in range(8)],
                reads=[ysb, c["ident_b"]], writes=[P[2]])
        T.op("act", lambda e: e.activation(hos.t[:], tr2[:, 0:1024].rearrange("p (k t) -> p k t", t=128), AF.Copy), reads=[P[2]], writes=[hos])
        T.dma("act", self.HF[d].t[0:1024, c0:c0 + 128].rearrange("(k p) n -> p k n", p=128), hos.t[:], reads=[hos], writes=[self.HF[d].k((0, c0 // 128))])

    def hgrn_prep(self, e_, d, c0, w, KI, rst):
        T, c, P = self.T, self.c, self.P
        qT, fT, Vt, A, KIN, LG, BC, TMP = w["qT"], w["fT"], w["Vt"], w["A"], w["KIN"], w["LG"], w["BC"], w["TMP"]
        tot, hdec, CST, qs, keT, ketm, qloc, hoh = w["tot"], w["hdec"], w["CST"], w["qs"], w["keT"], w["ketm"], w["qloc"], w["hoh"]
        T.dma("sp", qT.t[:], self.U_FM.t[2560:3584, c0:c0 + 128].rearrange("(k p) n -> p k n", p=128), writes=[qT])
        yield
        T.dma("sp", fT.t[:], self.U_FM.t[3584 + d * 1024:3584 + (d + 1) * 1024, c0:c0 + 128].rearrange("(k p) n -> p k n", p=128), writes=[fT])
        yield
        T.dma("sp", Vt.t[:], self.U_TM.t[c0:c0 + 128, 0:1024], writes=[Vt])
        yield
        lb = c["LB"].t[:, e_ * 8:(e_ + 1) * 8].unsqueeze(2).to_broadcast([128, 8, 128])
        lb1 = c["LB1"].t[:, e_ * 8:(e_ + 1) * 8].unsqueeze(2).to_broadcast([128, 8, 128])
        T.op("act", lambda e: e.activation(A.t[:], fT.t[:], AF.Sigmoid), reads=[fT], writes=[A])
        yield
        T.op("dve", lambda e: e.tensor_tensor(A.t[:], A.t[:], lb1, ALU.mult), reads=[A, c["LB1"]], writes=[A])
        yield
        T.op("dve", lambda e: e.tensor_tensor(A.t[:], A.t[:], lb, ALU.add), reads=[A, c["LB"]], writes=[A])
        yield
        T.op("dve", lambda e: e.tensor_scalar(KIN.t[:], A.t[:], -1.0, 1.0, ALU.mult, ALU.add), reads=[A], writes=[KIN])
        yield
        T.op("act", lambda e: e.activation(LG.t[:], A.t[:], AF.Ln), reads=[A], writes=[LG])
        yield
        fl = lambda t: t.t[:].rearrange("p h t -> p (h t)")
        T.op("dve", lambda e: e.tensor_tensor_scan(fl(BC), rst.t[:], fl(LG), 0.0, ALU.mult, ALU.add), reads=[rst, LG], writes=[BC])
        yield
        T.op("dve", lambda e: e.tensor_copy(tot.t[:].unsqueeze(2), BC.t[:, :, 127:128]), reads=[BC], writes=[tot])
        yield
        totb = tot.t[:].unsqueeze(2).to_broadcast([128, 8, 128])
        if d == 1:
            T.op("dve", lambda e: e.tensor_tensor(BC.t[:], LG.t[:], BC.t[:], ALU.subtract), reads=[LG, BC], writes=[BC])
            yield
            T.op("dve", lambda e: e.tensor_tensor(BC.t[:], BC.t[:], totb, ALU.add), reads=[BC, tot], writes=[BC])
            yield
        T.op("act", lambda e: e.activation(TMP.t[:], BC.t[:], AF.Exp), reads=[BC], writes=[TMP])
        yield
        T.op("dve", lambda e: e.tensor_tensor(qs.t[:], qT.t[:], TMP.t[:], ALU.mult), reads=[qT, TMP], writes=[qs])
        yield
        T.op("dve", lambda e: e.tensor_tensor(TMP.t[:], totb, BC.t[:], ALU.subtract), reads=[tot, BC, TMP], writes=[TMP])
        yield
        T.op("act", lambda e: e.activation(TMP.t[:], TMP.t[:], AF.Exp), reads=[TMP], writes=[TMP])
        yield
        T.op("dve", lambda e: e.tensor_tensor(keT.t[:], TMP.t[:], KIN.t[:], ALU.mult), reads=[TMP, KIN], writes=[keT])
        yield
        T.op("act", lambda e: e.activation(hdec.t[:], tot.t[:], AF.Exp), reads=[tot], writes=[hdec])
        yield
        tr2 = self.trps(2)
        T.group("pe", [(lambda e, h=h: e.transpose(tr2[:, h * 128:(h + 1) * 128], keT.t[:, h, :], c["ident_b"].t[:])) for h in range(8)],
                reads=[keT, c["ident_b"]], writes=[P[2]])
        yield
        T.op("act", lambda e: e.activation(ketm.t[:], tr2[:, 0:1024], AF.Copy), reads=[P[2]], writes=[ketm])
        yield
        if d == 0:
            T.op("pool", lambda e: e.memset(CST.t[:, :, 0:1], 0.0), writes=[CST])
            yield
            T.op("dve", lambda e: e.tensor_copy(CST.t[:, :, 1:8], BC.t[:, :, 15:127:16]), reads=[BC, CST], writes=[CST])
            yield
        else:
            T.op("pool", lambda e: e.memset(CST.t[:, :, 7:8], 0.0), writes=[CST])
            yield
            T.op("dve", lambda e: e.tensor_copy(CST.t[:, :, 0:7], BC.t[:, :, 16:128:16]), reads=[BC, CST], writes=[CST])
            yield
        v64 = lambda t: t.t[:].rearrange("p h (i u) -> p (h i) u", u=16)
        T.op("dve", lambda e: e.tensor_tensor(v64(TMP), v64(BC), CST.t[:].rearrange("p h i -> p (h i)").unsqueeze(2).to_broadcast([128, 64, 16]), ALU.subtract),
             reads=[BC, CST, TMP], writes=[TMP])
        yield
        T.op("act", lambda e: e.activation(TMP.t[:], TMP.t[:], AF.Exp), reads=[TMP], writes=[TMP])
        yield
        T.op("dve", lambda e: e.tensor_tensor(qloc.t[:], qT.t[:], TMP.t[:], ALU.mult), reads=[qT, TMP], writes=[qloc])
        yield
        for i in range(8):
            lo, hi = (0, 16 * (i + 1)) if d == 0 else (16 * i, 128)
            Ki = KI[i]
            T.op("dve", lambda e: e.tensor_tensor(TMP.t[:, :, lo:hi], CST.t[:, :, i:i + 1].to_broadcast([128, 8, hi - lo]), BC.t[:, :, lo:hi], ALU.subtract),
                 reads=[CST, BC, TMP], writes=[TMP])
            yield
            T.op("act", lambda e: e.activation(TMP.t[:, :, lo:hi], TMP.t[:, :, lo:hi], AF.Exp), reads=[TMP], writes=[TMP])
            yield
            T.op("dve", lambda e: e.tensor_tensor(Ki.t[:, :, lo:hi], TMP.t[:, :, lo:hi], KIN.t[:, :, lo:hi], ALU.mult), reads=[TMP, KIN], writes=[Ki])
            yield
    def hgrn_heads(self, e_, d, c0, w, sf, sb, KI):
        T, c, P = self.T, self.c, self.P
        Vt, qs, ketm, qloc, hdec, hoh = w["Vt"], w["qs"], w["ketm"], w["qloc"], w["hdec"], w["hoh"]
        def F(h):
            PA = P[h % 2]
            prA = PA
            a0 = 0
            attm = w["attm"][h % 2]
            T.group("pe", [(lambda e, i=i: e.matmul(PA.t[:, a0 + 16 * i:a0 + 16 * i + 16], KI[i].t[:, h, :], qloc.t[:, h, 16 * i:16 * i + 16], start=True, stop=True))
                           for i in range(8)], reads=KI + [qloc], writes=[prA])
            T.op("dve", lambda e: e.tensor_tensor(attm.t[:], PA.t[:, a0:a0 + 128], c[f"tri{d}"].t[:], ALU.mult), reads=[prA, c[f"tri{d}"]], writes=[attm])

        def B(h):
            attm = w["attm"][h % 2]
            PO = P[3 + h // 4]
            dst = PO.t[:, (h % 4) * 128:(h % 4 + 1) * 128]
            T.group("pe", [lambda e: e.matmul(dst, Vt.t[:, h * 128:(h + 1) * 128], attm.t[:], start=True, stop=False),
                           lambda e: e.matmul(dst, sb.t[:, h, :], qs.t[:, h, :], start=False, stop=True)],
                    reads=[Vt, attm, sb, qs], writes=[PO])
            Pc = P[5 + h % 2]
            T.op("pe", lambda e: e.matmul(Pc.t[:, 0:128], ketm.t[:, h * 128:(h + 1) * 128], Vt.t[:, h * 128:(h + 1) * 128], start=True, stop=True),
                 reads=[ketm, Vt], writes=[Pc])
            T.op("dve", lambda e: e.scalar_tensor_tensor(sf.t[:, h, :], sf.t[:, h, :], hdec.t[:, h:h + 1], Pc.t[:, 0:128], ALU.mult, ALU.add),
                 reads=[sf, hdec, Pc], writes=[sf])
        F(0)
        for h in range(8):
            if h + 1 < 8:
                F(h + 1)
            B(h)
        T.op("act", lambda e: e.activation(sb.t[:], sf.t[:], AF.Copy), reads=[sf], writes=[sb])
        T.op("act", lambda e: e.activation(hoh.t[:, 0:4, :], P[3].t[:].rearrange("p (h t) -> p h t", t=128), AF.Copy), reads=[P[3]], writes=[hoh])
        T.op("dve", lambda e: e.tensor_copy(hoh.t[:, 4:8, :], P[4].t[:].rearrange("p (h t) -> p h t", t=128)), reads=[P[4]], writes=[hoh])
        T.dma("act", self.HF[d].t[1024:2048, c0:c0 + 128].rearrange("(k p) n -> p k n", p=128), hoh.t[:], reads=[hoh], writes=[self.HF[d].k((1, c0 // 128))])


_W_NAMES = ["mod_w", "mod_b", "norm_w", "final_norm_w", "mlp_w1", "mlp_w2", "even_w_in", "even_w_out",
            "ssd_conv_w", "ssd_conv_b", "ssd_a_log", "ssd_dt_bias", "ssd_d", "ssd_norm_w", "hgrn_lb", "hgrn_norm_w",
            "odd_w_in", "odd_w_out", "mlstm_conv_w", "mlstm_conv_b", "mlstm_gate_b", "mlstm_norm_w"]


def make_in_maps(inputs, cores):
    f = lambda a: np.ascontiguousarray(np.asarray(a, dtype=np.float32))
    shared = {k: f(inputs[k]) for k in _W_NAMES}
    maps = []
    for ci in cores:
        m = dict(shared)
        m["x"] = f(inputs["x"][2 * ci:2 * ci + 2])
        m["ctx"] = f(inputs["ctx"][2 * ci:2 * ci + 2])
        m["cvec"] = f(np.stack([inputs["c"][2 * ci], inputs["c"][2 * ci + 1], inputs["c_ctx"]], axis=0))
        maps.append(m)
    return maps


def kernel(**inputs):
    prog = Prog()
    nc = prog.build()
    maps = make_in_maps(inputs, list(range(8)))
    res = run_bass_kernel_spmd(nc, maps, core_ids=list(range(8)))
    return np.concatenate([np.asarray(r["out"], dtype=np.float32) for r in res.results], axis=0)
```
